# Optimizing a Trainium2 kernel written in Bass

```python
import math, functools
import jax, jax.numpy as jnp
from jax import lax
import numpy as np

D_MODEL = 1024
BATCH = 4
SEQ = 4096
DEPTH = 1
DEC_BATCH = 128
DEC_SEQ = 4
PAST_LEN = 8192
PAGE_SIZE = 128

N_HEADS = 8
QK_NOPE = 64
QK_ROPE = 32
QK_HEAD = QK_NOPE + QK_ROPE
V_HEAD = 64
Q_LORA = 384
KV_LORA = 256
ROPE_THETA = 10000.0
Q_BLOCK = 128
D_CONV = D_MODEL // 2
CONV_WIDTH = 31
D_FF = -(-8 * D_MODEL // (3 * 256)) * 256
D_IN = 2 * D_CONV + Q_LORA + KV_LORA + QK_ROPE + 2 * D_MODEL
EPS = 1e-6

kernel_name = "conformer_conv_mla_gated_hybrid_step"


def _rmsnorm(x, g):
    x32 = x.astype(jnp.float32)
    y = x32 * lax.rsqrt(jnp.mean(x32 * x32, axis=-1, keepdims=True) + EPS)
    return (y * g.astype(jnp.float32)).astype(x.dtype)


def _layernorm(x, g, b):
    x32 = x.astype(jnp.float32)
    mu = jnp.mean(x32, axis=-1, keepdims=True)
    xc = x32 - mu
    y = xc * lax.rsqrt(jnp.mean(xc * xc, axis=-1, keepdims=True) + EPS)
    return (y * g.astype(jnp.float32) + b.astype(jnp.float32)).astype(x.dtype)


def _head_norm(v, g_compact):
    g = jnp.concatenate([g_compact, g_compact[QK_NOPE:]], axis=0)
    return _rmsnorm(v, g)


def _rope_angles(pos):
    inv_freq = 1.0 / (ROPE_THETA ** (jnp.arange(0, QK_ROPE, 2, dtype=jnp.float32) / QK_ROPE))
    ang = pos.astype(jnp.float32)[:, None] * inv_freq[None, :]
    return jnp.cos(ang), jnp.sin(ang)


def _apply_rope(x, cos, sin):
    half = QK_ROPE // 2
    x1, x2 = x[..., :half], x[..., half:]
    c = cos.astype(x.dtype)
    s = sin.astype(x.dtype)
    return jnp.concatenate([x1 * c - x2 * s, x2 * c + x1 * s], axis=-1)


def _split_in(p):
    i0 = 2 * D_CONV
    i1 = i0 + Q_LORA
    i2 = i1 + KV_LORA
    i3 = i2 + QK_ROPE
    i4 = i3 + D_MODEL
    return p[..., :i0], p[..., i0:i1], p[..., i1:i2], p[..., i2:i3], p[..., i3:i4], p[..., i4:]


def _mla_keys(c_kv, k_pe, w_uk, k_norm_g):
    k_nope = jnp.einsum('...tc,chd->...thd', c_kv, w_uk)
    k_pe_h = jnp.broadcast_to(k_pe[..., None, :], k_nope.shape[:-1] + (QK_ROPE,))
    return _head_norm(jnp.concatenate([k_nope, k_pe_h], axis=-1), k_norm_g)


def _mla_prompt(q, c_kv, k_pe, w_uk, w_uv, k_norm_g):
    b, s = q.shape[0], q.shape[1]
    k = _mla_keys(c_kv, k_pe, w_uk, k_norm_g)
    v = jnp.einsum('btc,chd->bthd', c_kv, w_uv)
    nb = s // Q_BLOCK
    qb = q.reshape(b, nb, Q_BLOCK, N_HEADS, QK_HEAD).transpose(1, 0, 2, 3, 4)
    kpos = jnp.arange(s)
    scale = 1.0 / math.sqrt(QK_HEAD)

    def block(args):
        qi, i = args
        sc = jnp.einsum('bqhd,bkhd->bhqk', qi, k).astype(jnp.float32) * scale
        qpos = i * Q_BLOCK + jnp.arange(Q_BLOCK)
        sc = jnp.where(kpos[None, :] <= qpos[:, None], sc, -jnp.inf)
        p = jax.nn.softmax(sc, axis=-1).astype(v.dtype)
        return jnp.einsum('bhqk,bkhd->bqhd', p, v)

    o = lax.map(block, (qb, jnp.arange(nb)))
    return o.transpose(1, 0, 2, 3, 4).reshape(b, s, N_HEADS * V_HEAD)


def _mla_sample(q, c_kv, k_pe, cache_lat, cache_kpe, page_table, w_uk, w_uv, k_norm_g):
    n_pages = page_table.shape[1]
    past = n_pages * PAGE_SIZE
    t_new = q.shape[1]
    kpos = jnp.arange(past + t_new)
    qpos = past + jnp.arange(t_new)
    mask = kpos[None, :] <= qpos[:, None]
    scale = 1.0 / math.sqrt(QK_HEAD)

    def one(args):
        qi, ci, kpi, pt = args
        c_all = jnp.concatenate([cache_lat[pt].reshape(past, KV_LORA), ci], axis=0)
        kpe_all = jnp.concatenate([cache_kpe[pt].reshape(past, QK_ROPE), kpi], axis=0)
        k = _mla_keys(c_all, kpe_all, w_uk, k_norm_g)
        sc = jnp.einsum('qhd,khd->hqk', qi, k).astype(jnp.float32) * scale
        sc = jnp.where(mask[None], sc, -jnp.inf)
        p = jax.nn.softmax(sc, axis=-1).astype(c_all.dtype)
        o_lat = jnp.einsum('hqk,kc->qhc', p, c_all)
        return jnp.einsum('qhc,chd->qhd', o_lat, w_uv)

    o = lax.map(one, (q, c_kv, k_pe, page_table))
    return o.reshape(q.shape[0], t_new, N_HEADS * V_HEAD)


def _conv_branch(glu_in, conv_prev, conv_w, conv_b, ln_g, ln_b, w_conv_out):
    a, g = glu_in[..., :D_CONV], glu_in[..., D_CONV:]
    u = a * jax.nn.sigmoid(g)
    u_ext = jnp.concatenate([conv_prev.astype(u.dtype), u], axis=1)
    y = lax.conv_general_dilated(u_ext, conv_w[:, None, :].astype(u.dtype), window_strides=(1,),
                                 padding='VALID', dimension_numbers=('NWC', 'WIO', 'NWC'),
                                 feature_group_count=D_CONV) + conv_b
    y = jax.nn.silu(_layernorm(y, ln_g, ln_b))
    return y @ w_conv_out, u_ext[:, -(CONV_WIDTH - 1):]


def _layer(x, pos, conv_prev, attend, norm_mix_g, w_in, q_a_norm_g, w_uq, kv_a_norm_g, q_norm_g,
           w_o_mla, conv_w, conv_b, conv_ln_g, conv_ln_b, w_conv_out, w_out, norm_ffn_g,
           w_gate, w_up, w_down):
    n, t = x.shape[0], x.shape[1]
    h = _rmsnorm(x, norm_mix_g)
    glu_in, c_q, c_kv, k_pe, g_conv, g_mla = _split_in(h @ w_in)
    conv_out, conv_state = _conv_branch(glu_in, conv_prev, conv_w, conv_b, conv_ln_g, conv_ln_b, w_conv_out)
    cos, sin = _rope_angles(pos)
    q = (_rmsnorm(c_q, q_a_norm_g) @ w_uq).reshape(n, t, N_HEADS, QK_HEAD)
    q = jnp.concatenate([q[..., :QK_NOPE], _apply_rope(q[..., QK_NOPE:], cos[:, None, :], sin[:, None, :])], axis=-1)
    q = _head_norm(q, q_norm_g)
    c_kv = _rmsnorm(c_kv, kv_a_norm_g)
    k_pe = _apply_rope(k_pe, cos, sin)
    mla_out = attend(q, c_kv, k_pe) @ w_o_mla
    merged = jax.nn.sigmoid(g_conv) * conv_out + jax.nn.sigmoid(g_mla) * mla_out
    x = x + merged @ w_out
    h2 = _rmsnorm(x, norm_ffn_g)
    x = x + (jax.nn.silu(h2 @ w_gate) * (h2 @ w_up)) @ w_down
    return x, c_kv, k_pe, conv_state


def setup_inputs(seed: int = 0) -> dict:
    key = jax.random.key(seed)
    ks = jax.random.split(key, 32)
    n_pages = PAST_LEN // PAGE_SIZE
    n_used = DEC_BATCH * n_pages
    n_phys = n_used + n_used // 4
    f32 = jnp.float32

    def w(k, shape, fan_in):
        return jax.random.normal(k, shape, f32) * (fan_in ** -0.5)

    def gain(k, shape):
        return 1.0 + 0.05 * jax.random.normal(k, shape, f32)

    page_table = jax.random.permutation(ks[5], n_phys)[:n_used].reshape(DEC_BATCH, n_pages).astype(jnp.int32)
    return {
        "x_prompt": jax.random.normal(ks[0], (BATCH, SEQ, D_MODEL), f32),
        "x_sample": jax.random.normal(ks[1], (DEC_BATCH, DEC_SEQ, D_MODEL), f32),
        "cache_kv_latent": jax.random.normal(ks[2], (DEPTH, n_phys, PAGE_SIZE, KV_LORA), f32),
        "cache_k_rope": jax.random.normal(ks[3], (DEPTH, n_phys, PAGE_SIZE, QK_ROPE), f32),
        "state_conv": 0.5 * jax.random.normal(ks[4], (DEPTH, DEC_BATCH, CONV_WIDTH - 1, D_CONV), f32),
        "page_table": page_table,
        "norm_mix_g": gain(ks[6], (DEPTH, D_MODEL)),
        "w_in": w(ks[7], (DEPTH, D_MODEL, D_IN), D_MODEL),
        "q_a_norm_g": gain(ks[8], (DEPTH, Q_LORA)),
        "w_uq": w(ks[9], (DEPTH, Q_LORA, N_HEADS * QK_HEAD), Q_LORA),
        "kv_a_norm_g": gain(ks[10], (DEPTH, KV_LORA)),
        "w_uk": w(ks[11], (DEPTH, KV_LORA, N_HEADS, QK_NOPE), KV_LORA),
        "w_uv": w(ks[12], (DEPTH, KV_LORA, N_HEADS, V_HEAD), KV_LORA),
        "q_norm_g": gain(ks[13], (DEPTH, QK_NOPE + QK_ROPE // 2)),
        "k_norm_g": gain(ks[14], (DEPTH, QK_NOPE + QK_ROPE // 2)),
        "w_o_mla": w(ks[15], (DEPTH, N_HEADS * V_HEAD, D_MODEL), N_HEADS * V_HEAD),
        "conv_w": w(ks[16], (DEPTH, CONV_WIDTH, D_CONV), CONV_WIDTH),
        "conv_b": 0.02 * jax.random.normal(ks[17], (DEPTH, D_CONV), f32),
        "conv_ln_g": gain(ks[18], (DEPTH, D_CONV)),
        "conv_ln_b": 0.02 * jax.random.normal(ks[19], (DEPTH, D_CONV), f32),
        "w_conv_out": w(ks[20], (DEPTH, D_CONV, D_MODEL), D_CONV),
        "w_out": w(ks[21], (DEPTH, D_MODEL, D_MODEL), D_MODEL),
        "norm_ffn_g": gain(ks[22], (DEPTH, D_MODEL)),
        "w_gate": w(ks[23], (DEPTH, D_MODEL, D_FF), D_MODEL),
        "w_up": w(ks[24], (DEPTH, D_MODEL, D_FF), D_MODEL),
        "w_down": w(ks[25], (DEPTH, D_FF, D_MODEL), D_FF),
    }


def reference(x_prompt, x_sample, cache_kv_latent, cache_k_rope, state_conv, page_table,
              norm_mix_g, w_in, q_a_norm_g, w_uq, kv_a_norm_g, w_uk, w_uv, q_norm_g, k_norm_g,
              w_o_mla, conv_w, conv_b, conv_ln_g, conv_ln_b, w_conv_out, w_out, norm_ffn_g,
              w_gate, w_up, w_down):
    past = page_table.shape[1] * PAGE_SIZE
    pos_p = jnp.arange(x_prompt.shape[1])
    pos_s = past + jnp.arange(x_sample.shape[1])
    yp, ys = x_prompt, x_sample
    lat_p, kpe_p, conv_p, lat_s, kpe_s, conv_s = [], [], [], [], [], []
    for l in range(DEPTH):
        lw = (norm_mix_g[l], w_in[l], q_a_norm_g[l], w_uq[l], kv_a_norm_g[l], q_norm_g[l],
              w_o_mla[l], conv_w[l], conv_b[l], conv_ln_g[l], conv_ln_b[l], w_conv_out[l],
              w_out[l], norm_ffn_g[l], w_gate[l], w_up[l], w_down[l])
        attend_p = functools.partial(_mla_prompt, w_uk=w_uk[l], w_uv=w_uv[l], k_norm_g=k_norm_g[l])
        zeros_prev = jnp.zeros((yp.shape[0], CONV_WIDTH - 1, D_CONV), yp.dtype)
        yp, c1, k1, s1 = _layer(yp, pos_p, zeros_prev, attend_p, *lw)
        attend_s = functools.partial(_mla_sample, cache_lat=cache_kv_latent[l], cache_kpe=cache_k_rope[l],
                                     page_table=page_table, w_uk=w_uk[l], w_uv=w_uv[l], k_norm_g=k_norm_g[l])
        ys, c2, k2, s2 = _layer(ys, pos_s, state_conv[l], attend_s, *lw)
        lat_p.append(c1); kpe_p.append(k1); conv_p.append(s1)
        lat_s.append(c2); kpe_s.append(k2); conv_s.append(s2)
    return (yp, ys, jnp.stack(lat_p), jnp.stack(kpe_p), jnp.stack(conv_p),
            jnp.stack(lat_s), jnp.stack(kpe_s), jnp.stack(conv_s))
```

```python
import math
import numpy as np
import concourse.bass as bass
import concourse.mybir as mybir
from concourse.bass_utils import run_bass_kernel_spmd

F32 = mybir.dt.float32
BF16 = mybir.dt.bfloat16
I32 = mybir.dt.int32
U8 = mybir.dt.uint8
AF = mybir.ActivationFunctionType
ALU = mybir.AluOpType
AX = mybir.AxisListType

D = 1024
NH = 8
DQK = 96
LAT = 256
ROPE = 32
DCONV = 512
CW = 31
DFF = 2816
DIN = 3744
QL = 384
EPS = 1e-6
NEG = -30000.0
I_CQ = 1024
I_KV = 1408
I_GC = 1696
I_GM = 2720
DMAQ = ("sp", "actq", "poolq")


class _Op:
    __slots__ = ("eng", "fn", "reads", "writes", "is_dma", "semkey", "waits", "ticket", "sem", "idx",
                 "needs_sig", "final")


def _phys(eng):
    return {"pe": "pe", "act": "act", "dve": "dve", "pool": "pool", "sp": "sp", "actq": "act",
            "poolq": "pool"}[eng]


class Prog:
    def __init__(self, nc):
        self.nc = nc
        self.ops = []
        self.last_writer = {}
        self.readers = {}
        self.bar = None
        self.bar_seen = set()
        self.last_on = {}
        self.dma_since = []

    def barrier(self):
        deps = set(self.last_on.values()) | set(self.dma_since)
        if self.bar is not None:
            deps |= self.bar
        self.bar = deps
        self.bar_seen = set()
        self.dma_since = []

    def op(self, eng, fn, reads=(), writes=(), semkey=None, final=False):
        o = _Op()
        o.eng, o.fn = eng, fn
        o.reads, o.writes = tuple(reads), tuple(writes)
        o.is_dma = eng in DMAQ
        o.semkey = semkey
        o.final = final
        o.idx = len(self.ops)
        o.needs_sig = False
        deps = set()
        for r in o.reads:
            w = self.last_writer.get(r)
            if w is not None:
                deps.add(w)
        for w_ in o.writes:
            w = self.last_writer.get(w_)
            if w is not None:
                deps.add(w)
            deps.update(self.readers.get(w_, ()))
        ph = _phys(eng)
        if self.bar is not None and ph not in self.bar_seen:
            deps |= self.bar
            self.bar_seen.add(ph)
        o.waits = deps
        for r in o.reads:
            self.readers.setdefault(r, []).append(o.idx)
        for w_ in o.writes:
            self.last_writer[w_] = o.idx
            self.readers[w_] = []
        self.ops.append(o)
        self.last_on[ph] = o.idx
        if o.is_dma:
            self.dma_since.append(o.idx)
        return o

    def emit(self, final_wait_eng="sp"):
        nc, ops = self.nc, self.ops
        streams = {"pe": [], "act": [], "dve": [], "pool": [], "sp": []}
        for o in ops:
            streams[_phys(o.eng)].append(o)
        for o in ops:
            keep = set()
            ph = _phys(o.eng)
            for d in o.waits:
                p = ops[d]
                if _phys(p.eng) == ph and not p.is_dma:
                    if ph == "pe":
                        continue
                    if not (set(p.writes) & set(o.reads)):
                        continue
                keep.add(d)
            o.waits = keep
            for d in keep:
                ops[d].needs_sig = True
        semh, semc = {}, {}

        def getsem(key):
            if key not in semh:
                semh[key] = nc.alloc_semaphore("s%d" % len(semh))
                semc[key] = 0
            return semh[key]

        finals = []
        for o in ops:
            if o.is_dma:
                key = ("dma", o.semkey if o.semkey is not None else (o.writes[0] if o.writes else o.reads[0]))
                o.sem = getsem(key)
                semc[key] += 16
                o.ticket = semc[key]
                o.needs_sig = True
                if o.final:
                    finals.append(o)
            elif o.needs_sig:
                key = ("eng", _phys(o.eng))
                o.sem = getsem(key)
                semc[key] += 1
                o.ticket = semc[key]
        self.n_sems = len(semh)

        def run_stream(name, e):
            waited = {}
            for o in streams[name]:
                need = {}
                for d in o.waits:
                    p = ops[d]
                    k = id(p.sem)
                    if k not in need or need[k][1] < p.ticket:
                        need[k] = (p.sem, p.ticket)
                for k, (s, t) in need.items():
                    if waited.get(k, 0) >= t:
                        continue
                    e.wait_ge(s, t)
                    waited[k] = t
                ins = o.fn(e)
                if o.needs_sig:
                    ins.then_inc(o.sem, 16 if o.is_dma else 1)
            if name == final_wait_eng:
                need = {}
                for o in finals:
                    k = id(o.sem)
                    if k not in need or need[k][1] < o.ticket:
                        need[k] = (o.sem, o.ticket)
                for k, (s, t) in need.items():
                    e.wait_ge(s, t)

        with nc.Block() as block:
            @block.sync
            def _(e):
                run_stream("sp", e)

            @block.tensor
            def _(e):
                run_stream("pe", e)

            @block.scalar
            def _(e):
                run_stream("act", e)

            @block.vector
            def _(e):
                run_stream("dve", e)

            @block.gpsimd
            def _(e):
                run_stream("pool", e)


class Arena:
    def __init__(self, nc, nbytes):
        self.t = nc.alloc_sbuf_tensor("arena", [128, nbytes], U8)
        self.n = nbytes
        self.off = 0
        self.peak = 0

    def alloc(self, shape, dtype, parts=128):
        esz = {F32: 4, BF16: 2, I32: 4, U8: 1}[dtype]
        n = 1
        for s in shape:
            n *= s
        nb = n * esz
        self.off = (self.off + 63) // 64 * 64
        assert self.off + nb <= self.n, ("SBUF arena overflow", self.off, nb, self.n)
        v = self.t[0:parts, self.off:self.off + nb]
        if dtype != U8:
            v = v.bitcast(dtype)
        if len(shape) == 2:
            v = v.rearrange("p (a b) -> p a b", a=shape[0])
        elif len(shape) == 3:
            v = v.rearrange("p (a b c) -> p a b c", a=shape[0], b=shape[1])
        elif len(shape) == 4:
            v = v.rearrange("p (a b c d) -> p a b c d", a=shape[0], b=shape[1], c=shape[2])
        self.off += nb
        self.peak = max(self.peak, self.off)
        return v

    def mark(self):
        return self.off

    def release(self, m):
        self.off = m


class Cfg:
    def __init__(self, nb=4, seq=4096, db=128, past=8192):
        self.NB, self.SEQ, self.DB, self.PAST = nb, seq, db, past
        self.NC = 2 * nb
        self.NSB = seq // 512
        self.NSQ = db // self.NC
        assert self.NSQ % 2 == 0 and past == 8192 and seq % 512 == 0
        self.NPAIR = self.NSQ // 2
        self.TP = self.NSB * 256
        self.TS = self.NSQ * 4
        self.T = self.TP + self.TS
        self.NK = self.NSB * 512
        self.NKB = self.NK // 128
        self.NPG = past // 128
        self.NPHYS = db * self.NPG + (db * self.NPG) // 4
        self.NG = 2 if self.NSB >= 2 else 1
        self.SBG = self.NSB // self.NG
        self.SQG = self.NSQ // self.NG


class _Stop(Exception):
    pass


def build(cfg):
    nc = bass.Bass("TRN2", target_bir_lowering=False)
    P = Prog(nc)
    try:
        _build_body(cfg, nc, P)
    except _Stop:
        pass
    return nc, P, None


def _build_body(cfg, nc, P):
    def done(tag):
        if getattr(cfg, 'stop', None) == tag:
            P.emit()
            raise _Stop()
    NSB, NSQ, NPAIR, TP, TS, T, NK, NKB = cfg.NSB, cfg.NSQ, cfg.NPAIR, cfg.TP, cfg.TS, cfg.T, cfg.NK, cfg.NKB

    def din(name, shape, dt=F32):
        return nc.dram_tensor(name, list(shape), dt, kind="ExternalInput").ap()

    def dout(name, shape, dt=F32):
        return nc.dram_tensor(name, list(shape), dt, kind="ExternalOutput").ap()

    xk = din("xk", [NK, D]); xs = din("xs", [TS, D]); xh = din("xh", [NSB * 32, D])
    cs_k = din("cs_k", [NK, 64]); cs_s = din("cs_s", [TS, 64])
    maskp_d = din("maskp", [128, 4, 256]); maskpair_d = din("maskpair", [128, 64]); masknew_d = din("masknew", [TS, NSQ * 32])
    ident_d = din("ident", [128, 128]); ind_d = din("ind", [128, 4, 8])
    cache_lat = din("cache_lat", [cfg.NPHYS, 128 * LAT]); cache_rope = din("cache_rope", [cfg.NPHYS, 128 * ROPE])
    state_d = din("state", [NSQ * 30, DCONV]); ptT_d = din("ptT", [128, NPAIR], I32)
    w_in = din("w_in", [D, DIN]); w_uq = din("w_uq", [QL, NH * DQK]); w_uk = din("w_uk", [LAT, 512]); w_uv = din("w_uv", [LAT, 512])
    w_o = din("w_o_mla", [512, D]); w_co = din("w_conv_out", [512, D]); w_out = din("w_out", [D, D])
    w_gate = din("w_gate", [D, DFF]); w_up = din("w_up", [D, DFF]); w_down = din("w_down", [DFF, D])
    gmixT_d = din("gmixT", [128, 8]); gffnT_d = din("gffnT", [128, 8]); convwT_d = din("convwT", [128, 4, CW])
    convbT_d = din("convbT", [128, 4]); lngT_d = din("lngT", [128, 4]); lnbT_d = din("lnbT", [128, 4])
    gqa_d = din("gqa", [1, QL]); gkv_d = din("gkv", [1, LAT]); gq_d = din("gq", [1, 80]); gk_d = din("gk", [1, 80])

    y_own = dout("y_own", [T, D]); lat_k = dout("lat_k", [NK, LAT]); kpe_k = dout("kpe_k", [NK, ROPE])
    lat_s = dout("lat_s", [TS, LAT]); kpe_s = dout("kpe_s", [TS, ROPE])
    cst_p = dout("cst_p", [32, DCONV]); cst_s = dout("cst_s", [NSQ, 30, DCONV])

    A = Arena(nc, 207 * 1024)
    psb = [nc.alloc_psum_tensor("psb%d" % i, [128, 512], F32) for i in range(8)]
    cnt = {"ev": 0, "q": 0}

    def psf(b):
        return psb[b][:]

    def psh(b):
        return psb[b][:].bitcast(BF16)

    def dmaq():
        cnt["q"] += 1
        return "sp" if cnt["q"] % 2 else "actq"

    def dma(q, out, in_, reads=(), writes=(), semkey=None, final=False):
        return P.op(q, lambda e: e.dma_start(out=out, in_=in_), reads=reads, writes=writes, semkey=semkey, final=final)

    def mm(out, lhsT, rhs, start, stop, reads, writes):
        return P.op("pe", lambda e: e.matmul(out, lhsT=lhsT, rhs=rhs, start=start, stop=stop), reads=reads, writes=writes)

    def tr(out, in_, idn, reads, writes):
        return P.op("pe", lambda e: e.transpose(out, in_, idn), reads=reads, writes=writes)

    def act(out, in_, func, reads, writes, scale=1.0, bias=0.0, accum=None):
        if accum is not None:
            return P.op("act", lambda e: e.activation(out=out, in_=in_, func=func, scale=scale, bias=bias, accum_out=accum), reads=reads, writes=writes)
        return P.op("act", lambda e: e.activation(out=out, in_=in_, func=func, scale=scale, bias=bias), reads=reads, writes=writes)

    def evac(out, in_, reads, writes, eng=None):
        if eng is None:
            cnt["ev"] += 1
            eng = "act" if cnt["ev"] % 2 else "dve"
        if eng == "act":
            return P.op("act", lambda e: e.activation(out=out, in_=in_, func=AF.Copy), reads=reads, writes=writes)
        return P.op(eng, lambda e: e.tensor_copy(out, in_), reads=reads, writes=writes)

    def tt(out, a, b, op, reads, writes, eng="dve"):
        return P.op(eng, lambda e: e.tensor_tensor(out, a, b, op), reads=reads, writes=writes)

    def ts(out, a, s1, s2, op0, op1, reads, writes, eng="dve"):
        if s2 is None:
            return P.op(eng, lambda e: e.tensor_scalar(out, a, s1, None, op0), reads=reads, writes=writes)
        return P.op(eng, lambda e: e.tensor_scalar(out, a, s1, s2, op0, op1), reads=reads, writes=writes)

    def stt(out, a, s, b, op0, op1, reads, writes, eng="dve"):
        return P.op(eng, lambda e: e.scalar_tensor_tensor(out, a, s, b, op0, op1), reads=reads, writes=writes)

    def recip(out, in_, reads, writes):
        return P.op("dve", lambda e: e.reciprocal(out, in_), reads=reads, writes=writes)

    def rsqrt_of(dst, src, scale, r, key_src, key_dst):
        act(dst, src, AF.Sqrt, [key_src, "epsc"], [key_dst], scale=scale, bias=epsc[0:r, 0:1])
        recip(dst, dst, [key_dst], [key_dst])

    ident_f = A.alloc([128], F32); ident_b = A.alloc([128], BF16)
    ind_b = A.alloc([4, 8], BF16); ones_b = A.alloc([128], BF16); ones_f = A.alloc([128], F32)
    epsc = A.alloc([1], F32)
    gmixT = A.alloc([8], F32); gffnT = A.alloc([8], F32)
    gqa_b = A.alloc([QL], F32); gkv_b = A.alloc([LAT], F32)
    gqk96 = A.alloc([DQK], F32)
    OT = A.alloc([4, T], BF16)
    wuk_b = A.alloc([2, 512], BF16); wuv_b = A.alloc([2, 512], BF16)
    stage = A.alloc([4, 8], F32)
    tmp80a = A.alloc([80], F32); tmp80b = A.alloc([80], F32)

    dma("sp", ident_f, ident_d, writes=["identf"])
    dma("actq", stage, ind_d, writes=["stage"])
    dma("sp", gmixT, gmixT_d, writes=["gmixT"]); dma("actq", gffnT, gffnT_d, writes=["gffnT"])
    dma("sp", gqa_b, gqa_d.partition_broadcast(128), writes=["gqa"])
    dma("actq", gkv_b, gkv_d.partition_broadcast(128), writes=["gkv"])
    dma("sp", tmp80a, gq_d.partition_broadcast(128), writes=["t80a"])
    dma("actq", tmp80b, gk_d.partition_broadcast(128), writes=["t80b"])
    dma("poolq", wuk_b, w_uk.rearrange("(c p) n -> p c n", p=128), writes=["wuk"])
    dma("poolq", wuv_b, w_uv.rearrange("(c p) n -> p c n", p=128), writes=["wuv"])
    P.op("dve", lambda e: e.tensor_copy(ident_b, ident_f), reads=["identf"], writes=["ident"])
    P.op("dve", lambda e: e.tensor_copy(ind_b, stage), reads=["stage"], writes=["ind"])
    P.op("pool", lambda e: e.memset(ones_b, 1.0), writes=["onesb"])
    P.op("pool", lambda e: e.memset(ones_f, 1.0), writes=["onesf"])
    P.op("pool", lambda e: e.memset(epsc, EPS), writes=["epsc"])
    stt(tmp80a, tmp80a, 1.0 / math.sqrt(DQK), tmp80b, ALU.mult, ALU.mult, ["t80a", "t80b"], ["t80a"])
    evac(gqk96[:, 0:80], tmp80a, ["t80a"], ["gqk"], eng="dve")
    evac(gqk96[:, 80:96], tmp80a[:, 64:80], ["t80a"], ["gqk"], eng="dve")

    qlatT = A.alloc([2, TS, NH], BF16)
    qpeT = A.alloc([TS, NH], BF16)
    CnewT = A.alloc([2, TS], BF16); kpenewT = A.alloc([TS], BF16); Cnew = A.alloc([LAT], BF16)
    rinvnew = A.alloc([NH], F32)
    wukT = A.alloc([NH, LAT], BF16)
    done('const')
    m_persist = A.mark()

    KT = A.alloc([NH, NK], BF16)
    V = A.alloc([NKB, NH, 65], BF16)
    wkv_b = A.alloc([8, 288], BF16); wcq_b = A.alloc([8, QL], BF16); wuq_b = A.alloc([3, NH * DQK], BF16)
    maskp_f = A.alloc([4, 256], F32); maskp_b = A.alloc([4, 256], BF16)
    NXS = 3
    xst = [A.alloc([D], F32) for _ in range(NXS)]
    hn = [A.alloc([D], BF16) for _ in range(2)]
    hTb = [A.alloc([8, 128], BF16) for _ in range(2)]
    cst = [A.alloc([64], F32) for _ in range(2)]
    small = [A.alloc([64], F32) for _ in range(2)]
    ckvf = [A.alloc([LAT], F32) for _ in range(2)]
    ckvb = [A.alloc([LAT], BF16) for _ in range(2)]
    kpef = [A.alloc([ROPE], F32) for _ in range(2)]
    kt1 = A.alloc([ROPE], F32); kt2 = A.alloc([ROPE], F32)
    CTb = [A.alloc([2, 128], BF16) for _ in range(2)]
    sq = A.alloc([1024], BF16)
    Kn = [A.alloc([NH, DQK], BF16) for _ in range(2)]
    cqn = A.alloc([QL], BF16); cqT = A.alloc([3, 128], BF16)
    qf = A.alloc([NH, DQK], F32); qt1 = A.alloc([NH, ROPE], F32); qt2 = A.alloc([NH, ROPE], F32)
    Qn = A.alloc([NH, DQK], BF16)
    QT = [A.alloc([NH, 256], BF16) for _ in range(2)]
    PT = [A.alloc([2, 256], BF16) for _ in range(3)]
    Otok = A.alloc([2, NH, 64], BF16)
    rden = A.alloc([2], F32)

    dma("poolq", wkv_b, w_in[:, I_KV:I_KV + 288].rearrange("(c p) n -> p c n", p=128), writes=["wkv"])
    dma("poolq", wcq_b, w_in[:, I_CQ:I_CQ + QL].rearrange("(c p) n -> p c n", p=128), writes=["wcq"])
    dma("poolq", wuq_b, w_uq.rearrange("(c p) n -> p c n", p=128), writes=["wuq"])
    dma("sp", maskp_f, maskp_d, writes=["maskpf"])
    evac(maskp_b, maskp_f, ["maskpf"], ["maskp"], eng="dve")
    P.op("pool", lambda e: e.memset(V[:, :, :, 64:65], 1.0), writes=["Vones"])

    for h in range(NH):
        for c in range(2):
            bnk = 4 + (h * 2 + c) % 4
            tr(psh(bnk)[0:64, 0:128], wuk_b[:, c, h * 64:(h + 1) * 64], ident_b, ["wuk", "ident"], ["ps%d" % bnk])
            evac(wukT[0:64, h, c * 128:(c + 1) * 128], psh(bnk)[0:64, 0:128], ["ps%d" % bnk], ["wukT"])

    ring = {"g": 0}

    def gbank():
        ring["g"] = (ring["g"] + 1) % 4
        return ring["g"]

    blk_ctr = {"n": 0}

    def token_block(xsrc, cssrc, r, mode, kb=None, qslot=None, qcol=None, latdst=None, kpedst=None):
        n = blk_ctr["n"]; blk_ctr["n"] += 1
        s2 = n % 2
        xt = xst[n % NXS]; kx = "xst%d" % (n % NXS)
        sm = small[s2]; ksm = "small%d" % s2
        hnb = hn[s2]; khn = "hn%d" % s2
        hT = hTb[s2]; khT = "hT%d" % s2
        cs = cst[s2]; kcs = "cs%d" % s2
        dma(dmaq(), xt[0:r], xsrc, writes=[kx])
        dma(dmaq(), cs[0:r], cssrc, writes=[kcs])
        act(sq[0:r, 0:D], xt[0:r], AF.Square, [kx], ["sq", ksm + "a"], accum=sm[0:r, 0:1])
        rsqrt_of(sm[0:r, 1:2], sm[0:r, 0:1], 1.0 / D, r, ksm + "a", ksm + "b")
        ts(hnb[0:r], xt[0:r], sm[0:r, 1:2], None, ALU.mult, None, [kx, ksm + "b"], [khn])
        b0 = gbank(); k0 = "ps%d" % b0
        pT = psh(b0).rearrange("p (c t) -> p c t", c=8)
        for c in range(8):
            tr(pT[:, c, 0:r], hnb[0:r, c * 128:(c + 1) * 128], ident_b[0:r, 0:r], [khn, "ident"], [k0])
        tt(hT[:, :, 0:r], pT[:, :, 0:r], gmixT.unsqueeze(2).to_broadcast([128, 8, r]), ALU.mult, [k0, "gmixT"], [khT])
        b1 = gbank(); k1 = "ps%d" % b1
        kvp = psf(b1)
        for c in range(8):
            mm(kvp[0:r, 0:288], hT[:, c, 0:r], wkv_b[:, c, :], c == 0, c == 7, [khT, "wkv"], [k1])
        cf = ckvf[s2]; kcf = "ckvf%d" % s2
        cb = ckvb[s2]; kcb = "ckvb%d" % s2
        kp = kpef[s2]; kkp = "kpef%d" % s2
        act(sq[0:r, 0:LAT], kvp[0:r, 0:LAT], AF.Square, [k1], ["sq", ksm + "c"], accum=sm[0:r, 2:3])
        rsqrt_of(sm[0:r, 3:4], sm[0:r, 2:3], 1.0 / LAT, r, ksm + "c", ksm + "d")
        stt(cf[0:r], kvp[0:r, 0:LAT], sm[0:r, 3:4], gkv_b[0:r], ALU.mult, ALU.mult, [k1, ksm + "d", "gkv"], [kcf])
        dma(dmaq(), latdst, cf[0:r], reads=[kcf], semkey="o_lat%d" % s2, final=True)
        evac(cb[0:r], cf[0:r], [kcf], [kcb], eng="act")
        tt(kt1[0:r], kvp[0:r, 256:288], cs[0:r, 0:32], ALU.mult, [k1, kcs], ["kt1"])
        tt(kt2[0:r, 0:16], kvp[0:r, 272:288], cs[0:r, 32:48], ALU.mult, [k1, kcs], ["kt2a"])
        tt(kt2[0:r, 16:32], kvp[0:r, 256:272], cs[0:r, 48:64], ALU.mult, [k1, kcs], ["kt2b"])
        tt(kp[0:r], kt1[0:r], kt2[0:r], ALU.add, ["kt1", "kt2a", "kt2b"], [kkp])
        dma(dmaq(), kpedst, kp[0:r], reads=[kkp], semkey="o_kpe%d" % s2, final=True)
        b2 = gbank(); k2 = "ps%d" % b2
        cTp = psh(b2).rearrange("p (c t) -> p c t", c=8)
        for c in range(2):
            tr(cTp[:, c, 0:r], cb[0:r, c * 128:(c + 1) * 128], ident_b[0:r, 0:r], [kcb, "ident"], [k2])
        if mode == "sample":
            CT = CnewT; kCT = "CnewT"
            evac(CT[:, :, 0:r], cTp[:, 0:2, 0:r], [k2], [kCT])
        else:
            CT = CTb[s2]; kCT = "CT%d" % s2
            evac(CT[:, :, 0:r], cTp[:, 0:2, 0:r], [k2], [kCT])
        b3 = gbank(); k3 = "ps%d" % b3
        knp = psf(b3)
        for c in range(2):
            mm(knp[0:r, :], CT[:, c, 0:r], wuk_b[:, c, :], c == 0, c == 1, [kCT, "wuk"], [k3])
        act(sq[0:r, 0:512], knp[0:r, :], AF.Square, [k3], ["sq"])
        P.op("dve", lambda e: e.tensor_reduce(sm[0:r, 8:16], sq[0:r, 0:512].rearrange("p (h d) -> p h d", h=NH), AX.X, ALU.add),
             reads=["sq"], writes=[ksm + "e"])
        act(kt1[0:r], kp[0:r], AF.Square, [kkp], ["kt1", ksm + "f"], accum=sm[0:r, 4:5])
        ts(sm[0:r, 8:16], sm[0:r, 8:16], sm[0:r, 4:5], None, ALU.add, None, [ksm + "e", ksm + "f"], [ksm + "e"])
        rinv = rinvnew if mode == "sample" else sm[:, 8:16]
        krinv = "rinvnew" if mode == "sample" else ksm + "g"
        rsqrt_of(rinv[0:r], sm[0:r, 8:16], 1.0 / DQK, r, ksm + "e", krinv)
        if mode == "sample":
            evac(Cnew[0:r], cf[0:r], [kcf], ["Cnew"], eng="dve")
            b4 = gbank(); k4 = "ps%d" % b4
            tr(psf(b4)[0:32, 0:r], kp[0:r], ident_f[0:r, 0:r], [kkp, "identf"], [k4])
            evac(kpenewT[0:32, 0:r], psf(b4)[0:32, 0:r], [k4], ["kpenewT"])
        else:
            knb = Kn[s2]; kkn = "Kn%d" % s2
            tt(knb[0:r, :, 0:64], knp[0:r, :].rearrange("p (h d) -> p h d", h=NH),
               sm[0:r, 8:16].unsqueeze(2).to_broadcast([r, NH, 64]), ALU.mult, [k3, krinv], [kkn + "a"])
            tt(knb[0:r, :, 64:96], kp[0:r].unsqueeze(1).to_broadcast([r, NH, ROPE]),
               sm[0:r, 8:16].unsqueeze(2).to_broadcast([r, NH, ROPE]), ALU.mult, [kkp, krinv], [kkn + "b"])
            b4 = gbank(); k4 = "ps%d" % b4
            kTp = psh(b4).rearrange("p (h t) -> p h t", h=NH)
            for h in range(NH):
                tr(kTp[0:DQK, h, 0:r], knb[0:r, h, :], ident_b[0:r, 0:r], [kkn + "a", kkn + "b", "ident"], [k4])
            evac(KT[0:DQK, :, kb * 128:kb * 128 + r], kTp[0:DQK, :, 0:r], [k4], ["KT%d" % kb])
            b5 = gbank(); k5 = "ps%d" % b5
            vp = psf(b5)
            for c in range(2):
                mm(vp[0:r, :], CT[:, c, 0:r], wuv_b[:, c, :], c == 0, c == 1, [kCT, "wuv"], [k5])
            evac(V[0:r, kb, :, 0:64], vp[0:r, :].rearrange("p (h d) -> p h d", h=NH), [k5, "Vones"], ["V%d" % kb])
        if mode == "other":
            return
        b6 = gbank(); k6 = "ps%d" % b6
        cqp = psf(b6)
        for c in range(8):
            mm(cqp[0:r, 0:QL], hT[:, c, 0:r], wcq_b[:, c, :], c == 0, c == 7, [khT, "wcq"], [k6])
        act(sq[0:r, 0:QL], cqp[0:r, 0:QL], AF.Square, [k6], ["sq", ksm + "h"], accum=sm[0:r, 5:6])
        rsqrt_of(sm[0:r, 6:7], sm[0:r, 5:6], 1.0 / QL, r, ksm + "h", ksm + "i")
        stt(cqn[0:r], cqp[0:r, 0:QL], sm[0:r, 6:7], gqa_b[0:r], ALU.mult, ALU.mult, [k6, ksm + "i", "gqa"], ["cqn"])
        b7 = gbank(); k7 = "ps%d" % b7
        cqTp = psh(b7).rearrange("p (c t) -> p c t", c=8)
        for c in range(3):
            tr(cqTp[:, c, 0:r], cqn[0:r, c * 128:(c + 1) * 128], ident_b[0:r, 0:r], ["cqn", "ident"], [k7])
        evac(cqT[:, :, 0:r], cqTp[:, 0:3, 0:r], [k7], ["cqT"])
        qfl = qf.rearrange("p h d -> p (h d)")
        for half in range(2):
            b8 = gbank(); k8 = "ps%d" % b8
            qp = psf(b8)
            for c in range(3):
                mm(qp[0:r, 0:QL], cqT[:, c, 0:r], wuq_b[:, c, half * QL:(half + 1) * QL], c == 0, c == 2, ["cqT", "wuq"], [k8])
            evac(qfl[0:r, half * QL:(half + 1) * QL], qp[0:r, 0:QL], [k8], ["qf%d" % half])
        kq = ["qf0", "qf1"]
        qr = qf[0:r, :, 64:96]
        tt(qt1[0:r], qr, cs[0:r, 0:32].unsqueeze(1).to_broadcast([r, NH, 32]), ALU.mult, kq + [kcs], ["qt1"])
        tt(qt2[0:r, :, 0:16], qf[0:r, :, 80:96], cs[0:r, 32:48].unsqueeze(1).to_broadcast([r, NH, 16]), ALU.mult, kq + [kcs], ["qt2a"])
        tt(qt2[0:r, :, 16:32], qf[0:r, :, 64:80], cs[0:r, 48:64].unsqueeze(1).to_broadcast([r, NH, 16]), ALU.mult, kq + [kcs], ["qt2b"])
        tt(qr, qt1[0:r], qt2[0:r], ALU.add, ["qt1", "qt2a", "qt2b"], ["qf0", "qf1"])
        act(sq[0:r, 0:768], qfl[0:r], AF.Square, kq, ["sq"])
        P.op("dve", lambda e: e.tensor_reduce(sm[0:r, 16:24], sq[0:r, 0:768].rearrange("p (h d) -> p h d", h=NH), AX.X, ALU.add),
             reads=["sq"], writes=[ksm + "j"])
        rsqrt_of(sm[0:r, 16:24], sm[0:r, 16:24], 1.0 / DQK, r, ksm + "j", ksm + "j")
        tt(qf[0:r], qf[0:r], sm[0:r, 16:24].unsqueeze(2).to_broadcast([r, NH, DQK]), ALU.mult, kq + [ksm + "j"], kq)
        tt(Qn[0:r], qf[0:r], gqk96[0:r].unsqueeze(1).to_broadcast([r, NH, DQK]), ALU.mult, kq + ["gqk"], ["Qn"])
        if mode == "own":
            b9 = gbank(); k9 = "ps%d" % b9
            qTp = psh(b9).rearrange("p (h t) -> p h t", h=NH)
            for h in range(NH):
                tr(qTp[0:DQK, h, 0:r], Qn[0:r, h, :], ident_b[0:r, 0:r], ["Qn", "ident"], [k9])
            evac(QT[qslot][0:DQK, :, qcol:qcol + r], qTp[0:DQK, :, 0:r], [k9], ["QT%d_%d" % (qslot, qcol)])
        else:
            b9 = gbank(); k9 = "ps%d" % b9
            qnTp = psh(b9).rearrange("p (h t) -> p h t", h=NH)
            for h in range(NH):
                tr(qnTp[0:64, h, 0:r], Qn[0:r, h, 0:64], ident_b[0:r, 0:r], ["Qn", "ident"], [k9])
            qnT = Kn[0]
            qnTv = qnT.rearrange("p h d -> p (h d)")[0:64, 0:NH * r].rearrange("p (h t) -> p h t", h=NH)
            evac(qnTv, qnTp[0:64, :, 0:r], [k9], ["Kn0a", "Kn0b"])
            b10 = gbank(); k10 = "ps%d" % b10
            qpTp = psh(b10).rearrange("p (h t) -> p h t", h=NH)
            for h in range(NH):
                tr(qpTp[0:32, h, 0:r], Qn[0:r, h, 64:96], ident_b[0:r, 0:r], ["Qn", "ident"], [k10])
            evac(qpeT[0:32, 0:r, :].rearrange("p t h -> p h t"), qpTp[0:32, :, 0:r], [k10], ["qpeT"])
            for c in range(2):
                b11 = gbank(); k11 = "ps%d" % b11
                ql = psf(b11).rearrange("p (h t) -> p h t", h=NH)
                for h in range(NH):
                    mm(ql[:, h, 0:r], wukT[0:64, h, c * 128:(c + 1) * 128], qnTv[:, h, :], True, True, ["wukT", "Kn0a", "Kn0b"], [k11])
                evac(qlatT[:, c, 0:r, :].rearrange("p t h -> p h t"), ql[:, :, 0:r], [k11], ["qlatT"])

    def attention_superblock(j, qslot):
        nkb = 4 * j + 4
        qk = ["QT%d_0" % qslot, "QT%d_128" % qslot]
        for h in range(NH):
            ob = [4 + 2 * (h % 2), 5 + 2 * (h % 2)]
            okeys = ["ps%d" % ob[0], "ps%d" % ob[1]]
            for kp2 in range(nkb // 2):
                sb_ = gbank(); ks = "ps%d" % sb_
                sT = psf(sb_).rearrange("p (a q) -> p a q", a=2)
                pslot = (h * 64 + kp2) % 3
                pt = PT[pslot]; kpt = "PT%d" % pslot
                for a in range(2):
                    kb = kp2 * 2 + a
                    masked = kb >= 4 * j
                    mm(sT[:, a, :], KT[0:DQK, h, kb * 128:(kb + 1) * 128], QT[qslot][0:DQK, h, :], True, not masked,
                       ["KT%d" % kb] + qk, [ks])
                    if masked:
                        mm(sT[:, a, :], ident_b, maskp_b[:, kb - 4 * j, :], False, True, ["ident", "maskp"], [ks])
                act(pt, sT, AF.Exp, [ks], [kpt])
                for a in range(2):
                    kb = kp2 * 2 + a
                    for half in range(2):
                        mm(psf(ob[half])[:, 0:65], pt[:, a, half * 128:(half + 1) * 128], V[:, kb, h, :],
                           kb == 0, kb == nkb - 1, [kpt, "V%d" % kb, "Vones"], [okeys[half]])
            for half in range(2):
                recip(rden[:, half:half + 1], psf(ob[half])[:, 64:65], [okeys[half]], ["rden%d" % half])
                ts(Otok[:, half, h, :], psf(ob[half])[:, 0:64], rden[:, half:half + 1], None, ALU.mult, None,
                   [okeys[half], "rden%d" % half], ["Otok%d" % half])
        for half in range(2):
            b = gbank(); k = "ps%d" % b
            oTp = psh(b).rearrange("p (c t) -> p c t", c=8)
            ofl = Otok[:, half].rearrange("p h d -> p (h d)")
            for c in range(4):
                tr(oTp[:, c, 0:128], ofl[:, c * 128:(c + 1) * 128], ident_b, ["Otok%d" % half, "ident"], [k])
            col = j * 256 + half * 128
            evac(OT[:, :, col:col + 128], oTp[:, 0:4, 0:128], [k], ["OT"])

    done('p1pre')
    token_block(xs[0:TS, :], cs_s[0:TS, :], TS, "sample", latdst=lat_s[0:TS, :], kpedst=kpe_s[0:TS, :])
    done('p1s')
    for i in range(NSB):
        for blk in range(4):
            kb = i * 4 + blk
            rows = slice(kb * 128, (kb + 1) * 128)
            token_block(xk[rows, :], cs_k[rows, :], 128, "own" if blk < 2 else "other", kb=kb, qslot=i % 2,
                        qcol=(blk % 2) * 128, latdst=lat_k[rows, :], kpedst=kpe_k[rows, :])
        done('p1k%d' % i)
        attention_superblock(i, i % 2)
        done('p1a%d' % i)

    P.barrier()
    A.release(m_persist)
    NGB = 6
    Cg = [A.alloc([8, LAT], BF16) for _ in range(NGB)]
    Rg = [A.alloc([128, ROPE], BF16) for _ in range(2)]
    ptT = A.alloc([NPAIR], I32); ptf = A.alloc([NPAIR], F32)
    rgc_f = A.alloc([16], F32)
    idxl_f = A.alloc([NPAIR, 16], F32); idxl = A.alloc([NPAIR, 16], I32)
    idxr_f = A.alloc([NPAIR, 2], F32); idxr = A.alloc([NPAIR, 2], I32)
    mpair_f = A.alloc([64], F32); mpair_b = A.alloc([64], BF16)
    mnew_f = A.alloc([NSQ * 32], F32); mnew_b = A.alloc([NSQ * 32], BF16)
    CTs = [A.alloc([2, 256], BF16) for _ in range(2)]
    kpTs = [A.alloc([256], BF16) for _ in range(2)]
    kpTq = [A.alloc([256], BF16) for _ in range(2)]
    sqs = [A.alloc([4, 256], BF16) for _ in range(2)]
    rinv_s = [A.alloc([2, NH], F32) for _ in range(2)]
    scs = [A.alloc([2, 64], F32) for _ in range(2)]
    PTs = [A.alloc([2, 64], BF16) for _ in range(2)]
    scn_f = A.alloc([NSQ * 32], F32); PnewT = A.alloc([NSQ * 32], BF16)
    olat_n = A.alloc([LAT], BF16); rden_s = A.alloc([1], F32)
    olatT_all = A.alloc([2, NH, TS], BF16)
    Os = A.alloc([512], BF16)
    NQ = NSQ * 32

    dma("sp", ptT, ptT_d, writes=["ptT"])
    dma("actq", mpair_f, maskpair_d, writes=["mpairf"])
    dma("sp", mnew_f[0:TS], masknew_d, writes=["mnewf"])
    evac(mpair_b, mpair_f, ["mpairf"], ["mpair"], eng="dve")
    evac(mnew_b[0:TS], mnew_f[0:TS], ["mnewf"], ["mnew"], eng="dve")
    for k in range(16):
        P.op("pool", lambda e, k=k: e.memset(rgc_f[:, k:k + 1], float(k)), writes=["rgc%d" % k])
    rgk = ["rgc%d" % k for k in range(16)]
    evac(ptf, ptT, ["ptT"], ["ptf"], eng="dve")
    for pr in range(NPAIR):
        stt(idxl[:, pr, :], ptf[:, pr:pr + 1].to_broadcast([128, 16]), 16.0, rgc_f, ALU.mult, ALU.add, ["ptf"] + rgk, ["idxl"])
        stt(idxr[:, pr, :], ptf[:, pr:pr + 1].to_broadcast([128, 2]), 2.0, rgc_f[:, 0:2], ALU.mult, ALU.add, ["ptf"] + rgk, ["idxr"])
    lat16 = cache_lat.rearrange("n (g e) -> (n g) e", g=16)
    rope2 = cache_rope.rearrange("n (g e) -> (n g) e", g=2)

    def gather(out2d, src, idx_ap, reads, writes):
        return P.op("poolq", lambda e: e.indirect_dma_start(out=out2d, out_offset=None, in_=src,
                                                            in_offset=bass.IndirectOffsetOnAxis(ap=idx_ap, axis=0)),
                    reads=reads, writes=writes)

    scn = psf(2)
    qlat_all = [qlatT[:, c].rearrange("p t h -> p (t h)") for c in range(2)]
    qpe_all = qpeT[0:32].rearrange("p t h -> p (t h)")
    for c in range(2):
        mm(scn[0:TS, 0:NQ], CnewT[:, c, 0:TS], qlat_all[c], c == 0, False, ["CnewT", "qlatT"], ["ps2"])
    mm(scn[0:TS, 0:NQ], kpenewT[0:32, 0:TS], qpe_all, False, False, ["kpenewT", "qpeT"], ["ps2"])
    mm(scn[0:TS, 0:NQ], ident_b[0:TS, 0:TS], mnew_b[0:TS, :], False, True, ["ident", "mnew"], ["ps2"])
    tt(scn_f[0:TS].rearrange("p (t h) -> p t h", h=NH), scn[0:TS, 0:NQ].rearrange("p (t h) -> p t h", h=NH),
       rinvnew[0:TS].unsqueeze(1).to_broadcast([TS, TS, NH]), ALU.mult, ["ps2", "rinvnew"], ["scnf"])
    act(PnewT[0:TS], scn_f[0:TS], AF.Exp, ["scnf"], ["PnewT"])

    done('p2pre')
    Tb = psh(0); kT = "ps0"
    OL = psf(7); DEN = psf(1)
    it = 0
    for pr in range(NPAIR):
        rgt = Rg[pr % 2]; krg = "Rg%d" % (pr % 2)
        for hf in range(2):
            gather(rgt[:, hf * 64:(hf + 1) * 64, :].rearrange("p r d -> p (r d)"), rope2, idxr[:, pr, hf:hf + 1], ["idxr"], [krg])
        qlp = [qlatT[:, c, pr * 8:(pr + 1) * 8, :].rearrange("p t h -> p (t h)") for c in range(2)]
        qpp = qpeT[0:32, pr * 8:(pr + 1) * 8, :].rearrange("p t h -> p (t h)")
        for rg in range(16):
            gi = (pr * 16 + rg) % NGB
            cg = Cg[gi]; kcg = "Cg%d" % gi
            gather(cg.rearrange("p r c -> p (r c)"), lat16, idxl[:, pr, rg:rg + 1], ["idxl"], [kcg])
            for r2 in range(4):
                sl = it % 2; it += 1
                r0 = r2 * 2
                rglob = rg * 8 + r0
                for a in range(2):
                    for c in range(2):
                        tr(Tb[:, c * 256 + a * 128:c * 256 + (a + 1) * 128], cg[:, r0 + a, c * 128:(c + 1) * 128], ident_b, [kcg, "ident"], [kT])
                for a in range(2):
                    tr(Tb[0:32, 512 + sl * 256 + a * 128:512 + sl * 256 + (a + 1) * 128], rgt[:, rglob + a, :], ident_b, [krg, "ident"], [kT])
                evac(CTs[sl], Tb[:, 0:512].rearrange("p (c k) -> p c k", c=2), [kT], ["CTs%d" % sl], eng="act")
                done('p2it0@%d' % it)
                evac(kpTs[sl][0:32], Tb[0:32, 512 + sl * 256:768 + sl * 256], [kT], ["kpTs%d" % sl], eng="act")
                done('p2a1@%d' % it)
                tt(kpTq[sl][0:32], kpTs[sl][0:32], kpTs[sl][0:32], ALU.mult, ["kpTs%d" % sl], ["kpTq%d" % sl])
                done('p2a@%d' % it)
                kb0 = 2 + 2 * sl
                kkn = ["ps%d" % kb0, "ps%d" % (kb0 + 1)]
                for hc in range(4):
                    dst = psf(kb0 + hc // 2)[:, (hc % 2) * 256:(hc % 2) * 256 + 256]
                    for c in range(2):
                        mm(dst, wuk_b[:, c, hc * 128:(hc + 1) * 128], CTs[sl][:, c, :], c == 0, c == 1, ["wuk", "CTs%d" % sl], [kkn[hc // 2]])
                done('p2b@%d' % it)
                for b2 in range(2):
                    act(sqs[sl][:, 2 * b2:2 * b2 + 2, :].rearrange("p a k -> p (a k)"), psf(kb0 + b2), AF.Square, [kkn[b2]], ["sqs%d_%d" % (sl, b2)])
                done('p2kn@%d' % it)
                so = sl * 256
                SS = psf(6)
                for a in range(2):
                    for hc in range(4):
                        mm(SS[:, so + a * 8:so + a * 8 + 8], sqs[sl][:, hc, a * 128:(a + 1) * 128], ind_b[:, hc, :], hc == 0, False,
                           ["sqs%d_%d" % (sl, hc // 2), "ind"], ["ps6_%d" % sl])
                    mm(SS[:, so + a * 8:so + a * 8 + 8], kpTq[sl][0:32, a * 128:(a + 1) * 128], ones_b[0:32, 0:8], False, True,
                       ["kpTq%d" % sl, "onesb"], ["ps6_%d" % sl])
                rv = rinv_s[sl]
                rsqrt_of(rv.rearrange("p a h -> p (a h)"), SS[:, so:so + 16], 1.0 / DQK, 128, "ps6_%d" % sl, "rinvs%d" % sl)
                done('p2ss@%d' % it)
                for a in range(2):
                    dst = SS[:, so + 64 + a * 64:so + 128 + a * 64]
                    for c in range(2):
                        mm(dst, CTs[sl][:, c, a * 128:(a + 1) * 128], qlp[c], c == 0, False, ["CTs%d" % sl, "qlatT"], ["ps6_%d" % sl])
                    mm(dst, kpTs[sl][0:32, a * 128:(a + 1) * 128], qpp, False, False, ["kpTs%d" % sl, "qpeT"], ["ps6_%d" % sl])
                    mm(dst, ident_b, mpair_b, False, True, ["ident", "mpair"], ["ps6_%d" % sl])
                tt(scs[sl].rearrange("p a (t h) -> p a t h", h=NH),
                   SS[:, so + 64:so + 192].rearrange("p (a t h) -> p a t h", a=2, h=NH),
                   rv.unsqueeze(2).to_broadcast([128, 2, 8, NH]), ALU.mult, ["ps6_%d" % sl, "rinvs%d" % sl], ["scs%d" % sl])
                act(PTs[sl], scs[sl], AF.Exp, ["scs%d" % sl], ["PTs%d" % sl])
                done('p2sc@%d' % it)
                for a in range(2):
                    if it == 2 and a == 0:
                        done('p2it1')
                    first = (rg == 0 and r2 == 0 and a == 0)
                    mm(OL[0:64, 0:LAT], PTs[sl][:, a, :], cg[:, r0 + a, :], first, False, ["PTs%d" % sl, kcg], ["ps7"])
                    mm(DEN[0:64, 0:8], PTs[sl][:, a, :], ones_b[:, 0:8], first, False, ["PTs%d" % sl, "onesb"], ["ps1"])
                done('p2ol@%d' % it)
        done('p2loop')
        mm(OL[0:64, 0:LAT], PnewT[0:TS, pr * 64:(pr + 1) * 64], Cnew[0:TS, :], False, True, ["PnewT", "Cnew"], ["ps7"])
        mm(DEN[0:64, 0:8], PnewT[0:TS, pr * 64:(pr + 1) * 64], ones_b[0:TS, 0:8], False, True, ["PnewT", "onesb"], ["ps1"])
        recip(rden_s[0:64], DEN[0:64, 0:1], ["ps1"], ["rdens"])
        ts(olat_n[0:64], OL[0:64, 0:LAT], rden_s[0:64, 0:1], None, ALU.mult, None, ["ps7", "rdens"], ["olatn"])
        for c in range(2):
            tr(Tb[:, c * 64:(c + 1) * 64], olat_n[0:64, c * 128:(c + 1) * 128], ident_b[0:64, 0:64], ["olatn", "ident"], [kT])
        for c in range(2):
            evac(olatT_all[:, c, :, pr * 8:(pr + 1) * 8], Tb[:, c * 64:(c + 1) * 64].rearrange("p (t h) -> p h t", h=NH), [kT], ["olatT"])
    OSP = psf(2)
    for h in range(NH):
        for c in range(2):
            mm(OSP[0:TS, h * 64:(h + 1) * 64], olatT_all[:, c, h, :], wuv_b[:, c, h * 64:(h + 1) * 64], c == 0, c == 1, ["olatT", "wuv"], ["ps2"])
    evac(Os[0:TS], OSP[0:TS, :], ["ps2"], ["Os"])
    for c in range(4):
        tr(Tb[:, c * 64:c * 64 + TS], Os[0:TS, c * 128:(c + 1) * 128], ident_b[0:TS, 0:TS], ["Os", "ident"], [kT])
    for c in range(4):
        evac(OT[:, c, TP:TP + TS], Tb[:, c * 64:c * 64 + TS], [kT], ["OT"])

    done('p2')
    P.barrier()
    A.release(m_persist)
    NG, SBG, SQG = cfg.NG, cfg.SBG, cfg.SQG
    TGp, TGs, TH = SBG * 256, SQG * 4, SBG * 32
    TG = TGp + TGs
    NBLK = SBG * 2 + 1
    x1 = A.alloc([NBLK, D], F32)
    hTg = A.alloc([8, TG + TH], BF16)
    YT = A.alloc([4, TG], BF16)
    lngT = A.alloc([4], F32); lnbT = A.alloc([4], F32); convwT = A.alloc([4, CW], F32); convbT = A.alloc([4], F32)
    sm3 = A.alloc([4 * NBLK + 8], F32)
    hn3 = [A.alloc([D], BF16) for _ in range(2)]
    m_p3 = A.mark()
    dma("sp", lngT, lngT_d, writes=["lngT"]); dma("actq", lnbT, lnbT_d, writes=["lnbT"])
    dma("sp", convwT, convwT_d, writes=["convwT"]); dma("actq", convbT, convbT_d, writes=["convbT"])
    ring8 = {"g": 0}

    def gb8():
        ring8["g"] = (ring8["g"] + 1) % 8
        return ring8["g"]

    def ntiles(total, step):
        return [(n0, min(step, total - n0)) for n0 in range(0, total, step)]

    def to_feature_major(src_rows, r, col, gT, kgT, kx, slot, dstT, kdst):
        hb = hn3[slot % 2]; kh = "hn3_%d" % (slot % 2)
        c0 = 4 * NBLK + (slot % 2) * 4
        act(hb[0:r], src_rows, AF.Square, [kx], [kh, "sm3_%d" % (slot % 2)], accum=sm3[0:r, c0:c0 + 1])
        rsqrt_of(sm3[0:r, c0 + 1:c0 + 2], sm3[0:r, c0:c0 + 1], 1.0 / D, r, "sm3_%d" % (slot % 2), "sm3b_%d" % (slot % 2))
        ts(hb[0:r], src_rows, sm3[0:r, c0 + 1:c0 + 2], None, ALU.mult, None, [kx, "sm3b_%d" % (slot % 2)], [kh])
        b = gb8(); k = "ps%d" % b
        pT = psh(b).rearrange("p (c t) -> p c t", c=8)
        for c in range(8):
            tr(pT[:, c, 0:r], hb[0:r, c * 128:(c + 1) * 128], ident_b[0:r, 0:r], [kh, "ident"], [k])
        tt(dstT[:, :, col:col + r], pT[:, :, 0:r], gT.unsqueeze(2).to_broadcast([128, 8, r]), ALU.mult, [k, kgT], [kdst])

    for g in range(NG):
        sb0, sq0 = g * SBG, g * SQG
        blocks = []
        for s_ in range(SBG):
            for b2 in range(2):
                rows = slice((sb0 + s_) * 512 + b2 * 128, (sb0 + s_) * 512 + b2 * 128 + 128)
                yrows = slice((sb0 + s_) * 256 + b2 * 128, (sb0 + s_) * 256 + b2 * 128 + 128)
                blocks.append((xk[rows, :], y_own[yrows, :], 128, s_ * 256 + b2 * 128))
        blocks.append((xs[sq0 * 4:sq0 * 4 + TGs, :], y_own[TP + sq0 * 4:TP + sq0 * 4 + TGs, :], TGs, TGp))
        A.release(m_p3)
        ubuf = A.alloc([4, SBG, 288], BF16); ubuf_s = A.alloc([4, SQG, 36], BF16)
        wglu = A.alloc([8, 1024], BF16)
        xhs = A.alloc([D], F32)
        ycv = A.alloc([4, TG], F32); ysq = A.alloc([4, 512], F32)
        mu = A.alloc([512], F32); var = A.alloc([512], F32); rs = A.alloc([512], F32)
        sg = [A.alloc([256], F32) for _ in range(2)]
        sts = A.alloc([DCONV], F32); usn = A.alloc([4, 32], BF16); cstp = A.alloc([DCONV], F32)
        dma("poolq", wglu, w_in[:, 0:1024].rearrange("(c p) n -> p c n", p=128), writes=["wglu"])
        for bi, (xsrc, ydst, r, col) in enumerate(blocks):
            dma(dmaq(), x1[0:r, bi, :], xsrc, writes=["x1_%d" % bi])
            to_feature_major(x1[0:r, bi, :], r, col, gmixT, "gmixT", "x1_%d" % bi, bi, hTg, "hTg")
        for h0, hr in ntiles(TH, 128):
            dma(dmaq(), xhs[0:hr], xh[sb0 * 32 + h0:sb0 * 32 + h0 + hr, :], writes=["xhs"])
            to_feature_major(xhs[0:hr], hr, TG + h0, gmixT, "gmixT", "xhs", h0 // 128, hTg, "hTg")
        P.op("pool", lambda e: e.memset(ubuf_s, 0.0), writes=["ubuf_s"])
        gl_tiles = [(s_ * 256, 256, ("sb", s_)) for s_ in range(SBG)] + [(TGp, TGs, ("smp", 0)), (TG, TH, ("halo", 0))]
        gi_ = 0
        for c in range(4):
            for (n0, n, kind) in gl_tiles:
                ba = gb8(); bg = gb8()
                for k in range(8):
                    mm(psf(ba)[:, 0:n], wglu[:, k, c * 128:(c + 1) * 128], hTg[:, k, n0:n0 + n], k == 0, k == 7, ["wglu", "hTg"], ["ps%d" % ba])
                for k in range(8):
                    mm(psf(bg)[:, 0:n], wglu[:, k, 512 + c * 128:512 + (c + 1) * 128], hTg[:, k, n0:n0 + n], k == 0, k == 7, ["wglu", "hTg"], ["ps%d" % bg])
                sgi = sg[gi_ % 2]; ksg = "sg%d" % (gi_ % 2); gi_ += 1
                act(sgi[:, 0:n], psf(bg)[:, 0:n], AF.Sigmoid, ["ps%d" % bg], [ksg])
                if kind[0] == "sb":
                    dst = ubuf[:, c, kind[1], 32:288]; a_ = psf(ba)[:, 0:n]; s_v = sgi[:, 0:n]
                elif kind[0] == "smp":
                    dst = ubuf_s[:, c, :, 30:34]
                    a_ = psf(ba)[:, 0:n].rearrange("p (s t) -> p s t", t=4); s_v = sgi[:, 0:n].rearrange("p (s t) -> p s t", t=4)
                else:
                    dst = ubuf[:, c, :, 0:32]
                    a_ = psf(ba)[:, 0:n].rearrange("p (s t) -> p s t", t=32); s_v = sgi[:, 0:n].rearrange("p (s t) -> p s t", t=32)
                tt(dst, a_, s_v, ALU.mult, ["ps%d" % ba, ksg], ["ubuf%d" % c if kind[0] != "smp" else "ubuf_s"])
        for s0_, ns in ntiles(SQG, 4):
            rws = ns * 30
            dma(dmaq(), sts[0:rws], state_d[(sq0 + s0_) * 30:(sq0 + s0_) * 30 + rws, :], writes=["sts"])
            for c in range(4):
                b = gb8()
                tr(psf(b)[:, 0:rws], sts[0:rws, c * 128:(c + 1) * 128], ident_f[0:rws, 0:rws], ["sts", "identf"], ["ps%d" % b])
                evac(ubuf_s[:, c, s0_:s0_ + ns, 0:30], psf(b)[:, 0:rws].rearrange("p (s t) -> p s t", t=30), ["ps%d" % b], ["ubuf_s"])
        dma("sp", cst_s[sq0:sq0 + SQG, 0:26, :], state_d.rearrange("(s t) c -> s t c", t=30)[sq0:sq0 + SQG, 4:30, :],
            semkey="o_cs", final=True)
        for c in range(4):
            yp = ycv[:, c, 0:TGp].rearrange("p (s t) -> p s t", t=256)
            ys_ = ycv[:, c, TGp:TG].rearrange("p (s t) -> p s t", t=4)
            ts(yp, ubuf[:, c, :, 2:258], convwT[:, c, 0:1], convbT[:, c:c + 1], ALU.mult, ALU.add, ["ubuf%d" % c, "convwT", "convbT"], ["ycv%d" % c])
            ts(ys_, ubuf_s[:, c, :, 0:4], convwT[:, c, 0:1], convbT[:, c:c + 1], ALU.mult, ALU.add, ["ubuf_s", "convwT", "convbT"], ["ycvs%d" % c])
            for j in range(1, CW):
                stt(yp, ubuf[:, c, :, 2 + j:258 + j], convwT[:, c, j:j + 1], yp, ALU.mult, ALU.add, ["ubuf%d" % c, "convwT"], ["ycv%d" % c])
                stt(ys_, ubuf_s[:, c, :, j:j + 4], convwT[:, c, j:j + 1], ys_, ALU.mult, ALU.add, ["ubuf_s", "convwT"], ["ycvs%d" % c])
        ykeys = ["ycv%d" % c for c in range(4)] + ["ycvs%d" % c for c in range(4)]
        for (n0, n) in ntiles(TG, 512):
            act(ysq[:, :, 0:n], ycv[:, :, n0:n0 + n], AF.Square, ykeys, ["ysq"])
            b1 = gb8(); b2 = gb8()
            for c in range(4):
                mm(psf(b1)[:, 0:n], ones_f, ycv[:, c, n0:n0 + n], c == 0, c == 3, ["onesf"] + ykeys, ["ps%d" % b1])
            for c in range(4):
                mm(psf(b2)[:, 0:n], ones_f, ysq[:, c, 0:n], c == 0, c == 3, ["onesf", "ysq"], ["ps%d" % b2])
            P.op("act", lambda e, b1=b1, n=n: e.mul(mu[:, 0:n], psf(b1)[:, 0:n], 1.0 / DCONV), reads=["ps%d" % b1], writes=["mu"])
            tt(var[:, 0:n], mu[:, 0:n], mu[:, 0:n], ALU.mult, ["mu"], ["var"])
            stt(var[:, 0:n], psf(b2)[:, 0:n], 1.0 / DCONV, var[:, 0:n], ALU.mult, ALU.subtract, ["ps%d" % b2, "var"], ["var"])
            rsqrt_of(rs[:, 0:n], var[:, 0:n], 1.0, 128, "var", "rs")
            for c in range(4):
                tt(ysq[:, c, 0:n], ycv[:, c, n0:n0 + n], mu[:, 0:n], ALU.subtract, ykeys + ["mu", "ysq"], ["ysq"])
                tt(ysq[:, c, 0:n], ysq[:, c, 0:n], rs[:, 0:n], ALU.mult, ["ysq", "rs"], ["ysq"])
                act(YT[:, c, n0:n0 + n], ysq[:, c, 0:n], AF.Silu, ["ysq", "lngT", "lnbT"], ["YT"], scale=lngT[:, c:c + 1], bias=lnbT[:, c:c + 1])
        if g == NG - 1:
            for c in range(4):
                b = gb8()
                tr(psh(b)[0:32, 0:128], ubuf[:, c, SBG - 1, 256:288], ident_b, ["ubuf%d" % c, "ident"], ["ps%d" % b])
                evac(cstp[0:32, c * 128:(c + 1) * 128], psh(b)[0:32, 0:128], ["ps%d" % b], ["cstp"])
            dma("sp", cst_p, cstp[0:32], reads=["cstp"], semkey="o_cp", final=True)
        for c in range(4):
            evac(usn[:, c, 0:TGs].rearrange("p (s t) -> p s t", t=4), ubuf_s[:, c, :, 30:34], ["ubuf_s"], ["usn"], eng="dve")
        for c in range(4):
            b = gb8()
            tr(psh(b)[0:TGs, 0:128], usn[:, c, 0:TGs], ident_b, ["usn", "ident"], ["ps%d" % b])
            evac(sts[0:TGs, c * 128:(c + 1) * 128], psh(b)[0:TGs, 0:128], ["ps%d" % b], ["sts"])
        for s_ in range(SQG):
            dma(dmaq(), cst_s[sq0 + s_, 26:30, :], sts[s_ * 4:(s_ + 1) * 4, :], reads=["sts"], semkey="o_cs2", final=True)
        done('p3a%d' % g)
        P.barrier()
        A.release(m_p3)
        mergedT = A.alloc([8, TG], BF16)
        wgc = A.alloc([8, 1024], BF16); wgm = A.alloc([8, 1024], BF16)
        wco_b = A.alloc([4, 1024], BF16); wo_b = A.alloc([4, 1024], BF16); wout_b = A.alloc([8, 1024], BF16)
        sgc = [A.alloc([256], F32) for _ in range(2)]; sgm = [A.alloc([256], F32) for _ in range(2)]
        t1 = [A.alloc([256], F32) for _ in range(2)]; t2 = [A.alloc([256], F32) for _ in range(2)]
        dma("poolq", wgc, w_in[:, I_GC:I_GC + 1024].rearrange("(c p) n -> p c n", p=128), writes=["wgc"])
        dma("poolq", wco_b, w_co.rearrange("(c p) n -> p c n", p=128), writes=["wco"])
        dma("poolq", wgm, w_in[:, I_GM:I_GM + 1024].rearrange("(c p) n -> p c n", p=128), writes=["wgm"])
        dma("poolq", wo_b, w_o.rearrange("(c p) n -> p c n", p=128), writes=["wo"])
        dma("poolq", wout_b, w_out.rearrange("(c p) n -> p c n", p=128), writes=["wout"])
        mt = [(s_ * 256, 256, (sb0 + s_) * 256) for s_ in range(SBG)] + [(TGp, TGs, TP + sq0 * 4)]
        mi = 0
        for m in range(8):
            for (n0, n, ocol) in mt:
                s2 = mi % 2; mi += 1
                bA = gb8(); bB = gb8()
                kA, kB = "ps%d" % bA, "ps%d" % bB
                for k in range(8):
                    mm(psf(bA)[:, 0:n], wgc[:, k, m * 128:(m + 1) * 128], hTg[:, k, n0:n0 + n], k == 0, k == 7, ["wgc", "hTg"], [kA])
                for k in range(4):
                    mm(psf(bA)[:, 256:256 + n], wco_b[:, k, m * 128:(m + 1) * 128], YT[:, k, n0:n0 + n], k == 0, k == 3, ["wco", "YT"], [kA])
                for k in range(8):
                    mm(psf(bB)[:, 0:n], wgm[:, k, m * 128:(m + 1) * 128], hTg[:, k, n0:n0 + n], k == 0, k == 7, ["wgm", "hTg"], [kB])
                for k in range(4):
                    mm(psf(bB)[:, 256:256 + n], wo_b[:, k, m * 128:(m + 1) * 128], OT[:, k, ocol:ocol + n], k == 0, k == 3, ["wo", "OT"], [kB])
                act(sgc[s2][:, 0:n], psf(bA)[:, 0:n], AF.Sigmoid, [kA], ["sgc%d" % s2])
                tt(t1[s2][:, 0:n], psf(bA)[:, 256:256 + n], sgc[s2][:, 0:n], ALU.mult, [kA, "sgc%d" % s2], ["t1_%d" % s2])
                act(sgm[s2][:, 0:n], psf(bB)[:, 0:n], AF.Sigmoid, [kB], ["sgm%d" % s2])
                tt(t2[s2][:, 0:n], psf(bB)[:, 256:256 + n], sgm[s2][:, 0:n], ALU.mult, [kB, "sgm%d" % s2], ["t2_%d" % s2])
                tt(mergedT[:, m, n0:n0 + n], t1[s2][:, 0:n], t2[s2][:, 0:n], ALU.add, ["t1_%d" % s2, "t2_%d" % s2], ["mergedT"])
        for bi, (xsrc, ydst, r, col) in enumerate(blocks):
            for half in range(2):
                b = gb8(); k_ = "ps%d" % b
                for k in range(8):
                    mm(psf(b)[0:r, :], mergedT[:, k, col:col + r], wout_b[:, k, half * 512:(half + 1) * 512], k == 0, k == 7, ["mergedT", "wout"], [k_])
                tt(x1[0:r, bi, half * 512:(half + 1) * 512], psf(b)[0:r, :], x1[0:r, bi, half * 512:(half + 1) * 512], ALU.add,
                   [k_, "x1_%d" % bi], ["x1_%d" % bi])
        done('p3b%d' % g)
        P.barrier()
        A.release(m_p3)
        h2T = hTg
        actT = [A.alloc([4, TG], BF16) for _ in range(2)]
        wg_b = [A.alloc([8, 512], BF16) for _ in range(2)]; wu_b = [A.alloc([8, 512], BF16) for _ in range(2)]
        wd_b = [A.alloc([4, 1024], BF16) for _ in range(2)]
        sgt = [A.alloc([512], F32) for _ in range(2)]
        for bi, (xsrc, ydst, r, col) in enumerate(blocks):
            to_feature_major(x1[0:r, bi, :], r, col, gffnT, "gffnT", "x1_%d" % bi, bi, h2T, "h2T")
        fgroups = ntiles(DFF // 128, 4)
        si = 0
        for fg, (f0, nf) in enumerate(fgroups):
            s2 = fg % 2
            dma("poolq", wg_b[s2][:, :, 0:nf * 128], w_gate[:, f0 * 128:(f0 + nf) * 128].rearrange("(c p) n -> p c n", p=128), writes=["wg%d" % s2])
            dma("poolq", wu_b[s2][:, :, 0:nf * 128], w_up[:, f0 * 128:(f0 + nf) * 128].rearrange("(c p) n -> p c n", p=128), writes=["wu%d" % s2])
            dma("poolq", wd_b[s2][:, 0:nf, :], w_down[f0 * 128:(f0 + nf) * 128, :].rearrange("(c p) n -> p c n", p=128), writes=["wd%d" % s2])
            for fi in range(nf):
                for (n0, n) in ntiles(TG, 512):
                    bG = gb8(); bU = gb8()
                    for k in range(8):
                        mm(psf(bG)[:, 0:n], wg_b[s2][:, k, fi * 128:(fi + 1) * 128], h2T[:, k, n0:n0 + n], k == 0, k == 7, ["wg%d" % s2, "h2T"], ["ps%d" % bG])
                    for k in range(8):
                        mm(psf(bU)[:, 0:n], wu_b[s2][:, k, fi * 128:(fi + 1) * 128], h2T[:, k, n0:n0 + n], k == 0, k == 7, ["wu%d" % s2, "h2T"], ["ps%d" % bU])
                    st_ = sgt[si % 2]; kst = "sgt%d" % (si % 2); si += 1
                    act(st_[:, 0:n], psf(bG)[:, 0:n], AF.Silu, ["ps%d" % bG], [kst])
                    tt(actT[s2][:, fi, n0:n0 + n], psf(bU)[:, 0:n], st_[:, 0:n], ALU.mult, ["ps%d" % bU, kst], ["actT%d" % s2])
            for bi, (xsrc, ydst, r, col) in enumerate(blocks):
                for half in range(2):
                    b = gb8(); k_ = "ps%d" % b
                    for fi in range(nf):
                        mm(psf(b)[0:r, :], actT[s2][:, fi, col:col + r], wd_b[s2][:, fi, half * 512:(half + 1) * 512], fi == 0, fi == nf - 1,
                           ["actT%d" % s2, "wd%d" % s2], [k_])
                    tt(x1[0:r, bi, half * 512:(half + 1) * 512], psf(b)[0:r, :], x1[0:r, bi, half * 512:(half + 1) * 512], ALU.add,
                       [k_, "x1_%d" % bi], ["x1_%d" % bi])
        for bi, (xsrc, ydst, r, col) in enumerate(blocks):
            dma(dmaq(), ydst, x1[0:r, bi, :], reads=["x1_%d" % bi], semkey="o_y%d" % (bi % 4), final=True)
        P.barrier()

    P.emit()
    cfg.arena_peak = A.peak


def _host_inputs(cfg, inp):
    f32 = np.float32
    NSB, NSQ, TS = cfg.NSB, cfg.NSQ, cfg.TS
    inv_freq = (1.0 / (10000.0 ** (np.arange(0, ROPE, 2, dtype=f32) / f32(ROPE)))).astype(f32)

    def cs_table(pos):
        ang = pos.astype(f32)[:, None] * inv_freq[None, :]
        c, s = np.cos(ang).astype(f32), np.sin(ang).astype(f32)
        return np.ascontiguousarray(np.concatenate([c, c, -s, s], axis=1))

    ident = np.eye(128, dtype=f32)
    ind = np.zeros((128, 4, 8), f32)
    for c in range(4):
        for p in range(128):
            ind[p, c, (c * 128 + p) // 64] = 1.0
    maskpair = np.full((128, 64), NEG, f32)
    for s2 in range(2):
        maskpair[s2 * 64:(s2 + 1) * 64, s2 * 32:(s2 + 1) * 32] = 0.0
    masknew = np.full((TS, NSQ * 32), NEG, f32)
    for s in range(NSQ):
        for t in range(4):
            for q in range(t, 4):
                masknew[s * 4 + t, s * 32 + q * 8:s * 32 + q * 8 + 8] = 0.0
    cs_s = cs_table(np.tile(cfg.PAST + np.arange(4), NSQ))
    w = {k: np.ascontiguousarray(inp[k][0]) for k in ("w_in", "w_uq", "w_o_mla", "w_conv_out", "w_out", "w_gate", "w_up", "w_down")}
    w["w_uk"] = np.ascontiguousarray(inp["w_uk"][0].reshape(LAT, 512))
    w["w_uv"] = np.ascontiguousarray(inp["w_uv"][0].reshape(LAT, 512))
    shared = dict(w)
    shared.update(
        ident=ident, ind=ind, maskpair=maskpair, masknew=masknew, cs_s=cs_s,
        cache_lat=inp["cache_kv_latent"][0].reshape(cfg.NPHYS, 128 * LAT),
        cache_rope=inp["cache_k_rope"][0].reshape(cfg.NPHYS, 128 * ROPE),
        gmixT=np.ascontiguousarray(inp["norm_mix_g"][0].reshape(8, 128).T),
        gffnT=np.ascontiguousarray(inp["norm_ffn_g"][0].reshape(8, 128).T),
        convwT=np.ascontiguousarray(inp["conv_w"][0].reshape(CW, 4, 128).transpose(2, 1, 0)),
        convbT=np.ascontiguousarray(inp["conv_b"][0].reshape(4, 128).T),
        lngT=np.ascontiguousarray(inp["conv_ln_g"][0].reshape(4, 128).T),
        lnbT=np.ascontiguousarray(inp["conv_ln_b"][0].reshape(4, 128).T),
        gqa=inp["q_a_norm_g"].reshape(1, QL), gkv=inp["kv_a_norm_g"].reshape(1, LAT),
        gq=inp["q_norm_g"].reshape(1, 80), gk=inp["k_norm_g"].reshape(1, 80),
    )
    maps, meta = [], []
    for core in range(cfg.NC):
        b, p = core // 2, core % 2
        pos_k, pos_own = [], []
        for i in range(NSB):
            own = (2 * i + p) * 256 + np.arange(256)
            oth = (2 * i + 1 - p) * 256 + np.arange(256)
            pos_k += [own, oth]
            pos_own.append(own)
        pos_k = np.concatenate(pos_k); pos_own = np.concatenate(pos_own)
        xp = inp["x_prompt"][b]
        xh = np.zeros((NSB * 32, D), f32)
        for i in range(NSB):
            st = (2 * i + p) * 256
            if st > 0:
                xh[i * 32 + 2:(i + 1) * 32] = xp[st - 30:st]
        maskp = np.full((128, 4, 256), NEG, f32)
        kk = np.arange(128)[:, None]; qq = np.arange(256)[None, :]
        for m in range(2):
            maskp[:, m, :] = np.where(m * 128 + kk <= qq, 0.0, NEG)
        if p == 1:
            maskp[:, 2:4, :] = 0.0
        sq0 = core * NSQ
        pt = inp["page_table"][sq0:sq0 + NSQ]
        ptT = np.ascontiguousarray(pt.reshape(cfg.NPAIR, 128).T.astype(np.int32))
        m = dict(shared)
        m.update(
            xk=np.ascontiguousarray(xp[pos_k]), xs=np.ascontiguousarray(inp["x_sample"][sq0:sq0 + NSQ].reshape(TS, D)), xh=xh,
            cs_k=cs_table(pos_k), maskp=maskp,
            state=np.ascontiguousarray(inp["state_conv"][0, sq0:sq0 + NSQ].reshape(NSQ * 30, DCONV)), ptT=ptT,
        )
        maps.append(m)
        meta.append((b, p, pos_own, sq0))
    return maps, meta


_CACHE = {}


def run(cfg, inputs):
    key = (cfg.NB, cfg.SEQ, cfg.DB)
    if key not in _CACHE:
        _CACHE[key] = build(cfg)[0]
    nc = _CACHE[key]
    inp = {k: np.asarray(v) for k, v in inputs.items()}
    maps, meta = _host_inputs(cfg, inp)
    res = run_bass_kernel_spmd(nc, maps, core_ids=list(range(cfg.NC)))
    f32 = np.float32
    NB, SEQ, DB, NSQ = cfg.NB, cfg.SEQ, cfg.DB, cfg.NSQ
    y_p = np.zeros((NB, SEQ, D), f32); y_s = np.zeros((DB, 4, D), f32)
    lat_p = np.zeros((1, NB, SEQ, LAT), f32); kpe_p = np.zeros((1, NB, SEQ, ROPE), f32)
    cs_p = np.zeros((1, NB, 30, DCONV), f32)
    lat_s = np.zeros((1, DB, 4, LAT), f32); kpe_s = np.zeros((1, DB, 4, ROPE), f32); cs_s = np.zeros((1, DB, 30, DCONV), f32)
    for core, (b, p, pos_own, sq0) in enumerate(meta):
        r = res.results[core]
        y_p[b, pos_own] = r["y_own"][:cfg.TP]
        y_s[sq0:sq0 + NSQ] = r["y_own"][cfg.TP:].reshape(NSQ, 4, D)
        own_rows = np.concatenate([i * 512 + np.arange(256) for i in range(cfg.NSB)])
        lat_p[0, b, pos_own] = r["lat_k"][own_rows]
        kpe_p[0, b, pos_own] = r["kpe_k"][own_rows]
        if p == 1:
            cs_p[0, b] = r["cst_p"][2:32]
        lat_s[0, sq0:sq0 + NSQ] = r["lat_s"].reshape(NSQ, 4, LAT)
        kpe_s[0, sq0:sq0 + NSQ] = r["kpe_s"].reshape(NSQ, 4, ROPE)
        cs_s[0, sq0:sq0 + NSQ] = r["cst_s"]
    return (y_p, y_s, lat_p, kpe_p, cs_p, lat_s, kpe_s, cs_s)


def kernel(**inputs):
    return run(Cfg(), inputs)
```

```python
import math
import numpy as np
import concourse.bass as bass
import concourse.mybir as mybir
from concourse.bass_utils import run_bass_kernel_spmd

F32 = mybir.dt.float32
BF16 = mybir.dt.bfloat16
I32 = mybir.dt.int32
U8 = mybir.dt.uint8
AF = mybir.ActivationFunctionType
ALU = mybir.AluOpType
AX = mybir.AxisListType

D = 1024
NH = 8
DQK = 96
LAT = 256
ROPE = 32
DCONV = 512
CW = 31
DFF = 2816
DIN = 3744
QL = 384
EPS = 1e-6
NEG = -30000.0
I_CQ = 1024
I_KV = 1408
I_GC = 1696
I_GM = 2720
DMAQ = ("sp", "actq", "poolq")


class _Op:
    __slots__ = ("eng", "fn", "reads", "writes", "is_dma", "semkey", "waits", "ticket", "sem", "idx",
                 "needs_sig", "final")


def _phys(eng):
    return {"pe": "pe", "act": "act", "dve": "dve", "pool": "pool", "sp": "sp", "actq": "act",
            "poolq": "pool"}[eng]


class Prog:
    def __init__(self, nc):
        self.nc = nc
        self.ops = []
        self.last_writer = {}
        self.readers = {}
        self.bar = None
        self.bar_seen = set()
        self.last_on = {}
        self.dma_since = []

    def barrier(self):
        deps = set(self.last_on.values()) | set(self.dma_since)
        if self.bar is not None:
            deps |= self.bar
        self.bar = deps
        self.bar_seen = set()
        self.dma_since = []

    def op(self, eng, fn, reads=(), writes=(), semkey=None, final=False):
        o = _Op()
        o.eng, o.fn = eng, fn
        o.reads, o.writes = tuple(reads), tuple(writes)
        o.is_dma = eng in DMAQ
        o.semkey = semkey
        o.final = final
        o.idx = len(self.ops)
        o.needs_sig = False
        deps = set()
        for r in o.reads:
            w = self.last_writer.get(r)
            if w is not None:
                deps.add(w)
        for w_ in o.writes:
            w = self.last_writer.get(w_)
            if w is not None:
                deps.add(w)
            deps.update(self.readers.get(w_, ()))
        ph = _phys(eng)
        if self.bar is not None and ph not in self.bar_seen:
            deps |= self.bar
            self.bar_seen.add(ph)
        o.waits = deps
        for r in o.reads:
            self.readers.setdefault(r, []).append(o.idx)
        for w_ in o.writes:
            self.last_writer[w_] = o.idx
            self.readers[w_] = []
        self.ops.append(o)
        self.last_on[ph] = o.idx
        if o.is_dma:
            self.dma_since.append(o.idx)
        return o

    def emit(self, final_wait_eng="sp"):
        nc, ops = self.nc, self.ops
        streams = {"pe": [], "act": [], "dve": [], "pool": [], "sp": []}
        for o in ops:
            streams[_phys(o.eng)].append(o)
        for o in ops:
            keep = set()
            ph = _phys(o.eng)
            for d in o.waits:
                p = ops[d]
                if _phys(p.eng) == ph and not p.is_dma:
                    if ph == "pe":
                        continue
                    if not (set(p.writes) & set(o.reads)):
                        continue
                keep.add(d)
            o.waits = keep
            for d in keep:
                ops[d].needs_sig = True
        semh, semc = {}, {}

        def getsem(key):
            if key not in semh:
                semh[key] = nc.alloc_semaphore("s%d" % len(semh))
                semc[key] = 0
            return semh[key]

        finals = []
        for o in ops:
            if o.is_dma:
                key = ("dma", o.semkey if o.semkey is not None else (o.writes[0] if o.writes else o.reads[0]))
                o.sem = getsem(key)
                semc[key] += 16
                o.ticket = semc[key]
                o.needs_sig = True
                if o.final:
                    finals.append(o)
            elif o.needs_sig:
                key = ("eng", _phys(o.eng))
                o.sem = getsem(key)
                semc[key] += 1
                o.ticket = semc[key]
        self.n_sems = len(semh)

        def run_stream(name, e):
            waited = {}
            for o in streams[name]:
                need = {}
                for d in o.waits:
                    p = ops[d]
                    k = id(p.sem)
                    if k not in need or need[k][1] < p.ticket:
                        need[k] = (p.sem, p.ticket)
                for k, (s, t) in need.items():
                    if waited.get(k, 0) >= t:
                        continue
                    e.wait_ge(s, t)
                    waited[k] = t
                ins = o.fn(e)
                if o.needs_sig:
                    ins.then_inc(o.sem, 16 if o.is_dma else 1)
            if name == final_wait_eng:
                need = {}
                for o in finals:
                    k = id(o.sem)
                    if k not in need or need[k][1] < o.ticket:
                        need[k] = (o.sem, o.ticket)
                for k, (s, t) in need.items():
                    e.wait_ge(s, t)

        with nc.Block() as block:
            @block.sync
            def _(e):
                run_stream("sp", e)

            @block.tensor
            def _(e):
                run_stream("pe", e)

            @block.scalar
            def _(e):
                run_stream("act", e)

            @block.vector
            def _(e):
                run_stream("dve", e)

            @block.gpsimd
            def _(e):
                run_stream("pool", e)


class Arena:
    def __init__(self, nc, nbytes):
        self.t = nc.alloc_sbuf_tensor("arena", [128, nbytes], U8)
        self.n = nbytes
        self.off = 0
        self.peak = 0

    def alloc(self, shape, dtype, parts=128):
        esz = {F32: 4, BF16: 2, I32: 4, U8: 1}[dtype]
        n = 1
        for s in shape:
            n *= s
        nb = n * esz
        self.off = (self.off + 63) // 64 * 64
        assert self.off + nb <= self.n, ("SBUF arena overflow", self.off, nb, self.n)
        v = self.t[0:parts, self.off:self.off + nb]
        if dtype != U8:
            v = v.bitcast(dtype)
        if len(shape) == 2:
            v = v.rearrange("p (a b) -> p a b", a=shape[0])
        elif len(shape) == 3:
            v = v.rearrange("p (a b c) -> p a b c", a=shape[0], b=shape[1])
        elif len(shape) == 4:
            v = v.rearrange("p (a b c d) -> p a b c d", a=shape[0], b=shape[1], c=shape[2])
        self.off += nb
        self.peak = max(self.peak, self.off)
        return v

    def mark(self):
        return self.off

    def release(self, m):
        self.off = m


class Cfg:
    def __init__(self, nb=4, seq=4096, db=128, past=8192):
        self.NB, self.SEQ, self.DB, self.PAST = nb, seq, db, past
        self.NC = 2 * nb
        self.NSB = seq // 512
        self.NSQ = db // self.NC
        assert self.NSQ % 2 == 0 and past == 8192 and seq % 512 == 0
        self.NPAIR = self.NSQ // 2
        self.TP = self.NSB * 256
        self.TS = self.NSQ * 4
        self.T = self.TP + self.TS
        self.NK = self.NSB * 512
        self.NKB = self.NK // 128
        self.NPG = past // 128
        self.NPHYS = db * self.NPG + (db * self.NPG) // 4
        self.NG = 2 if self.NSB >= 2 else 1
        self.SBG = self.NSB // self.NG
        self.SQG = self.NSQ // self.NG


class _Stop(Exception):
    pass


def build(cfg):
    nc = bass.Bass("TRN2", target_bir_lowering=False)
    P = Prog(nc)
    try:
        _build_body(cfg, nc, P)
    except _Stop:
        pass
    return nc, P, None


def _build_body(cfg, nc, P):
    def done(tag):
        if getattr(cfg, 'stop', None) == tag:
            P.emit()
            raise _Stop()
    NSB, NSQ, NPAIR, TP, TS, T, NK, NKB = cfg.NSB, cfg.NSQ, cfg.NPAIR, cfg.TP, cfg.TS, cfg.T, cfg.NK, cfg.NKB

    def din(name, shape, dt=F32):
        return nc.dram_tensor(name, list(shape), dt, kind="ExternalInput").ap()

    def dout(name, shape, dt=F32):
        return nc.dram_tensor(name, list(shape), dt, kind="ExternalOutput").ap()

    xk = din("xk", [NK, D]); xs = din("xs", [TS, D]); xh = din("xh", [NSB * 32, D])
    cs_k = din("cs_k", [NK, 64]); cs_s = din("cs_s", [TS, 64])
    maskp_d = din("maskp", [128, 4, 256]); maskpair_d = din("maskpair", [128, 64]); masknew_d = din("masknew", [TS, NSQ * 32])
    ident_d = din("ident", [128, 128]); ind_d = din("ind", [128, 4, 8])
    cache_lat = din("cache_lat", [cfg.NPHYS, 128 * LAT]); cache_rope = din("cache_rope", [cfg.NPHYS, 128 * ROPE])
    state_d = din("state", [NSQ * 30, DCONV]); ptT_d = din("ptT", [128, NPAIR], I32)
    w_in = din("w_in", [D, DIN]); w_uq = din("w_uq", [QL, NH * DQK]); w_uk = din("w_uk", [LAT, 512]); w_uv = din("w_uv", [LAT, 512])
    w_o = din("w_o_mla", [512, D]); w_co = din("w_conv_out", [512, D]); w_out = din("w_out", [D, D])
    w_gate = din("w_gate", [D, DFF]); w_up = din("w_up", [D, DFF]); w_down = din("w_down", [DFF, D])
    gmixT_d = din("gmixT", [128, 8]); gffnT_d = din("gffnT", [128, 8]); convwT_d = din("convwT", [128, 4, CW])
    convbT_d = din("convbT", [128, 4]); lngT_d = din("lngT", [128, 4]); lnbT_d = din("lnbT", [128, 4])
    gqa_d = din("gqa", [1, QL]); gkv_d = din("gkv", [1, LAT]); gq_d = din("gq", [1, 80]); gk_d = din("gk", [1, 80])

    y_own = dout("y_own", [T, D]); lat_k = dout("lat_k", [NK, LAT]); kpe_k = dout("kpe_k", [NK, ROPE])
    lat_s = dout("lat_s", [TS, LAT]); kpe_s = dout("kpe_s", [TS, ROPE])
    cst_p = dout("cst_p", [32, DCONV]); cst_s = dout("cst_s", [NSQ, 30, DCONV])

    A = Arena(nc, 207 * 1024)
    psb = [nc.alloc_psum_tensor("psb%d" % i, [128, 512], F32) for i in range(8)]
    cnt = {"ev": 0, "q": 0}

    def psf(b):
        return psb[b][:]

    def psh(b):
        return psb[b][:].bitcast(BF16)

    def dmaq():
        cnt["q"] += 1
        return "sp" if cnt["q"] % 2 else "actq"

    def dma(q, out, in_, reads=(), writes=(), semkey=None, final=False):
        return P.op(q, lambda e: e.dma_start(out=out, in_=in_), reads=reads, writes=writes, semkey=semkey, final=final)

    def mm(out, lhsT, rhs, start, stop, reads, writes):
        return P.op("pe", lambda e: e.matmul(out, lhsT=lhsT, rhs=rhs, start=start, stop=stop), reads=reads, writes=writes)

    def tr(out, in_, idn, reads, writes):
        return P.op("pe", lambda e: e.transpose(out, in_, idn), reads=reads, writes=writes)

    def act(out, in_, func, reads, writes, scale=1.0, bias=0.0, accum=None):
        if accum is not None:
            return P.op("act", lambda e: e.activation(out=out, in_=in_, func=func, scale=scale, bias=bias, accum_out=accum), reads=reads, writes=writes)
        return P.op("act", lambda e: e.activation(out=out, in_=in_, func=func, scale=scale, bias=bias), reads=reads, writes=writes)

    def evac(out, in_, reads, writes, eng=None):
        if eng is None:
            cnt["ev"] += 1
            eng = "act" if cnt["ev"] % 2 else "dve"
        if eng == "act":
            return P.op("act", lambda e: e.activation(out=out, in_=in_, func=AF.Copy), reads=reads, writes=writes)
        return P.op(eng, lambda e: e.tensor_copy(out, in_), reads=reads, writes=writes)

    def tt(out, a, b, op, reads, writes, eng="dve"):
        return P.op(eng, lambda e: e.tensor_tensor(out, a, b, op), reads=reads, writes=writes)

    def ts(out, a, s1, s2, op0, op1, reads, writes, eng="dve"):
        if s2 is None:
            return P.op(eng, lambda e: e.tensor_scalar(out, a, s1, None, op0), reads=reads, writes=writes)
        return P.op(eng, lambda e: e.tensor_scalar(out, a, s1, s2, op0, op1), reads=reads, writes=writes)

    def stt(out, a, s, b, op0, op1, reads, writes, eng="dve"):
        return P.op(eng, lambda e: e.scalar_tensor_tensor(out, a, s, b, op0, op1), reads=reads, writes=writes)

    def recip(out, in_, reads, writes):
        return P.op("dve", lambda e: e.reciprocal(out, in_), reads=reads, writes=writes)

    def rsqrt_of(dst, src, scale, r, key_src, key_dst):
        act(dst, src, AF.Sqrt, [key_src, "epsc"], [key_dst], scale=scale, bias=epsc[0:r, 0:1])
        recip(dst, dst, [key_dst], [key_dst])

    ident_f = A.alloc([128], F32); ident_b = A.alloc([128], BF16)
    ind_b = A.alloc([4, 8], BF16); ones_b = A.alloc([128], BF16); ones_f = A.alloc([128], F32)
    epsc = A.alloc([1], F32)
    gmixT = A.alloc([8], F32); gffnT = A.alloc([8], F32)
    gqa_b = A.alloc([QL], F32); gkv_b = A.alloc([LAT], F32)
    gqk96 = A.alloc([DQK], F32)
    OT = A.alloc([4, T], BF16)
    wuk_b = A.alloc([2, 512], BF16); wuv_b = A.alloc([2, 512], BF16)
    stage = A.alloc([4, 8], F32)
    tmp80a = A.alloc([80], F32); tmp80b = A.alloc([80], F32)

    dma("sp", ident_f, ident_d, writes=["identf"])
    dma("actq", stage, ind_d, writes=["stage"])
    dma("sp", gmixT, gmixT_d, writes=["gmixT"]); dma("actq", gffnT, gffnT_d, writes=["gffnT"])
    dma("sp", gqa_b, gqa_d.partition_broadcast(128), writes=["gqa"])
    dma("actq", gkv_b, gkv_d.partition_broadcast(128), writes=["gkv"])
    dma("sp", tmp80a, gq_d.partition_broadcast(128), writes=["t80a"])
    dma("actq", tmp80b, gk_d.partition_broadcast(128), writes=["t80b"])
    dma("poolq", wuk_b, w_uk.rearrange("(c p) n -> p c n", p=128), writes=["wuk"])
    dma("poolq", wuv_b, w_uv.rearrange("(c p) n -> p c n", p=128), writes=["wuv"])
    P.op("dve", lambda e: e.tensor_copy(ident_b, ident_f), reads=["identf"], writes=["ident"])
    P.op("dve", lambda e: e.tensor_copy(ind_b, stage), reads=["stage"], writes=["ind"])
    P.op("pool", lambda e: e.memset(ones_b, 1.0), writes=["onesb"])
    P.op("pool", lambda e: e.memset(ones_f, 1.0), writes=["onesf"])
    P.op("pool", lambda e: e.memset(epsc, EPS), writes=["epsc"])
    stt(tmp80a, tmp80a, 1.0 / math.sqrt(DQK), tmp80b, ALU.mult, ALU.mult, ["t80a", "t80b"], ["t80a"])
    evac(gqk96[:, 0:80], tmp80a, ["t80a"], ["gqk"], eng="dve")
    evac(gqk96[:, 80:96], tmp80a[:, 64:80], ["t80a"], ["gqk"], eng="dve")

    qlatT = A.alloc([2, TS, NH], BF16)
    qpeT = A.alloc([TS, NH], BF16)
    CnewT = A.alloc([2, TS], BF16); kpenewT = A.alloc([TS], BF16); Cnew = A.alloc([LAT], BF16)
    rinvnew = A.alloc([NH], F32)
    wukT = A.alloc([NH, LAT], BF16)
    done('const')
    m_persist = A.mark()

    KT = A.alloc([NH, NK], BF16)
    V = A.alloc([NKB, NH, 65], BF16)
    wkv_b = A.alloc([8, 288], BF16); wcq_b = A.alloc([8, QL], BF16); wuq_b = A.alloc([3, NH * DQK], BF16)
    maskp_b = A.alloc([4, 256], BF16)
    NXS = 2
    xst = [A.alloc([D], F32) for _ in range(NXS)]
    hn = [A.alloc([D], BF16) for _ in range(2)]
    hTb = [A.alloc([8, 128], BF16) for _ in range(2)]
    cst = [A.alloc([64], F32) for _ in range(2)]
    small = [A.alloc([64], F32) for _ in range(2)]
    ckvf = [A.alloc([LAT], F32) for _ in range(2)]
    ckvb = [A.alloc([LAT], BF16) for _ in range(2)]
    kpef = [A.alloc([ROPE], F32) for _ in range(2)]
    kt1A = [A.alloc([ROPE], F32) for _ in range(2)]; kt2A = [A.alloc([ROPE], F32) for _ in range(2)]
    CTb = [A.alloc([2, 128], BF16) for _ in range(2)]
    sqA = [A.alloc([1024], BF16) for _ in range(2)]
    Kn = [A.alloc([NH, DQK], BF16) for _ in range(2)]
    cqnA = [A.alloc([QL], BF16) for _ in range(2)]; cqTA = [A.alloc([3, 128], BF16) for _ in range(2)]
    qfA = [A.alloc([NH, DQK], F32) for _ in range(2)]; qt1A = [A.alloc([NH, ROPE], F32) for _ in range(2)]; qt2A = [A.alloc([NH, ROPE], F32) for _ in range(2)]
    QnA = [A.alloc([NH, DQK], BF16) for _ in range(2)]
    QT = [A.alloc([NH, 256], BF16) for _ in range(2)]
    PT = [A.alloc([2, 256], BF16) for _ in range(3)]
    Otok = A.alloc([2, NH, 64], BF16)
    rden = A.alloc([2], F32)

    dma("poolq", wkv_b, w_in[:, I_KV:I_KV + 288].rearrange("(c p) n -> p c n", p=128), writes=["wkv"])
    dma("poolq", wcq_b, w_in[:, I_CQ:I_CQ + QL].rearrange("(c p) n -> p c n", p=128), writes=["wcq"])
    dma("poolq", wuq_b, w_uq.rearrange("(c p) n -> p c n", p=128), writes=["wuq"])
    maskp_f = xst[0].rearrange("p (a q) -> p a q", a=4)
    dma("sp", maskp_f, maskp_d, writes=["xst0"])
    evac(maskp_b, maskp_f, ["xst0"], ["maskp"], eng="dve")
    P.op("pool", lambda e: e.memset(V[:, :, :, 64:65], 1.0), writes=["Vones"])

    for h in range(NH):
        for c in range(2):
            bnk = 4 + (h * 2 + c) % 4
            tr(psh(bnk)[0:64, 0:128], wuk_b[:, c, h * 64:(h + 1) * 64], ident_b, ["wuk", "ident"], ["ps%d" % bnk])
            evac(wukT[0:64, h, c * 128:(c + 1) * 128], psh(bnk)[0:64, 0:128], ["ps%d" % bnk], ["wukT"])

    ring = {"g": 0}

    def gbank():
        ring["g"] = (ring["g"] + 1) % 4
        return ring["g"]

    blk_ctr = {"n": 0}

    def token_block(xsrc, cssrc, r, mode, kb=None, qslot=None, qcol=None, latdst=None, kpedst=None):
        n = blk_ctr["n"]; blk_ctr["n"] += 1
        s2 = n % 2
        xt = xst[n % NXS]; kx = "xst%d" % (n % NXS)
        sm = small[s2]; ksm = "small%d" % s2
        hnb = hn[s2]; khn = "hn%d" % s2
        hT = hTb[s2]; khT = "hT%d" % s2
        cs = cst[s2]; kcs = "cs%d" % s2
        sq = sqA[s2]; kt1 = kt1A[s2]; kt2 = kt2A[s2]; cqn = cqnA[s2]; cqT = cqTA[s2]
        qf = qfA[s2]; qt1 = qt1A[s2]; qt2 = qt2A[s2]; Qn = QnA[s2]
        S_ = "_%d" % s2
        dma(dmaq(), xt[0:r], xsrc, writes=[kx])
        dma(dmaq(), cs[0:r], cssrc, writes=[kcs])
        act(sq[0:r, 0:D], xt[0:r], AF.Square, [kx], ["sq" + S_, ksm + "a"], accum=sm[0:r, 0:1])
        rsqrt_of(sm[0:r, 1:2], sm[0:r, 0:1], 1.0 / D, r, ksm + "a", ksm + "b")
        ts(hnb[0:r], xt[0:r], sm[0:r, 1:2], None, ALU.mult, None, [kx, ksm + "b"], [khn])
        yield
        b0 = gbank(); k0 = "ps%d" % b0
        pT = psh(b0).rearrange("p (c t) -> p c t", c=8)
        for c in range(8):
            tr(pT[:, c, 0:r], hnb[0:r, c * 128:(c + 1) * 128], ident_b[0:r, 0:r], [khn, "ident"], [k0])
        tt(hT[:, :, 0:r], pT[:, :, 0:r], gmixT.unsqueeze(2).to_broadcast([128, 8, r]), ALU.mult, [k0, "gmixT"], [khT])
        yield
        b1 = gbank(); k1 = "ps%d" % b1
        kvp = psf(b1)
        for c in range(8):
            mm(kvp[0:r, 0:288], hT[:, c, 0:r], wkv_b[:, c, :], c == 0, c == 7, [khT, "wkv"], [k1])
        cf = ckvf[s2]; kcf = "ckvf%d" % s2
        cb = ckvb[s2]; kcb = "ckvb%d" % s2
        kp = kpef[s2]; kkp = "kpef%d" % s2
        act(sq[0:r, 0:LAT], kvp[0:r, 0:LAT], AF.Square, [k1], ["sq" + S_, ksm + "c"], accum=sm[0:r, 2:3])
        rsqrt_of(sm[0:r, 3:4], sm[0:r, 2:3], 1.0 / LAT, r, ksm + "c", ksm + "d")
        stt(cf[0:r], kvp[0:r, 0:LAT], sm[0:r, 3:4], gkv_b[0:r], ALU.mult, ALU.mult, [k1, ksm + "d", "gkv"], [kcf])
        dma(dmaq(), latdst, cf[0:r], reads=[kcf], semkey="o_lat%d" % s2, final=True)
        evac(cb[0:r], cf[0:r], [kcf], [kcb], eng="act")
        tt(kt1[0:r], kvp[0:r, 256:288], cs[0:r, 0:32], ALU.mult, [k1, kcs], ["kt1" + S_])
        tt(kt2[0:r, 0:16], kvp[0:r, 272:288], cs[0:r, 32:48], ALU.mult, [k1, kcs], ["kt2a" + S_])
        tt(kt2[0:r, 16:32], kvp[0:r, 256:272], cs[0:r, 48:64], ALU.mult, [k1, kcs], ["kt2b" + S_])
        tt(kp[0:r], kt1[0:r], kt2[0:r], ALU.add, ["kt1" + S_, "kt2a" + S_, "kt2b" + S_], [kkp])
        dma(dmaq(), kpedst, kp[0:r], reads=[kkp], semkey="o_kpe%d" % s2, final=True)
        yield
        b2 = gbank(); k2 = "ps%d" % b2
        cTp = psh(b2).rearrange("p (c t) -> p c t", c=8)
        for c in range(2):
            tr(cTp[:, c, 0:r], cb[0:r, c * 128:(c + 1) * 128], ident_b[0:r, 0:r], [kcb, "ident"], [k2])
        if mode == "sample":
            CT = CnewT; kCT = "CnewT"
            evac(CT[:, :, 0:r], cTp[:, 0:2, 0:r], [k2], [kCT])
        else:
            CT = CTb[s2]; kCT = "CT%d" % s2
            evac(CT[:, :, 0:r], cTp[:, 0:2, 0:r], [k2], [kCT])
        yield
        b3 = gbank(); k3 = "ps%d" % b3
        knp = psf(b3)
        for c in range(2):
            mm(knp[0:r, :], CT[:, c, 0:r], wuk_b[:, c, :], c == 0, c == 1, [kCT, "wuk"], [k3])
        act(sq[0:r, 0:512], knp[0:r, :], AF.Square, [k3], ["sq" + S_])
        P.op("dve", lambda e: e.tensor_reduce(sm[0:r, 8:16], sq[0:r, 0:512].rearrange("p (h d) -> p h d", h=NH), AX.X, ALU.add),
             reads=["sq" + S_], writes=[ksm + "e"])
        act(kt1[0:r], kp[0:r], AF.Square, [kkp], ["kt1" + S_, ksm + "f"], accum=sm[0:r, 4:5])
        ts(sm[0:r, 8:16], sm[0:r, 8:16], sm[0:r, 4:5], None, ALU.add, None, [ksm + "e", ksm + "f"], [ksm + "e"])
        rinv = rinvnew if mode == "sample" else sm[:, 8:16]
        krinv = "rinvnew" if mode == "sample" else ksm + "g"
        rsqrt_of(rinv[0:r], sm[0:r, 8:16], 1.0 / DQK, r, ksm + "e", krinv)
        if mode == "sample":
            evac(Cnew[0:r], cf[0:r], [kcf], ["Cnew"], eng="dve")
            yield
            b4 = gbank(); k4 = "ps%d" % b4
            tr(psf(b4)[0:32, 0:r], kp[0:r], ident_f[0:r, 0:r], [kkp, "identf"], [k4])
            evac(kpenewT[0:32, 0:r], psf(b4)[0:32, 0:r], [k4], ["kpenewT"])
        else:
            knb = Kn[s2]; kkn = "Kn%d" % s2
            tt(knb[0:r, :, 0:64], knp[0:r, :].rearrange("p (h d) -> p h d", h=NH),
               sm[0:r, 8:16].unsqueeze(2).to_broadcast([r, NH, 64]), ALU.mult, [k3, krinv], [kkn + "a"])
            tt(knb[0:r, :, 64:96], kp[0:r].unsqueeze(1).to_broadcast([r, NH, ROPE]),
               sm[0:r, 8:16].unsqueeze(2).to_broadcast([r, NH, ROPE]), ALU.mult, [kkp, krinv], [kkn + "b"])
            yield
            b4 = gbank(); k4 = "ps%d" % b4
            kTp = psh(b4).rearrange("p (h t) -> p h t", h=NH)
            for h in range(NH):
                tr(kTp[0:DQK, h, 0:r], knb[0:r, h, :], ident_b[0:r, 0:r], [kkn + "a", kkn + "b", "ident"], [k4])
            evac(KT[0:DQK, :, kb * 128:kb * 128 + r], kTp[0:DQK, :, 0:r], [k4], ["KT%d" % kb])
            yield
            b5 = gbank(); k5 = "ps%d" % b5
            vp = psf(b5)
            for c in range(2):
                mm(vp[0:r, :], CT[:, c, 0:r], wuv_b[:, c, :], c == 0, c == 1, [kCT, "wuv"], [k5])
            evac(V[0:r, kb, :, 0:64], vp[0:r, :].rearrange("p (h d) -> p h d", h=NH), [k5, "Vones"], ["V%d" % kb])
        if mode == "other":
            return
        yield
        yield
        b6 = gbank(); k6 = "ps%d" % b6
        cqp = psf(b6)
        for c in range(8):
            mm(cqp[0:r, 0:QL], hT[:, c, 0:r], wcq_b[:, c, :], c == 0, c == 7, [khT, "wcq"], [k6])
        act(sq[0:r, 0:QL], cqp[0:r, 0:QL], AF.Square, [k6], ["sq" + S_, ksm + "h"], accum=sm[0:r, 5:6])
        rsqrt_of(sm[0:r, 6:7], sm[0:r, 5:6], 1.0 / QL, r, ksm + "h", ksm + "i")
        stt(cqn[0:r], cqp[0:r, 0:QL], sm[0:r, 6:7], gqa_b[0:r], ALU.mult, ALU.mult, [k6, ksm + "i", "gqa"], ["cqn" + S_])
        yield
        b7 = gbank(); k7 = "ps%d" % b7
        cqTp = psh(b7).rearrange("p (c t) -> p c t", c=8)
        for c in range(3):
            tr(cqTp[:, c, 0:r], cqn[0:r, c * 128:(c + 1) * 128], ident_b[0:r, 0:r], ["cqn" + S_, "ident"], [k7])
        evac(cqT[:, :, 0:r], cqTp[:, 0:3, 0:r], [k7], ["cqT" + S_])
        qfl = qf.rearrange("p h d -> p (h d)")
        for half in range(2):
            yield
            b8 = gbank(); k8 = "ps%d" % b8
            qp = psf(b8)
            for c in range(3):
                mm(qp[0:r, 0:QL], cqT[:, c, 0:r], wuq_b[:, c, half * QL:(half + 1) * QL], c == 0, c == 2, ["cqT" + S_, "wuq"], [k8])
            evac(qfl[0:r, half * QL:(half + 1) * QL], qp[0:r, 0:QL], [k8], ["qf%d" % half + S_])
        kq = ["qf0" + S_, "qf1" + S_]
        qr = qf[0:r, :, 64:96]
        tt(qt1[0:r], qr, cs[0:r, 0:32].unsqueeze(1).to_broadcast([r, NH, 32]), ALU.mult, kq + [kcs], ["qt1" + S_])
        tt(qt2[0:r, :, 0:16], qf[0:r, :, 80:96], cs[0:r, 32:48].unsqueeze(1).to_broadcast([r, NH, 16]), ALU.mult, kq + [kcs], ["qt2a" + S_])
        tt(qt2[0:r, :, 16:32], qf[0:r, :, 64:80], cs[0:r, 48:64].unsqueeze(1).to_broadcast([r, NH, 16]), ALU.mult, kq + [kcs], ["qt2b" + S_])
        tt(qr, qt1[0:r], qt2[0:r], ALU.add, ["qt1" + S_, "qt2a" + S_, "qt2b" + S_], ["qf0" + S_, "qf1" + S_])
        act(sq[0:r, 0:768], qfl[0:r], AF.Square, kq, ["sq" + S_])
        P.op("dve", lambda e: e.tensor_reduce(sm[0:r, 16:24], sq[0:r, 0:768].rearrange("p (h d) -> p h d", h=NH), AX.X, ALU.add),
             reads=["sq" + S_], writes=[ksm + "j"])
        rsqrt_of(sm[0:r, 16:24], sm[0:r, 16:24], 1.0 / DQK, r, ksm + "j", ksm + "j")
        tt(qf[0:r], qf[0:r], sm[0:r, 16:24].unsqueeze(2).to_broadcast([r, NH, DQK]), ALU.mult, kq + [ksm + "j"], kq)
        tt(Qn[0:r], qf[0:r], gqk96[0:r].unsqueeze(1).to_broadcast([r, NH, DQK]), ALU.mult, kq + ["gqk"], ["Qn" + S_])
        if mode == "own":
            yield
            b9 = gbank(); k9 = "ps%d" % b9
            qTp = psh(b9).rearrange("p (h t) -> p h t", h=NH)
            for h in range(NH):
                tr(qTp[0:DQK, h, 0:r], Qn[0:r, h, :], ident_b[0:r, 0:r], ["Qn" + S_, "ident"], [k9])
            evac(QT[qslot][0:DQK, :, qcol:qcol + r], qTp[0:DQK, :, 0:r], [k9], ["QT%d_%d" % (qslot, qcol)])
        else:
            yield
            b9 = gbank(); k9 = "ps%d" % b9
            qnTp = psh(b9).rearrange("p (h t) -> p h t", h=NH)
            for h in range(NH):
                tr(qnTp[0:64, h, 0:r], Qn[0:r, h, 0:64], ident_b[0:r, 0:r], ["Qn" + S_, "ident"], [k9])
            qnT = Kn[0]
            qnTv = qnT.rearrange("p h d -> p (h d)")[0:64, 0:NH * r].rearrange("p (h t) -> p h t", h=NH)
            evac(qnTv, qnTp[0:64, :, 0:r], [k9], ["Kn0a", "Kn0b"])
            yield
            b10 = gbank(); k10 = "ps%d" % b10
            qpTp = psh(b10).rearrange("p (h t) -> p h t", h=NH)
            for h in range(NH):
                tr(qpTp[0:32, h, 0:r], Qn[0:r, h, 64:96], ident_b[0:r, 0:r], ["Qn" + S_, "ident"], [k10])
            evac(qpeT[0:32, 0:r, :].rearrange("p t h -> p h t"), qpTp[0:32, :, 0:r], [k10], ["qpeT"])
            for c in range(2):
                yield
                b11 = gbank(); k11 = "ps%d" % b11
                ql = psf(b11).rearrange("p (h t) -> p h t", h=NH)
                for h in range(NH):
                    mm(ql[:, h, 0:r], wukT[0:64, h, c * 128:(c + 1) * 128], qnTv[:, h, :], True, True, ["wukT", "Kn0a", "Kn0b"], [k11])
                evac(qlatT[:, c, 0:r, :].rearrange("p t h -> p h t"), ql[:, :, 0:r], [k11], ["qlatT"])

    att_u = {"n": 0}

    def attention_superblock(j, qslot):
        nkb = 4 * j + 4
        qk = ["QT%d_0" % qslot, "QT%d_128" % qslot]
        units = [(h, kp2) for h in range(NH) for kp2 in range(nkb // 2)]
        slots = []
        for idx in range(len(units) + 1):
            if idx < len(units):
                h, kp2 = units[idx]
                sb_ = gbank(); ks = "ps%d" % sb_
                sT = psf(sb_).rearrange("p (a q) -> p a q", a=2)
                pslot = att_u["n"] % 3; att_u["n"] += 1
                slots.append(pslot)
                pt = PT[pslot]; kpt = "PT%d" % pslot
                for a in range(2):
                    kb = kp2 * 2 + a
                    masked = kb >= 4 * j
                    mm(sT[:, a, :], KT[0:DQK, h, kb * 128:(kb + 1) * 128], QT[qslot][0:DQK, h, :], True, not masked,
                       ["KT%d" % kb] + qk, [ks])
                    if masked:
                        mm(sT[:, a, :], ident_b, maskp_b[:, kb - 4 * j, :], False, True, ["ident", "maskp"], [ks])
                act(pt, sT, AF.Exp, [ks], [kpt])
            if idx >= 1:
                h, kp2 = units[idx - 1]
                pslot = slots[idx - 1]
                pt = PT[pslot]; kpt = "PT%d" % pslot
                ob = [4 + 2 * (h % 2), 5 + 2 * (h % 2)]
                okeys = ["ps%d" % ob[0], "ps%d" % ob[1]]
                for a in range(2):
                    kb = kp2 * 2 + a
                    for half in range(2):
                        mm(psf(ob[half])[:, 0:65], pt[:, a, half * 128:(half + 1) * 128], V[:, kb, h, :],
                           kb == 0, kb == nkb - 1, [kpt, "V%d" % kb, "Vones"], [okeys[half]])
                if kp2 == nkb // 2 - 1:
                    for half in range(2):
                        recip(rden[:, half:half + 1], psf(ob[half])[:, 64:65], [okeys[half]], ["rden%d" % half])
                        ts(Otok[:, half, h, :], psf(ob[half])[:, 0:64], rden[:, half:half + 1], None, ALU.mult, None,
                           [okeys[half], "rden%d" % half], ["Otok%d" % half])
            yield
        for half in range(2):
            b = gbank(); k = "ps%d" % b
            oTp = psh(b).rearrange("p (c t) -> p c t", c=8)
            ofl = Otok[:, half].rearrange("p h d -> p (h d)")
            for c in range(4):
                tr(oTp[:, c, 0:128], ofl[:, c * 128:(c + 1) * 128], ident_b, ["Otok%d" % half, "ident"], [k])
            col = j * 256 + half * 128
            evac(OT[:, :, col:col + 128], oTp[:, 0:4, 0:128], [k], ["OT"])
        yield

    def interleave(g1, g2, r1=1, r2=1):
        it1, it2 = iter(g1), iter(g2)
        d1 = d2 = False
        while not (d1 and d2):
            for _ in range(r1):
                if not d1:
                    try:
                        next(it1)
                    except StopIteration:
                        d1 = True
            for _ in range(r2):
                if not d2:
                    try:
                        next(it2)
                    except StopIteration:
                        d2 = True
            yield

    def chain(gens):
        for g_ in gens:
            yield from g_

    def tile_blocks(i):
        gs = []
        for blk in range(4):
            kb = i * 4 + blk
            rows = slice(kb * 128, (kb + 1) * 128)
            gs.append(token_block(xk[rows, :], cs_k[rows, :], 128, "own" if blk < 2 else "other", kb=kb, qslot=i % 2,
                                  qcol=(blk % 2) * 128, latdst=lat_k[rows, :], kpedst=kpe_k[rows, :]))
        return chain([interleave(gs[0], gs[1]), interleave(gs[2], gs[3])])

    def drain(g_):
        for _ in g_:
            pass

    done('p1pre')
    drain(token_block(xs[0:TS, :], cs_s[0:TS, :], TS, "sample", latdst=lat_s[0:TS, :], kpedst=kpe_s[0:TS, :]))
    done('p1s')
    drain(tile_blocks(0))
    for i in range(NSB):
        nA = NH * (2 * i + 2) + 2
        if i + 1 < NSB:
            nB = 50
            drain(interleave(attention_superblock(i, i % 2), tile_blocks(i + 1), max(1, round(nA / nB)), max(1, round(nB / nA))))
        else:
            drain(attention_superblock(i, i % 2))
        done('p1a%d' % i)

    P.barrier()
    A.release(m_persist)
    NGB = 6
    Cg = [A.alloc([8, LAT], BF16) for _ in range(NGB)]
    Rg = [A.alloc([128, ROPE], BF16) for _ in range(2)]
    ptT = A.alloc([NPAIR], I32); ptf = A.alloc([NPAIR], F32)
    rgc_f = A.alloc([16], F32)
    idxl_f = A.alloc([NPAIR, 16], F32); idxl = A.alloc([NPAIR, 16], I32)
    idxr_f = A.alloc([NPAIR, 2], F32); idxr = A.alloc([NPAIR, 2], I32)
    mpair_f = A.alloc([64], F32); mpair_b = A.alloc([64], BF16)
    mnew_f = A.alloc([NSQ * 32], F32); mnew_b = A.alloc([NSQ * 32], BF16)
    CTs = [A.alloc([2, 256], BF16) for _ in range(4)]
    kpTs = [A.alloc([256], BF16) for _ in range(4)]
    kpTq = [A.alloc([256], BF16) for _ in range(4)]
    sqs = [A.alloc([4, 256], BF16) for _ in range(3)]
    rinv_s = [A.alloc([2, NH], F32) for _ in range(3)]
    scs = [A.alloc([2, 64], F32) for _ in range(3)]
    PTs = [A.alloc([2, 64], BF16) for _ in range(3)]
    scn_f = A.alloc([NSQ * 32], F32); PnewT = A.alloc([NSQ * 32], BF16)
    olat_n = A.alloc([LAT], BF16); rden_s = A.alloc([1], F32)
    olatT_all = A.alloc([2, NH, TS], BF16)
    Os = A.alloc([512], BF16)
    NQ = NSQ * 32

    dma("sp", ptT, ptT_d, writes=["ptT"])
    dma("actq", mpair_f, maskpair_d, writes=["mpairf"])
    dma("sp", mnew_f[0:TS], masknew_d, writes=["mnewf"])
    evac(mpair_b, mpair_f, ["mpairf"], ["mpair"], eng="dve")
    evac(mnew_b[0:TS], mnew_f[0:TS], ["mnewf"], ["mnew"], eng="dve")
    for k in range(16):
        P.op("pool", lambda e, k=k: e.memset(rgc_f[:, k:k + 1], float(k)), writes=["rgc%d" % k])
    rgk = ["rgc%d" % k for k in range(16)]
    evac(ptf, ptT, ["ptT"], ["ptf"], eng="dve")
    for pr in range(NPAIR):
        stt(idxl[:, pr, :], ptf[:, pr:pr + 1].to_broadcast([128, 16]), 16.0, rgc_f, ALU.mult, ALU.add, ["ptf"] + rgk, ["idxl"])
        stt(idxr[:, pr, :], ptf[:, pr:pr + 1].to_broadcast([128, 2]), 2.0, rgc_f[:, 0:2], ALU.mult, ALU.add, ["ptf"] + rgk, ["idxr"])
    lat16 = cache_lat.rearrange("n (g e) -> (n g) e", g=16)
    rope2 = cache_rope.rearrange("n (g e) -> (n g) e", g=2)

    def gather(out2d, src, idx_ap, reads, writes):
        return P.op("poolq", lambda e: e.indirect_dma_start(out=out2d, out_offset=None, in_=src,
                                                            in_offset=bass.IndirectOffsetOnAxis(ap=idx_ap, axis=0)),
                    reads=reads, writes=writes)

    scn = psf(2)
    qlat_all = [qlatT[:, c].rearrange("p t h -> p (t h)") for c in range(2)]
    qpe_all = qpeT[0:32].rearrange("p t h -> p (t h)")
    for c in range(2):
        mm(scn[0:TS, 0:NQ], CnewT[:, c, 0:TS], qlat_all[c], c == 0, False, ["CnewT", "qlatT"], ["ps2"])
    mm(scn[0:TS, 0:NQ], kpenewT[0:32, 0:TS], qpe_all, False, False, ["kpenewT", "qpeT"], ["ps2"])
    mm(scn[0:TS, 0:NQ], ident_b[0:TS, 0:TS], mnew_b[0:TS, :], False, True, ["ident", "mnew"], ["ps2"])
    tt(scn_f[0:TS].rearrange("p (t h) -> p t h", h=NH), scn[0:TS, 0:NQ].rearrange("p (t h) -> p t h", h=NH),
       rinvnew[0:TS].unsqueeze(1).to_broadcast([TS, TS, NH]), ALU.mult, ["ps2", "rinvnew"], ["scnf"])
    act(PnewT[0:TS], scn_f[0:TS], AF.Exp, ["scnf"], ["PnewT"])

    done('p2pre')
    Tb = psh(0); kT = "ps0"
    OL = psf(7); DEN = psf(1)
    SS = psf(6)
    NS4, NS3, PF = 4, 3, 3
    its = [(pr, rg, r2) for pr in range(NPAIR) for rg in range(16) for r2 in range(4)]
    NIT = len(its)

    def cg_of(pr, rg):
        gi = (pr * 16 + rg) % NGB
        return Cg[gi], "Cg%d" % gi

    def issue_gather(G):
        if G >= NPAIR * 16:
            return
        pr, rg = G // 16, G % 16
        cg, kcg = cg_of(pr, rg)
        gather(cg.rearrange("p r c -> p (r c)"), lat16, idxl[:, pr, rg:rg + 1], ["idxl"], [kcg])

    def issue_rope(pr):
        if pr >= NPAIR:
            return
        rgt = Rg[pr % 2]
        for hf in range(2):
            gather(rgt[:, hf * 64:(hf + 1) * 64, :].rearrange("p r d -> p (r d)"), rope2, idxr[:, pr, hf:hf + 1], ["idxr"], ["Rg%d" % (pr % 2)])

    def stage_A(n):
        pr, rg, r2 = its[n]
        if r2 == 0:
            if rg == 0:
                issue_rope(pr + 1)
            issue_gather(pr * 16 + rg + PF)
        cg, kcg = cg_of(pr, rg)
        rgt = Rg[pr % 2]; krg = "Rg%d" % (pr % 2)
        s4 = n % NS4
        r0 = r2 * 2
        rglob = rg * 8 + r0
        for a in range(2):
            for c in range(2):
                tr(Tb[:, c * 256 + a * 128:c * 256 + (a + 1) * 128], cg[:, r0 + a, c * 128:(c + 1) * 128], ident_b, [kcg, "ident"], [kT])
        for a in range(2):
            tr(Tb[0:32, 512 + a * 128:512 + (a + 1) * 128], rgt[:, rglob + a, :], ident_b, [krg, "ident"], [kT])
        evac(CTs[s4], Tb[:, 0:512].rearrange("p (c k) -> p c k", c=2), [kT], ["CTs%d" % s4], eng="act")
        evac(kpTs[s4][0:32], Tb[0:32, 512:768], [kT], ["kpTs%d" % s4], eng="act")
        tt(kpTq[s4][0:32], kpTs[s4][0:32], kpTs[s4][0:32], ALU.mult, ["kpTs%d" % s4], ["kpTq%d" % s4])

    def stage_B(n):
        s4, s3, sl = n % NS4, n % NS3, n % 2
        kb0 = 2 + 2 * sl
        kkn = ["ps%d" % kb0, "ps%d" % (kb0 + 1)]
        for hc in range(4):
            dst = psf(kb0 + hc // 2)[:, (hc % 2) * 256:(hc % 2) * 256 + 256]
            for c in range(2):
                mm(dst, wuk_b[:, c, hc * 128:(hc + 1) * 128], CTs[s4][:, c, :], c == 0, c == 1, ["wuk", "CTs%d" % s4], [kkn[hc // 2]])
        for b2 in range(2):
            act(sqs[s3][:, 2 * b2:2 * b2 + 2, :].rearrange("p a k -> p (a k)"), psf(kb0 + b2), AF.Square, [kkn[b2]], ["sqs%d_%d" % (s3, b2)])

    def stage_CD(n):
        pr, rg, r2 = its[n]
        s4, s3, sl = n % NS4, n % NS3, n % 2
        so = sl * 256
        kss = "ps6_%d" % sl
        for a in range(2):
            for hc in range(4):
                mm(SS[:, so + a * 8:so + a * 8 + 8], sqs[s3][:, hc, a * 128:(a + 1) * 128], ind_b[:, hc, :], hc == 0, False,
                   ["sqs%d_%d" % (s3, hc // 2), "ind"], [kss])
            mm(SS[:, so + a * 8:so + a * 8 + 8], kpTq[s4][0:32, a * 128:(a + 1) * 128], ones_b[0:32, 0:8], False, True,
               ["kpTq%d" % s4, "onesb"], [kss])
        rv = rinv_s[s3]
        rsqrt_of(rv.rearrange("p a h -> p (a h)"), SS[:, so:so + 16], 1.0 / DQK, 128, kss, "rinvs%d" % s3)
        qlp = [qlatT[:, c, pr * 8:(pr + 1) * 8, :].rearrange("p t h -> p (t h)") for c in range(2)]
        qpp = qpeT[0:32, pr * 8:(pr + 1) * 8, :].rearrange("p t h -> p (t h)")
        for a in range(2):
            dst = SS[:, so + 64 + a * 64:so + 128 + a * 64]
            for c in range(2):
                mm(dst, CTs[s4][:, c, a * 128:(a + 1) * 128], qlp[c], c == 0, False, ["CTs%d" % s4, "qlatT"], [kss])
            mm(dst, kpTs[s4][0:32, a * 128:(a + 1) * 128], qpp, False, False, ["kpTs%d" % s4, "qpeT"], [kss])
            mm(dst, ident_b, mpair_b, False, True, ["ident", "mpair"], [kss])
        tt(scs[s3].rearrange("p a (t h) -> p a t h", h=NH),
           SS[:, so + 64:so + 192].rearrange("p (a t h) -> p a t h", a=2, h=NH),
           rv.unsqueeze(2).to_broadcast([128, 2, 8, NH]), ALU.mult, [kss, "rinvs%d" % s3], ["scs%d" % s3])
        act(PTs[s3], scs[s3], AF.Exp, ["scs%d" % s3], ["PTs%d" % s3])

    def stage_E(n):
        pr, rg, r2 = its[n]
        s3 = n % NS3
        cg, kcg = cg_of(pr, rg)
        r0 = r2 * 2
        for a in range(2):
            first = (rg == 0 and r2 == 0 and a == 0)
            mm(OL[0:64, 0:LAT], PTs[s3][:, a, :], cg[:, r0 + a, :], first, False, ["PTs%d" % s3, kcg], ["ps7"])
            mm(DEN[0:64, 0:8], PTs[s3][:, a, :], ones_b[:, 0:8], first, False, ["PTs%d" % s3, "onesb"], ["ps1"])
        if rg == 15 and r2 == 3:
            mm(OL[0:64, 0:LAT], PnewT[0:TS, pr * 64:(pr + 1) * 64], Cnew[0:TS, :], False, True, ["PnewT", "Cnew"], ["ps7"])
            mm(DEN[0:64, 0:8], PnewT[0:TS, pr * 64:(pr + 1) * 64], ones_b[0:TS, 0:8], False, True, ["PnewT", "onesb"], ["ps1"])
            recip(rden_s[0:64], DEN[0:64, 0:1], ["ps1"], ["rdens"])
            ts(olat_n[0:64], OL[0:64, 0:LAT], rden_s[0:64, 0:1], None, ALU.mult, None, ["ps7", "rdens"], ["olatn"])
            for c in range(2):
                tr(Tb[:, c * 64:(c + 1) * 64], olat_n[0:64, c * 128:(c + 1) * 128], ident_b[0:64, 0:64], ["olatn", "ident"], [kT])
            for c in range(2):
                evac(olatT_all[:, c, :, pr * 8:(pr + 1) * 8], Tb[:, c * 64:(c + 1) * 64].rearrange("p (t h) -> p h t", h=NH), [kT], ["olatT"], eng="act")

    issue_rope(0)
    for G in range(PF):
        issue_gather(G)
    for n in range(NIT + 3):
        if n < NIT:
            stage_A(n)
        if 0 <= n - 1 < NIT:
            stage_B(n - 1)
        if 0 <= n - 2 < NIT:
            stage_CD(n - 2)
        if 0 <= n - 3 < NIT:
            stage_E(n - 3)

    OSP = psf(2)
    for h in range(NH):
        for c in range(2):
            mm(OSP[0:TS, h * 64:(h + 1) * 64], olatT_all[:, c, h, :], wuv_b[:, c, h * 64:(h + 1) * 64], c == 0, c == 1, ["olatT", "wuv"], ["ps2"])
    evac(Os[0:TS], OSP[0:TS, :], ["ps2"], ["Os"])
    for c in range(4):
        tr(Tb[:, c * 64:c * 64 + TS], Os[0:TS, c * 128:(c + 1) * 128], ident_b[0:TS, 0:TS], ["Os", "ident"], [kT])
    for c in range(4):
        evac(OT[:, c, TP:TP + TS], Tb[:, c * 64:c * 64 + TS], [kT], ["OT"])

    done('p2')
    P.barrier()
    A.release(m_persist)
    NG, SBG, SQG = cfg.NG, cfg.SBG, cfg.SQG
    TGp, TGs, TH = SBG * 256, SQG * 4, SBG * 32
    TG = TGp + TGs
    NBLK = SBG * 2 + 1
    x1 = A.alloc([NBLK, D], F32)
    hTg = A.alloc([8, TG + TH], BF16)
    YT = A.alloc([4, TG], BF16)
    lngT = A.alloc([4], F32); lnbT = A.alloc([4], F32); convwT = A.alloc([4, CW], F32); convbT = A.alloc([4], F32)
    sm3 = A.alloc([4 * NBLK + 8], F32)
    hn3 = [A.alloc([D], BF16) for _ in range(2)]
    m_p3 = A.mark()
    dma("sp", lngT, lngT_d, writes=["lngT"]); dma("actq", lnbT, lnbT_d, writes=["lnbT"])
    dma("sp", convwT, convwT_d, writes=["convwT"]); dma("actq", convbT, convbT_d, writes=["convbT"])
    ring8 = {"g": 0}

    def gb8():
        ring8["g"] = (ring8["g"] + 1) % 8
        return ring8["g"]

    def ntiles(total, step):
        return [(n0, min(step, total - n0)) for n0 in range(0, total, step)]

    def to_feature_major(src_rows, r, col, gT, kgT, kx, slot, dstT, kdst):
        hb = hn3[slot % 2]; kh = "hn3_%d" % (slot % 2)
        c0 = 4 * NBLK + (slot % 2) * 4
        act(hb[0:r], src_rows, AF.Square, [kx], [kh, "sm3_%d" % (slot % 2)], accum=sm3[0:r, c0:c0 + 1])
        rsqrt_of(sm3[0:r, c0 + 1:c0 + 2], sm3[0:r, c0:c0 + 1], 1.0 / D, r, "sm3_%d" % (slot % 2), "sm3b_%d" % (slot % 2))
        ts(hb[0:r], src_rows, sm3[0:r, c0 + 1:c0 + 2], None, ALU.mult, None, [kx, "sm3b_%d" % (slot % 2)], [kh])
        b = gb8(); k = "ps%d" % b
        pT = psh(b).rearrange("p (c t) -> p c t", c=8)
        for c in range(8):
            tr(pT[:, c, 0:r], hb[0:r, c * 128:(c + 1) * 128], ident_b[0:r, 0:r], [kh, "ident"], [k])
        tt(dstT[:, :, col:col + r], pT[:, :, 0:r], gT.unsqueeze(2).to_broadcast([128, 8, r]), ALU.mult, [k, kgT], [kdst])

    for g in range(NG):
        sb0, sq0 = g * SBG, g * SQG
        blocks = []
        for s_ in range(SBG):
            for b2 in range(2):
                rows = slice((sb0 + s_) * 512 + b2 * 128, (sb0 + s_) * 512 + b2 * 128 + 128)
                yrows = slice((sb0 + s_) * 256 + b2 * 128, (sb0 + s_) * 256 + b2 * 128 + 128)
                blocks.append((xk[rows, :], y_own[yrows, :], 128, s_ * 256 + b2 * 128))
        blocks.append((xs[sq0 * 4:sq0 * 4 + TGs, :], y_own[TP + sq0 * 4:TP + sq0 * 4 + TGs, :], TGs, TGp))
        A.release(m_p3)
        ubuf = A.alloc([4, SBG, 288], BF16); ubuf_s = A.alloc([4, SQG, 36], BF16)
        wglu = A.alloc([8, 1024], BF16)
        xhs = A.alloc([D], F32)
        ycv = A.alloc([4, TG], F32); ysq = A.alloc([4, 512], F32)
        mu = A.alloc([512], F32); var = A.alloc([512], F32); rs = A.alloc([512], F32)
        sg = [A.alloc([256], F32) for _ in range(2)]
        sts = A.alloc([DCONV], F32); usn = A.alloc([4, 32], BF16); cstp = A.alloc([DCONV], F32)
        dma("poolq", wglu, w_in[:, 0:1024].rearrange("(c p) n -> p c n", p=128), writes=["wglu"])
        for bi, (xsrc, ydst, r, col) in enumerate(blocks):
            dma(dmaq(), x1[0:r, bi, :], xsrc, writes=["x1_%d" % bi])
            to_feature_major(x1[0:r, bi, :], r, col, gmixT, "gmixT", "x1_%d" % bi, bi, hTg, "hTg")
        for h0, hr in ntiles(TH, 128):
            dma(dmaq(), xhs[0:hr], xh[sb0 * 32 + h0:sb0 * 32 + h0 + hr, :], writes=["xhs"])
            to_feature_major(xhs[0:hr], hr, TG + h0, gmixT, "gmixT", "xhs", h0 // 128, hTg, "hTg")
        P.op("pool", lambda e: e.memset(ubuf_s, 0.0), writes=["ubuf_s"])
        gl_tiles = [(s_ * 256, 256, ("sb", s_)) for s_ in range(SBG)] + [(TGp, TGs, ("smp", 0)), (TG, TH, ("halo", 0))]
        gi_ = 0
        for c in range(4):
            for (n0, n, kind) in gl_tiles:
                ba = gb8(); bg = gb8()
                for k in range(8):
                    mm(psf(ba)[:, 0:n], wglu[:, k, c * 128:(c + 1) * 128], hTg[:, k, n0:n0 + n], k == 0, k == 7, ["wglu", "hTg"], ["ps%d" % ba])
                for k in range(8):
                    mm(psf(bg)[:, 0:n], wglu[:, k, 512 + c * 128:512 + (c + 1) * 128], hTg[:, k, n0:n0 + n], k == 0, k == 7, ["wglu", "hTg"], ["ps%d" % bg])
                sgi = sg[gi_ % 2]; ksg = "sg%d" % (gi_ % 2); gi_ += 1
                act(sgi[:, 0:n], psf(bg)[:, 0:n], AF.Sigmoid, ["ps%d" % bg], [ksg])
                if kind[0] == "sb":
                    dst = ubuf[:, c, kind[1], 32:288]; a_ = psf(ba)[:, 0:n]; s_v = sgi[:, 0:n]
                elif kind[0] == "smp":
                    dst = ubuf_s[:, c, :, 30:34]
                    a_ = psf(ba)[:, 0:n].rearrange("p (s t) -> p s t", t=4); s_v = sgi[:, 0:n].rearrange("p (s t) -> p s t", t=4)
                else:
                    dst = ubuf[:, c, :, 0:32]
                    a_ = psf(ba)[:, 0:n].rearrange("p (s t) -> p s t", t=32); s_v = sgi[:, 0:n].rearrange("p (s t) -> p s t", t=32)
                tt(dst, a_, s_v, ALU.mult, ["ps%d" % ba, ksg], ["ubuf%d" % c if kind[0] != "smp" else "ubuf_s"])
        for s0_, ns in ntiles(SQG, 4):
            rws = ns * 30
            dma(dmaq(), sts[0:rws], state_d[(sq0 + s0_) * 30:(sq0 + s0_) * 30 + rws, :], writes=["sts"])
            for c in range(4):
                b = gb8()
                tr(psf(b)[:, 0:rws], sts[0:rws, c * 128:(c + 1) * 128], ident_f[0:rws, 0:rws], ["sts", "identf"], ["ps%d" % b])
                evac(ubuf_s[:, c, s0_:s0_ + ns, 0:30], psf(b)[:, 0:rws].rearrange("p (s t) -> p s t", t=30), ["ps%d" % b], ["ubuf_s"])
        dma("sp", cst_s[sq0:sq0 + SQG, 0:26, :], state_d.rearrange("(s t) c -> s t c", t=30)[sq0:sq0 + SQG, 4:30, :],
            semkey="o_cs", final=True)
        for c in range(4):
            yp = ycv[:, c, 0:TGp].rearrange("p (s t) -> p s t", t=256)
            ys_ = ycv[:, c, TGp:TG].rearrange("p (s t) -> p s t", t=4)
            ts(yp, ubuf[:, c, :, 2:258], convwT[:, c, 0:1], convbT[:, c:c + 1], ALU.mult, ALU.add, ["ubuf%d" % c, "convwT", "convbT"], ["ycv%d" % c])
            ts(ys_, ubuf_s[:, c, :, 0:4], convwT[:, c, 0:1], convbT[:, c:c + 1], ALU.mult, ALU.add, ["ubuf_s", "convwT", "convbT"], ["ycvs%d" % c])
            for j in range(1, CW):
                stt(yp, ubuf[:, c, :, 2 + j:258 + j], convwT[:, c, j:j + 1], yp, ALU.mult, ALU.add, ["ubuf%d" % c, "convwT"], ["ycv%d" % c])
                stt(ys_, ubuf_s[:, c, :, j:j + 4], convwT[:, c, j:j + 1], ys_, ALU.mult, ALU.add, ["ubuf_s", "convwT"], ["ycvs%d" % c])
        ykeys = ["ycv%d" % c for c in range(4)] + ["ycvs%d" % c for c in range(4)]
        for (n0, n) in ntiles(TG, 512):
            act(ysq[:, :, 0:n], ycv[:, :, n0:n0 + n], AF.Square, ykeys, ["ysq"])
            b1 = gb8(); b2 = gb8()
            for c in range(4):
                mm(psf(b1)[:, 0:n], ones_f, ycv[:, c, n0:n0 + n], c == 0, c == 3, ["onesf"] + ykeys, ["ps%d" % b1])
            for c in range(4):
                mm(psf(b2)[:, 0:n], ones_f, ysq[:, c, 0:n], c == 0, c == 3, ["onesf", "ysq"], ["ps%d" % b2])
            P.op("act", lambda e, b1=b1, n=n: e.mul(mu[:, 0:n], psf(b1)[:, 0:n], 1.0 / DCONV), reads=["ps%d" % b1], writes=["mu"])
            tt(var[:, 0:n], mu[:, 0:n], mu[:, 0:n], ALU.mult, ["mu"], ["var"])
            stt(var[:, 0:n], psf(b2)[:, 0:n], 1.0 / DCONV, var[:, 0:n], ALU.mult, ALU.subtract, ["ps%d" % b2, "var"], ["var"])
            rsqrt_of(rs[:, 0:n], var[:, 0:n], 1.0, 128, "var", "rs")
            for c in range(4):
                tt(ysq[:, c, 0:n], ycv[:, c, n0:n0 + n], mu[:, 0:n], ALU.subtract, ykeys + ["mu", "ysq"], ["ysq"])
                tt(ysq[:, c, 0:n], ysq[:, c, 0:n], rs[:, 0:n], ALU.mult, ["ysq", "rs"], ["ysq"])
                act(YT[:, c, n0:n0 + n], ysq[:, c, 0:n], AF.Silu, ["ysq", "lngT", "lnbT"], ["YT"], scale=lngT[:, c:c + 1], bias=lnbT[:, c:c + 1])
        if g == NG - 1:
            for c in range(4):
                b = gb8()
                tr(psh(b)[0:32, 0:128], ubuf[:, c, SBG - 1, 256:288], ident_b, ["ubuf%d" % c, "ident"], ["ps%d" % b])
                evac(cstp[0:32, c * 128:(c + 1) * 128], psh(b)[0:32, 0:128], ["ps%d" % b], ["cstp"])
            dma("sp", cst_p, cstp[0:32], reads=["cstp"], semkey="o_cp", final=True)
        for c in range(4):
            evac(usn[:, c, 0:TGs].rearrange("p (s t) -> p s t", t=4), ubuf_s[:, c, :, 30:34], ["ubuf_s"], ["usn"], eng="dve")
        for c in range(4):
            b = gb8()
            tr(psh(b)[0:TGs, 0:128], usn[:, c, 0:TGs], ident_b, ["usn", "ident"], ["ps%d" % b])
            evac(sts[0:TGs, c * 128:(c + 1) * 128], psh(b)[0:TGs, 0:128], ["ps%d" % b], ["sts"])
        for s_ in range(SQG):
            dma(dmaq(), cst_s[sq0 + s_, 26:30, :], sts[s_ * 4:(s_ + 1) * 4, :], reads=["sts"], semkey="o_cs2", final=True)
        done('p3a%d' % g)
        P.barrier()
        A.release(m_p3)
        mergedT = A.alloc([8, TG], BF16)
        wgc = A.alloc([8, 1024], BF16); wgm = A.alloc([8, 1024], BF16)
        wco_b = A.alloc([4, 1024], BF16); wo_b = A.alloc([4, 1024], BF16); wout_b = A.alloc([8, 1024], BF16)
        sgc = [A.alloc([256], F32) for _ in range(2)]; sgm = [A.alloc([256], F32) for _ in range(2)]
        t1 = [A.alloc([256], F32) for _ in range(2)]; t2 = [A.alloc([256], F32) for _ in range(2)]
        dma("poolq", wgc, w_in[:, I_GC:I_GC + 1024].rearrange("(c p) n -> p c n", p=128), writes=["wgc"])
        dma("poolq", wco_b, w_co.rearrange("(c p) n -> p c n", p=128), writes=["wco"])
        dma("poolq", wgm, w_in[:, I_GM:I_GM + 1024].rearrange("(c p) n -> p c n", p=128), writes=["wgm"])
        dma("poolq", wo_b, w_o.rearrange("(c p) n -> p c n", p=128), writes=["wo"])
        dma("poolq", wout_b, w_out.rearrange("(c p) n -> p c n", p=128), writes=["wout"])
        mt = [(s_ * 256, 256, (sb0 + s_) * 256) for s_ in range(SBG)] + [(TGp, TGs, TP + sq0 * 4)]
        mi = 0
        for m in range(8):
            for (n0, n, ocol) in mt:
                s2 = mi % 2; mi += 1
                bA = gb8(); bB = gb8()
                kA, kB = "ps%d" % bA, "ps%d" % bB
                for k in range(8):
                    mm(psf(bA)[:, 0:n], wgc[:, k, m * 128:(m + 1) * 128], hTg[:, k, n0:n0 + n], k == 0, k == 7, ["wgc", "hTg"], [kA])
                for k in range(4):
                    mm(psf(bA)[:, 256:256 + n], wco_b[:, k, m * 128:(m + 1) * 128], YT[:, k, n0:n0 + n], k == 0, k == 3, ["wco", "YT"], [kA])
                for k in range(8):
                    mm(psf(bB)[:, 0:n], wgm[:, k, m * 128:(m + 1) * 128], hTg[:, k, n0:n0 + n], k == 0, k == 7, ["wgm", "hTg"], [kB])
                for k in range(4):
                    mm(psf(bB)[:, 256:256 + n], wo_b[:, k, m * 128:(m + 1) * 128], OT[:, k, ocol:ocol + n], k == 0, k == 3, ["wo", "OT"], [kB])
                act(sgc[s2][:, 0:n], psf(bA)[:, 0:n], AF.Sigmoid, [kA], ["sgc%d" % s2])
                tt(t1[s2][:, 0:n], psf(bA)[:, 256:256 + n], sgc[s2][:, 0:n], ALU.mult, [kA, "sgc%d" % s2], ["t1_%d" % s2])
                act(sgm[s2][:, 0:n], psf(bB)[:, 0:n], AF.Sigmoid, [kB], ["sgm%d" % s2])
                tt(t2[s2][:, 0:n], psf(bB)[:, 256:256 + n], sgm[s2][:, 0:n], ALU.mult, [kB, "sgm%d" % s2], ["t2_%d" % s2])
                tt(mergedT[:, m, n0:n0 + n], t1[s2][:, 0:n], t2[s2][:, 0:n], ALU.add, ["t1_%d" % s2, "t2_%d" % s2], ["mergedT"])
        for bi, (xsrc, ydst, r, col) in enumerate(blocks):
            for half in range(2):
                b = gb8(); k_ = "ps%d" % b
                for k in range(8):
                    mm(psf(b)[0:r, :], mergedT[:, k, col:col + r], wout_b[:, k, half * 512:(half + 1) * 512], k == 0, k == 7, ["mergedT", "wout"], [k_])
                tt(x1[0:r, bi, half * 512:(half + 1) * 512], psf(b)[0:r, :], x1[0:r, bi, half * 512:(half + 1) * 512], ALU.add,
                   [k_, "x1_%d" % bi], ["x1_%d" % bi])
        done('p3b%d' % g)
        P.barrier()
        A.release(m_p3)
        h2T = hTg
        actT = [A.alloc([4, TG], BF16) for _ in range(2)]
        wg_b = [A.alloc([8, 512], BF16) for _ in range(2)]; wu_b = [A.alloc([8, 512], BF16) for _ in range(2)]
        wd_b = [A.alloc([4, 1024], BF16) for _ in range(2)]
        sgt = [A.alloc([512], F32) for _ in range(2)]
        for bi, (xsrc, ydst, r, col) in enumerate(blocks):
            to_feature_major(x1[0:r, bi, :], r, col, gffnT, "gffnT", "x1_%d" % bi, bi, h2T, "h2T")
        fgroups = ntiles(DFF // 128, 4)
        si = 0
        for fg, (f0, nf) in enumerate(fgroups):
            s2 = fg % 2
            dma("poolq", wg_b[s2][:, :, 0:nf * 128], w_gate[:, f0 * 128:(f0 + nf) * 128].rearrange("(c p) n -> p c n", p=128), writes=["wg%d" % s2])
            dma("poolq", wu_b[s2][:, :, 0:nf * 128], w_up[:, f0 * 128:(f0 + nf) * 128].rearrange("(c p) n -> p c n", p=128), writes=["wu%d" % s2])
            dma("poolq", wd_b[s2][:, 0:nf, :], w_down[f0 * 128:(f0 + nf) * 128, :].rearrange("(c p) n -> p c n", p=128), writes=["wd%d" % s2])
            for fi in range(nf):
                for (n0, n) in ntiles(TG, 512):
                    bG = gb8(); bU = gb8()
                    for k in range(8):
                        mm(psf(bG)[:, 0:n], wg_b[s2][:, k, fi * 128:(fi + 1) * 128], h2T[:, k, n0:n0 + n], k == 0, k == 7, ["wg%d" % s2, "h2T"], ["ps%d" % bG])
                    for k in range(8):
                        mm(psf(bU)[:, 0:n], wu_b[s2][:, k, fi * 128:(fi + 1) * 128], h2T[:, k, n0:n0 + n], k == 0, k == 7, ["wu%d" % s2, "h2T"], ["ps%d" % bU])
                    st_ = sgt[si % 2]; kst = "sgt%d" % (si % 2); si += 1
                    act(st_[:, 0:n], psf(bG)[:, 0:n], AF.Silu, ["ps%d" % bG], [kst])
                    tt(actT[s2][:, fi, n0:n0 + n], psf(bU)[:, 0:n], st_[:, 0:n], ALU.mult, ["ps%d" % bU, kst], ["actT%d" % s2])
            for bi, (xsrc, ydst, r, col) in enumerate(blocks):
                for half in range(2):
                    b = gb8(); k_ = "ps%d" % b
                    for fi in range(nf):
                        mm(psf(b)[0:r, :], actT[s2][:, fi, col:col + r], wd_b[s2][:, fi, half * 512:(half + 1) * 512], fi == 0, fi == nf - 1,
                           ["actT%d" % s2, "wd%d" % s2], [k_])
                    tt(x1[0:r, bi, half * 512:(half + 1) * 512], psf(b)[0:r, :], x1[0:r, bi, half * 512:(half + 1) * 512], ALU.add,
                       [k_, "x1_%d" % bi], ["x1_%d" % bi])
        for bi, (xsrc, ydst, r, col) in enumerate(blocks):
            dma(dmaq(), ydst, x1[0:r, bi, :], reads=["x1_%d" % bi], semkey="o_y%d" % (bi % 4), final=True)
        P.barrier()

    P.emit()
    cfg.arena_peak = A.peak


def _host_inputs(cfg, inp):
    f32 = np.float32
    NSB, NSQ, TS = cfg.NSB, cfg.NSQ, cfg.TS
    inv_freq = (1.0 / (10000.0 ** (np.arange(0, ROPE, 2, dtype=f32) / f32(ROPE)))).astype(f32)

    def cs_table(pos):
        ang = pos.astype(f32)[:, None] * inv_freq[None, :]
        c, s = np.cos(ang).astype(f32), np.sin(ang).astype(f32)
        return np.ascontiguousarray(np.concatenate([c, c, -s, s], axis=1))

    ident = np.eye(128, dtype=f32)
    ind = np.zeros((128, 4, 8), f32)
    for c in range(4):
        for p in range(128):
            ind[p, c, (c * 128 + p) // 64] = 1.0
    maskpair = np.full((128, 64), NEG, f32)
    for s2 in range(2):
        maskpair[s2 * 64:(s2 + 1) * 64, s2 * 32:(s2 + 1) * 32] = 0.0
    masknew = np.full((TS, NSQ * 32), NEG, f32)
    for s in range(NSQ):
        for t in range(4):
            for q in range(t, 4):
                masknew[s * 4 + t, s * 32 + q * 8:s * 32 + q * 8 + 8] = 0.0
    cs_s = cs_table(np.tile(cfg.PAST + np.arange(4), NSQ))
    w = {k: np.ascontiguousarray(inp[k][0]) for k in ("w_in", "w_uq", "w_o_mla", "w_conv_out", "w_out", "w_gate", "w_up", "w_down")}
    w["w_uk"] = np.ascontiguousarray(inp["w_uk"][0].reshape(LAT, 512))
    w["w_uv"] = np.ascontiguousarray(inp["w_uv"][0].reshape(LAT, 512))
    shared = dict(w)
    shared.update(
        ident=ident, ind=ind, maskpair=maskpair, masknew=masknew, cs_s=cs_s,
        cache_lat=inp["cache_kv_latent"][0].reshape(cfg.NPHYS, 128 * LAT),
        cache_rope=inp["cache_k_rope"][0].reshape(cfg.NPHYS, 128 * ROPE),
        gmixT=np.ascontiguousarray(inp["norm_mix_g"][0].reshape(8, 128).T),
        gffnT=np.ascontiguousarray(inp["norm_ffn_g"][0].reshape(8, 128).T),
        convwT=np.ascontiguousarray(inp["conv_w"][0].reshape(CW, 4, 128).transpose(2, 1, 0)),
        convbT=np.ascontiguousarray(inp["conv_b"][0].reshape(4, 128).T),
        lngT=np.ascontiguousarray(inp["conv_ln_g"][0].reshape(4, 128).T),
        lnbT=np.ascontiguousarray(inp["conv_ln_b"][0].reshape(4, 128).T),
        gqa=inp["q_a_norm_g"].reshape(1, QL), gkv=inp["kv_a_norm_g"].reshape(1, LAT),
        gq=inp["q_norm_g"].reshape(1, 80), gk=inp["k_norm_g"].reshape(1, 80),
    )
    maps, meta = [], []
    for core in range(cfg.NC):
        b, p = core // 2, core % 2
        pos_k, pos_own = [], []
        for i in range(NSB):
            own = (2 * i + p) * 256 + np.arange(256)
            oth = (2 * i + 1 - p) * 256 + np.arange(256)
            pos_k += [own, oth]
            pos_own.append(own)
        pos_k = np.concatenate(pos_k); pos_own = np.concatenate(pos_own)
        xp = inp["x_prompt"][b]
        xh = np.zeros((NSB * 32, D), f32)
        for i in range(NSB):
            st = (2 * i + p) * 256
            if st > 0:
                xh[i * 32 + 2:(i + 1) * 32] = xp[st - 30:st]
        maskp = np.full((128, 4, 256), NEG, f32)
        kk = np.arange(128)[:, None]; qq = np.arange(256)[None, :]
        for m in range(2):
            maskp[:, m, :] = np.where(m * 128 + kk <= qq, 0.0, NEG)
        if p == 1:
            maskp[:, 2:4, :] = 0.0
        sq0 = core * NSQ
        pt = inp["page_table"][sq0:sq0 + NSQ]
        ptT = np.ascontiguousarray(pt.reshape(cfg.NPAIR, 128).T.astype(np.int32))
        m = dict(shared)
        m.update(
            xk=np.ascontiguousarray(xp[pos_k]), xs=np.ascontiguousarray(inp["x_sample"][sq0:sq0 + NSQ].reshape(TS, D)), xh=xh,
            cs_k=cs_table(pos_k), maskp=maskp,
            state=np.ascontiguousarray(inp["state_conv"][0, sq0:sq0 + NSQ].reshape(NSQ * 30, DCONV)), ptT=ptT,
        )
        maps.append(m)
        meta.append((b, p, pos_own, sq0))
    return maps, meta


_CACHE = {}


def run(cfg, inputs):
    key = (cfg.NB, cfg.SEQ, cfg.DB)
    if key not in _CACHE:
        _CACHE[key] = build(cfg)[0]
    nc = _CACHE[key]
    inp = {k: np.asarray(v) for k, v in inputs.items()}
    maps, meta = _host_inputs(cfg, inp)
    res = run_bass_kernel_spmd(nc, maps, core_ids=list(range(cfg.NC)))
    f32 = np.float32
    NB, SEQ, DB, NSQ = cfg.NB, cfg.SEQ, cfg.DB, cfg.NSQ
    y_p = np.zeros((NB, SEQ, D), f32); y_s = np.zeros((DB, 4, D), f32)
    lat_p = np.zeros((1, NB, SEQ, LAT), f32); kpe_p = np.zeros((1, NB, SEQ, ROPE), f32)
    cs_p = np.zeros((1, NB, 30, DCONV), f32)
    lat_s = np.zeros((1, DB, 4, LAT), f32); kpe_s = np.zeros((1, DB, 4, ROPE), f32); cs_s = np.zeros((1, DB, 30, DCONV), f32)
    for core, (b, p, pos_own, sq0) in enumerate(meta):
        r = res.results[core]
        y_p[b, pos_own] = r["y_own"][:cfg.TP]
        y_s[sq0:sq0 + NSQ] = r["y_own"][cfg.TP:].reshape(NSQ, 4, D)
        own_rows = np.concatenate([i * 512 + np.arange(256) for i in range(cfg.NSB)])
        lat_p[0, b, pos_own] = r["lat_k"][own_rows]
        kpe_p[0, b, pos_own] = r["kpe_k"][own_rows]
        if p == 1:
            cs_p[0, b] = r["cst_p"][2:32]
        lat_s[0, sq0:sq0 + NSQ] = r["lat_s"].reshape(NSQ, 4, LAT)
        kpe_s[0, sq0:sq0 + NSQ] = r["kpe_s"].reshape(NSQ, 4, ROPE)
        cs_s[0, sq0:sq0 + NSQ] = r["cst_s"]
    return (y_p, y_s, lat_p, kpe_p, cs_p, lat_s, kpe_s, cs_s)


def kernel(**inputs):
    return run(Cfg(), inputs)
```

```python
import math
import numpy as np
import concourse.bass as bass
import concourse.mybir as mybir
from concourse.bass_utils import run_bass_kernel_spmd

F32 = mybir.dt.float32
BF16 = mybir.dt.bfloat16
I32 = mybir.dt.int32
U8 = mybir.dt.uint8
AF = mybir.ActivationFunctionType
ALU = mybir.AluOpType
AX = mybir.AxisListType

D = 1024
NH = 8
DQK = 96
LAT = 256
ROPE = 32
DCONV = 512
CW = 31
DFF = 2816
DIN = 3744
QL = 384
EPS = 1e-6
NEG = -30000.0
I_CQ = 1024
I_KV = 1408
I_GC = 1696
I_GM = 2720
DMAQ = ("sp", "actq", "poolq")


class _Op:
    __slots__ = ("eng", "fn", "reads", "writes", "is_dma", "semkey", "waits", "ticket", "sem", "idx",
                 "needs_sig", "final")


def _phys(eng):
    return {"pe": "pe", "act": "act", "dve": "dve", "pool": "pool", "sp": "sp", "actq": "act",
            "poolq": "pool"}[eng]


class Prog:
    def __init__(self, nc):
        self.nc = nc
        self.ops = []
        self.last_writer = {}
        self.readers = {}
        self.bar = None
        self.bar_seen = set()
        self.last_on = {}
        self.dma_since = []

    def barrier(self):
        deps = set(self.last_on.values()) | set(self.dma_since)
        if self.bar is not None:
            deps |= self.bar
        self.bar = deps
        self.bar_seen = set()
        self.dma_since = []

    def op(self, eng, fn, reads=(), writes=(), semkey=None, final=False):
        o = _Op()
        o.eng, o.fn = eng, fn
        o.reads, o.writes = tuple(reads), tuple(writes)
        o.is_dma = eng in DMAQ
        o.semkey = semkey
        o.final = final
        o.idx = len(self.ops)
        o.needs_sig = False
        deps = set()
        for r in o.reads:
            w = self.last_writer.get(r)
            if w is not None:
                deps.add(w)
        for w_ in o.writes:
            w = self.last_writer.get(w_)
            if w is not None:
                deps.add(w)
            deps.update(self.readers.get(w_, ()))
        ph = _phys(eng)
        if self.bar is not None and ph not in self.bar_seen:
            deps |= self.bar
            self.bar_seen.add(ph)
        o.waits = deps
        for r in o.reads:
            self.readers.setdefault(r, []).append(o.idx)
        for w_ in o.writes:
            self.last_writer[w_] = o.idx
            self.readers[w_] = []
        self.ops.append(o)
        self.last_on[ph] = o.idx
        if o.is_dma:
            self.dma_since.append(o.idx)
        return o

    def emit(self, final_wait_eng="sp"):
        nc, ops = self.nc, self.ops
        streams = {"pe": [], "act": [], "dve": [], "pool": [], "sp": []}
        for o in ops:
            streams[_phys(o.eng)].append(o)
        for o in ops:
            keep = set()
            ph = _phys(o.eng)
            for d in o.waits:
                p = ops[d]
                if _phys(p.eng) == ph and not p.is_dma:
                    if ph == "pe":
                        continue
                    if not (set(p.writes) & set(o.reads)):
                        continue
                keep.add(d)
            o.waits = keep
            for d in keep:
                ops[d].needs_sig = True
        semh, semc = {}, {}

        def getsem(key):
            if key not in semh:
                semh[key] = nc.alloc_semaphore("s%d" % len(semh))
                semc[key] = 0
            return semh[key]

        finals = []
        for o in ops:
            if o.is_dma:
                key = ("dma", o.semkey if o.semkey is not None else (o.writes[0] if o.writes else o.reads[0]))
                o.sem = getsem(key)
                semc[key] += 16
                o.ticket = semc[key]
                o.needs_sig = True
                if o.final:
                    finals.append(o)
            elif o.needs_sig:
                key = ("eng", _phys(o.eng))
                o.sem = getsem(key)
                semc[key] += 1
                o.ticket = semc[key]
        self.n_sems = len(semh)

        def run_stream(name, e):
            waited = {}
            for o in streams[name]:
                need = {}
                for d in o.waits:
                    p = ops[d]
                    k = id(p.sem)
                    if k not in need or need[k][1] < p.ticket:
                        need[k] = (p.sem, p.ticket)
                for k, (s, t) in need.items():
                    if waited.get(k, 0) >= t:
                        continue
                    e.wait_ge(s, t)
                    waited[k] = t
                ins = o.fn(e)
                if o.needs_sig:
                    ins.then_inc(o.sem, 16 if o.is_dma else 1)
            if name == final_wait_eng:
                need = {}
                for o in finals:
                    k = id(o.sem)
                    if k not in need or need[k][1] < o.ticket:
                        need[k] = (o.sem, o.ticket)
                for k, (s, t) in need.items():
                    e.wait_ge(s, t)

        with nc.Block() as block:
            @block.sync
            def _(e):
                run_stream("sp", e)

            @block.tensor
            def _(e):
                run_stream("pe", e)

            @block.scalar
            def _(e):
                run_stream("act", e)

            @block.vector
            def _(e):
                run_stream("dve", e)

            @block.gpsimd
            def _(e):
                run_stream("pool", e)


class Arena:
    def __init__(self, nc, nbytes):
        self.t = nc.alloc_sbuf_tensor("arena", [128, nbytes], U8)
        self.n = nbytes
        self.off = 0
        self.peak = 0

    def alloc(self, shape, dtype, parts=128):
        esz = {F32: 4, BF16: 2, I32: 4, U8: 1}[dtype]
        n = 1
        for s in shape:
            n *= s
        nb = n * esz
        self.off = (self.off + 63) // 64 * 64
        assert self.off + nb <= self.n, ("SBUF arena overflow", self.off, nb, self.n)
        v = self.t[0:parts, self.off:self.off + nb]
        if dtype != U8:
            v = v.bitcast(dtype)
        if len(shape) == 2:
            v = v.rearrange("p (a b) -> p a b", a=shape[0])
        elif len(shape) == 3:
            v = v.rearrange("p (a b c) -> p a b c", a=shape[0], b=shape[1])
        elif len(shape) == 4:
            v = v.rearrange("p (a b c d) -> p a b c d", a=shape[0], b=shape[1], c=shape[2])
        self.off += nb
        self.peak = max(self.peak, self.off)
        return v

    def mark(self):
        return self.off

    def release(self, m):
        self.off = m


class Cfg:
    def __init__(self, nb=4, seq=4096, db=128, past=8192):
        self.NB, self.SEQ, self.DB, self.PAST = nb, seq, db, past
        self.NC = 2 * nb
        self.NSB = seq // 512
        self.NSQ = db // self.NC
        assert self.NSQ % 2 == 0 and past == 8192 and seq % 512 == 0
        self.NPAIR = self.NSQ // 2
        self.TP = self.NSB * 256
        self.TS = self.NSQ * 4
        self.T = self.TP + self.TS
        self.NK = self.NSB * 512
        self.NKB = self.NK // 128
        self.NPG = past // 128
        self.NPHYS = db * self.NPG + (db * self.NPG) // 4
        self.NG = 2 if self.NSB >= 2 else 1
        self.SBG = self.NSB // self.NG
        self.SQG = self.NSQ // self.NG


class _Stop(Exception):
    pass


def build(cfg):
    nc = bass.Bass("TRN2", target_bir_lowering=False)
    P = Prog(nc)
    try:
        _build_body(cfg, nc, P)
    except _Stop:
        pass
    return nc, P, None


def _build_body(cfg, nc, P):
    def done(tag):
        if getattr(cfg, 'stop', None) == tag:
            P.emit()
            raise _Stop()
    NSB, NSQ, NPAIR, TP, TS, T, NK, NKB = cfg.NSB, cfg.NSQ, cfg.NPAIR, cfg.TP, cfg.TS, cfg.T, cfg.NK, cfg.NKB

    def din(name, shape, dt=F32):
        return nc.dram_tensor(name, list(shape), dt, kind="ExternalInput").ap()

    def dout(name, shape, dt=F32):
        return nc.dram_tensor(name, list(shape), dt, kind="ExternalOutput").ap()

    xk = din("xk", [NK, D]); xs = din("xs", [TS, D]); xh = din("xh", [NSB * 32, D])
    cs_k = din("cs_k", [NK, 64]); cs_s = din("cs_s", [TS, 64])
    maskp_d = din("maskp", [128, 4, 256]); maskpair_d = din("maskpair", [128, 64]); masknew_d = din("masknew", [TS, NSQ * 32])
    ident_d = din("ident", [128, 128]); ind_d = din("ind", [128, 4, 8])
    cache_lat = din("cache_lat", [cfg.NPHYS, 128 * LAT]); cache_rope = din("cache_rope", [cfg.NPHYS, 128 * ROPE])
    state_d = din("state", [NSQ * 30, DCONV]); ptT_d = din("ptT", [128, NPAIR], I32)
    w_in = din("w_in", [D, DIN]); w_uq = din("w_uq", [QL, NH * DQK]); w_uk = din("w_uk", [LAT, 512]); w_uv = din("w_uv", [LAT, 512])
    w_o = din("w_o_mla", [512, D]); w_co = din("w_conv_out", [512, D]); w_out = din("w_out", [D, D])
    w_gate = din("w_gate", [D, DFF]); w_up = din("w_up", [D, DFF]); w_down = din("w_down", [DFF, D])
    gmixT_d = din("gmixT", [128, 8]); gffnT_d = din("gffnT", [128, 8]); convwT_d = din("convwT", [128, 4, CW])
    convbT_d = din("convbT", [128, 4]); lngT_d = din("lngT", [128, 4]); lnbT_d = din("lnbT", [128, 4])
    gqa_d = din("gqa", [1, QL]); gkv_d = din("gkv", [1, LAT]); gq_d = din("gq", [1, 80]); gk_d = din("gk", [1, 80])

    y_own = dout("y_own", [T, D]); lat_k = dout("lat_k", [NK, LAT]); kpe_k = dout("kpe_k", [NK, ROPE])
    lat_s = dout("lat_s", [TS, LAT]); kpe_s = dout("kpe_s", [TS, ROPE])
    cst_p = dout("cst_p", [32, DCONV]); cst_s = dout("cst_s", [NSQ, 30, DCONV])

    A = Arena(nc, 207 * 1024)
    psb = [nc.alloc_psum_tensor("psb%d" % i, [128, 512], F32) for i in range(8)]
    cnt = {"ev": 0, "q": 0}

    def psf(b):
        return psb[b][:]

    def psh(b):
        return psb[b][:].bitcast(BF16)

    def dmaq():
        cnt["q"] += 1
        return "sp" if cnt["q"] % 2 else "actq"

    def dma(q, out, in_, reads=(), writes=(), semkey=None, final=False):
        return P.op(q, lambda e: e.dma_start(out=out, in_=in_), reads=reads, writes=writes, semkey=semkey, final=final)

    def mm(out, lhsT, rhs, start, stop, reads, writes):
        return P.op("pe", lambda e: e.matmul(out, lhsT=lhsT, rhs=rhs, start=start, stop=stop), reads=reads, writes=writes)

    def tr(out, in_, idn, reads, writes):
        return P.op("pe", lambda e: e.transpose(out, in_, idn), reads=reads, writes=writes)

    def act(out, in_, func, reads, writes, scale=1.0, bias=0.0, accum=None):
        if accum is not None:
            return P.op("act", lambda e: e.activation(out=out, in_=in_, func=func, scale=scale, bias=bias, accum_out=accum), reads=reads, writes=writes)
        return P.op("act", lambda e: e.activation(out=out, in_=in_, func=func, scale=scale, bias=bias), reads=reads, writes=writes)

    def evac(out, in_, reads, writes, eng=None):
        if eng is None:
            cnt["ev"] += 1
            eng = "act" if cnt["ev"] % 2 else "dve"
        if eng == "act":
            return P.op("act", lambda e: e.activation(out=out, in_=in_, func=AF.Copy), reads=reads, writes=writes)
        return P.op(eng, lambda e: e.tensor_copy(out, in_), reads=reads, writes=writes)

    def tt(out, a, b, op, reads, writes, eng="dve"):
        return P.op(eng, lambda e: e.tensor_tensor(out, a, b, op), reads=reads, writes=writes)

    def ts(out, a, s1, s2, op0, op1, reads, writes, eng="dve"):
        if s2 is None:
            return P.op(eng, lambda e: e.tensor_scalar(out, a, s1, None, op0), reads=reads, writes=writes)
        return P.op(eng, lambda e: e.tensor_scalar(out, a, s1, s2, op0, op1), reads=reads, writes=writes)

    def stt(out, a, s, b, op0, op1, reads, writes, eng="dve"):
        return P.op(eng, lambda e: e.scalar_tensor_tensor(out, a, s, b, op0, op1), reads=reads, writes=writes)

    def recip(out, in_, reads, writes):
        return P.op("dve", lambda e: e.reciprocal(out, in_), reads=reads, writes=writes)

    def rsqrt_of(dst, src, scale, r, key_src, key_dst):
        act(dst, src, AF.Ln, [key_src, "epsc"], [key_dst], scale=scale, bias=epsc[0:r, 0:1])
        act(dst, dst, AF.Exp, [key_dst], [key_dst], scale=-0.5)

    ident_f = A.alloc([128], F32); ident_b = A.alloc([128], BF16)
    ind_b = A.alloc([4, 8], BF16); ones_b = A.alloc([128], BF16); ones_f = A.alloc([128], F32)
    epsc = A.alloc([1], F32)
    gmixT = A.alloc([8], F32); gffnT = A.alloc([8], F32)
    gqa_b = A.alloc([QL], F32); gkv_b = A.alloc([LAT], F32)
    gqk96 = A.alloc([DQK], F32)
    OT = A.alloc([4, T], BF16)
    wuk_b = A.alloc([2, 512], BF16); wuv_b = A.alloc([2, 512], BF16)
    stage = A.alloc([4, 8], F32)
    tmp80a = A.alloc([80], F32); tmp80b = A.alloc([80], F32)

    dma("sp", ident_f, ident_d, writes=["identf"])
    dma("actq", stage, ind_d, writes=["stage"])
    dma("sp", gmixT, gmixT_d, writes=["gmixT"]); dma("actq", gffnT, gffnT_d, writes=["gffnT"])
    dma("sp", gqa_b, gqa_d.partition_broadcast(128), writes=["gqa"])
    dma("actq", gkv_b, gkv_d.partition_broadcast(128), writes=["gkv"])
    dma("sp", tmp80a, gq_d.partition_broadcast(128), writes=["t80a"])
    dma("actq", tmp80b, gk_d.partition_broadcast(128), writes=["t80b"])
    dma("poolq", wuk_b, w_uk.rearrange("(c p) n -> p c n", p=128), writes=["wuk"])
    dma("poolq", wuv_b, w_uv.rearrange("(c p) n -> p c n", p=128), writes=["wuv"])
    P.op("dve", lambda e: e.tensor_copy(ident_b, ident_f), reads=["identf"], writes=["ident"])
    P.op("dve", lambda e: e.tensor_copy(ind_b, stage), reads=["stage"], writes=["ind"])
    P.op("pool", lambda e: e.memset(ones_b, 1.0), writes=["onesb"])
    P.op("pool", lambda e: e.memset(ones_f, 1.0), writes=["onesf"])
    P.op("pool", lambda e: e.memset(epsc, EPS), writes=["epsc"])
    stt(tmp80a, tmp80a, 1.0 / math.sqrt(DQK), tmp80b, ALU.mult, ALU.mult, ["t80a", "t80b"], ["t80a"])
    evac(gqk96[:, 0:80], tmp80a, ["t80a"], ["gqk"], eng="dve")
    evac(gqk96[:, 80:96], tmp80a[:, 64:80], ["t80a"], ["gqk"], eng="dve")

    qlatT = A.alloc([2, TS, NH], BF16)
    qpeT = A.alloc([TS, NH], BF16)
    CnewT = A.alloc([2, TS], BF16); kpenewT = A.alloc([TS], BF16); Cnew = A.alloc([LAT], BF16)
    rinvnew = A.alloc([NH], F32)
    wukT = A.alloc([NH, LAT], BF16)
    done('const')
    m_persist = A.mark()

    KT = A.alloc([NH, NK], BF16)
    V = A.alloc([NKB, NH, 65], BF16)
    wkv_b = A.alloc([8, 288], BF16); wcq_b = A.alloc([8, QL], BF16); wuq_b = A.alloc([3, NH * DQK], BF16)
    maskp_b = A.alloc([4, 256], BF16)
    NXS = 2
    xst = [A.alloc([D], F32) for _ in range(NXS)]
    hn = [A.alloc([D], BF16) for _ in range(2)]
    hTb = [A.alloc([8, 128], BF16) for _ in range(2)]
    cst = [A.alloc([64], F32) for _ in range(2)]
    small = [A.alloc([64], F32) for _ in range(2)]
    ckvf = [A.alloc([LAT], F32) for _ in range(2)]
    ckvb = [A.alloc([LAT], BF16) for _ in range(2)]
    kpef = [A.alloc([ROPE], F32) for _ in range(2)]
    kt1A = [A.alloc([ROPE], F32) for _ in range(2)]; kt2A = [A.alloc([ROPE], F32) for _ in range(2)]
    CTb = [A.alloc([2, 128], BF16) for _ in range(2)]
    sqA = [A.alloc([1024], BF16) for _ in range(2)]
    Kn = [A.alloc([NH, DQK], BF16) for _ in range(2)]
    cqnA = [A.alloc([QL], BF16) for _ in range(2)]; cqTA = [A.alloc([3, 128], BF16) for _ in range(2)]
    qfA = [A.alloc([NH, DQK], F32) for _ in range(2)]; qt1A = [A.alloc([NH, ROPE], F32) for _ in range(2)]; qt2A = [A.alloc([NH, ROPE], F32) for _ in range(2)]
    QnA = [A.alloc([NH, DQK], BF16) for _ in range(2)]
    QT = [A.alloc([NH, 256], BF16) for _ in range(2)]
    PT = [A.alloc([2, 256], BF16) for _ in range(3)]
    Otok = A.alloc([2, NH, 64], BF16)
    rden = A.alloc([2], F32)

    dma("poolq", wkv_b, w_in[:, I_KV:I_KV + 288].rearrange("(c p) n -> p c n", p=128), writes=["wkv"])
    dma("poolq", wcq_b, w_in[:, I_CQ:I_CQ + QL].rearrange("(c p) n -> p c n", p=128), writes=["wcq"])
    dma("poolq", wuq_b, w_uq.rearrange("(c p) n -> p c n", p=128), writes=["wuq"])
    maskp_f = xst[0].rearrange("p (a q) -> p a q", a=4)
    dma("sp", maskp_f, maskp_d, writes=["xst0"])
    evac(maskp_b, maskp_f, ["xst0"], ["maskp"], eng="dve")
    P.op("pool", lambda e: e.memset(V[:, :, :, 64:65], 1.0), writes=["Vones"])

    for h in range(NH):
        for c in range(2):
            bnk = 4 + (h * 2 + c) % 4
            tr(psh(bnk)[0:64, 0:128], wuk_b[:, c, h * 64:(h + 1) * 64], ident_b, ["wuk", "ident"], ["ps%d" % bnk])
            evac(wukT[0:64, h, c * 128:(c + 1) * 128], psh(bnk)[0:64, 0:128], ["ps%d" % bnk], ["wukT"])

    ring = {"g": 0}

    def gbank():
        ring["g"] = (ring["g"] + 1) % 4
        return ring["g"]

    blk_ctr = {"n": 0}

    def token_block(xsrc, cssrc, r, mode, kb=None, qslot=None, qcol=None, latdst=None, kpedst=None):
        n = blk_ctr["n"]; blk_ctr["n"] += 1
        s2 = n % 2
        xt = xst[n % NXS]; kx = "xst%d" % (n % NXS)
        sm = small[s2]; ksm = "small%d" % s2
        hnb = hn[s2]; khn = "hn%d" % s2
        hT = hTb[s2]; khT = "hT%d" % s2
        cs = cst[s2]; kcs = "cs%d" % s2
        sq = sqA[s2]; kt1 = kt1A[s2]; kt2 = kt2A[s2]; cqn = cqnA[s2]; cqT = cqTA[s2]
        qf = qfA[s2]; qt1 = qt1A[s2]; qt2 = qt2A[s2]; Qn = QnA[s2]
        S_ = "_%d" % s2
        dma(dmaq(), xt[0:r], xsrc, writes=[kx])
        dma(dmaq(), cs[0:r], cssrc, writes=[kcs])
        act(sq[0:r, 0:D], xt[0:r], AF.Square, [kx], ["sq" + S_, ksm + "a"], accum=sm[0:r, 0:1])
        rsqrt_of(sm[0:r, 1:2], sm[0:r, 0:1], 1.0 / D, r, ksm + "a", ksm + "b")
        ts(hnb[0:r], xt[0:r], sm[0:r, 1:2], None, ALU.mult, None, [kx, ksm + "b"], [khn])
        yield
        b0 = gbank(); k0 = "ps%d" % b0
        pT = psh(b0).rearrange("p (c t) -> p c t", c=8)
        for c in range(8):
            tr(pT[:, c, 0:r], hnb[0:r, c * 128:(c + 1) * 128], ident_b[0:r, 0:r], [khn, "ident"], [k0])
        tt(hT[:, :, 0:r], pT[:, :, 0:r], gmixT.unsqueeze(2).to_broadcast([128, 8, r]), ALU.mult, [k0, "gmixT"], [khT])
        yield
        b1 = gbank(); k1 = "ps%d" % b1
        kvp = psf(b1)
        for c in range(8):
            mm(kvp[0:r, 0:288], hT[:, c, 0:r], wkv_b[:, c, :], c == 0, c == 7, [khT, "wkv"], [k1])
        cf = ckvf[s2]; kcf = "ckvf%d" % s2
        cb = ckvb[s2]; kcb = "ckvb%d" % s2
        kp = kpef[s2]; kkp = "kpef%d" % s2
        act(sq[0:r, 0:LAT], kvp[0:r, 0:LAT], AF.Square, [k1], ["sq" + S_, ksm + "c"], accum=sm[0:r, 2:3])
        rsqrt_of(sm[0:r, 3:4], sm[0:r, 2:3], 1.0 / LAT, r, ksm + "c", ksm + "d")
        stt(cf[0:r], kvp[0:r, 0:LAT], sm[0:r, 3:4], gkv_b[0:r], ALU.mult, ALU.mult, [k1, ksm + "d", "gkv"], [kcf])
        dma(dmaq(), latdst, cf[0:r], reads=[kcf], semkey="o_lat%d" % s2, final=True)
        evac(cb[0:r], cf[0:r], [kcf], [kcb], eng="act")
        tt(kt1[0:r], kvp[0:r, 256:288], cs[0:r, 0:32], ALU.mult, [k1, kcs], ["kt1" + S_])
        tt(kt2[0:r, 0:16], kvp[0:r, 272:288], cs[0:r, 32:48], ALU.mult, [k1, kcs], ["kt2a" + S_])
        tt(kt2[0:r, 16:32], kvp[0:r, 256:272], cs[0:r, 48:64], ALU.mult, [k1, kcs], ["kt2b" + S_])
        tt(kp[0:r], kt1[0:r], kt2[0:r], ALU.add, ["kt1" + S_, "kt2a" + S_, "kt2b" + S_], [kkp])
        dma(dmaq(), kpedst, kp[0:r], reads=[kkp], semkey="o_kpe%d" % s2, final=True)
        yield
        b2 = gbank(); k2 = "ps%d" % b2
        cTp = psh(b2).rearrange("p (c t) -> p c t", c=8)
        for c in range(2):
            tr(cTp[:, c, 0:r], cb[0:r, c * 128:(c + 1) * 128], ident_b[0:r, 0:r], [kcb, "ident"], [k2])
        if mode == "sample":
            CT = CnewT; kCT = "CnewT"
            evac(CT[:, :, 0:r], cTp[:, 0:2, 0:r], [k2], [kCT])
        else:
            CT = CTb[s2]; kCT = "CT%d" % s2
            evac(CT[:, :, 0:r], cTp[:, 0:2, 0:r], [k2], [kCT])
        yield
        b3 = gbank(); k3 = "ps%d" % b3
        knp = psf(b3)
        for c in range(2):
            mm(knp[0:r, :], CT[:, c, 0:r], wuk_b[:, c, :], c == 0, c == 1, [kCT, "wuk"], [k3])
        act(sq[0:r, 0:512], knp[0:r, :], AF.Square, [k3], ["sq" + S_])
        P.op("dve", lambda e: e.tensor_reduce(sm[0:r, 8:16], sq[0:r, 0:512].rearrange("p (h d) -> p h d", h=NH), AX.X, ALU.add),
             reads=["sq" + S_], writes=[ksm + "e"])
        act(kt1[0:r], kp[0:r], AF.Square, [kkp], ["kt1" + S_, ksm + "f"], accum=sm[0:r, 4:5])
        ts(sm[0:r, 8:16], sm[0:r, 8:16], sm[0:r, 4:5], None, ALU.add, None, [ksm + "e", ksm + "f"], [ksm + "e"])
        rinv = rinvnew if mode == "sample" else sm[:, 8:16]
        krinv = "rinvnew" if mode == "sample" else ksm + "g"
        rsqrt_of(rinv[0:r], sm[0:r, 8:16], 1.0 / DQK, r, ksm + "e", krinv)
        if mode == "sample":
            evac(Cnew[0:r], cf[0:r], [kcf], ["Cnew"], eng="dve")
            yield
            b4 = gbank(); k4 = "ps%d" % b4
            tr(psf(b4)[0:32, 0:r], kp[0:r], ident_f[0:r, 0:r], [kkp, "identf"], [k4])
            evac(kpenewT[0:32, 0:r], psf(b4)[0:32, 0:r], [k4], ["kpenewT"])
        else:
            knb = Kn[s2]; kkn = "Kn%d" % s2
            tt(knb[0:r, :, 0:64], knp[0:r, :].rearrange("p (h d) -> p h d", h=NH),
               sm[0:r, 8:16].unsqueeze(2).to_broadcast([r, NH, 64]), ALU.mult, [k3, krinv], [kkn + "a"])
            tt(knb[0:r, :, 64:96], kp[0:r].unsqueeze(1).to_broadcast([r, NH, ROPE]),
               sm[0:r, 8:16].unsqueeze(2).to_broadcast([r, NH, ROPE]), ALU.mult, [kkp, krinv], [kkn + "b"])
            yield
            b4 = gbank(); k4 = "ps%d" % b4
            kTp = psh(b4).rearrange("p (h t) -> p h t", h=NH)
            for h in range(NH):
                tr(kTp[0:DQK, h, 0:r], knb[0:r, h, :], ident_b[0:r, 0:r], [kkn + "a", kkn + "b", "ident"], [k4])
            evac(KT[0:DQK, :, kb * 128:kb * 128 + r], kTp[0:DQK, :, 0:r], [k4], ["KT%d" % kb])
            yield
            b5 = gbank(); k5 = "ps%d" % b5
            vp = psf(b5)
            for c in range(2):
                mm(vp[0:r, :], CT[:, c, 0:r], wuv_b[:, c, :], c == 0, c == 1, [kCT, "wuv"], [k5])
            evac(V[0:r, kb, :, 0:64], vp[0:r, :].rearrange("p (h d) -> p h d", h=NH), [k5, "Vones"], ["V%d" % kb])
        if mode == "other":
            return
        yield
        yield
        b6 = gbank(); k6 = "ps%d" % b6
        cqp = psf(b6)
        for c in range(8):
            mm(cqp[0:r, 0:QL], hT[:, c, 0:r], wcq_b[:, c, :], c == 0, c == 7, [khT, "wcq"], [k6])
        act(sq[0:r, 0:QL], cqp[0:r, 0:QL], AF.Square, [k6], ["sq" + S_, ksm + "h"], accum=sm[0:r, 5:6])
        rsqrt_of(sm[0:r, 6:7], sm[0:r, 5:6], 1.0 / QL, r, ksm + "h", ksm + "i")
        stt(cqn[0:r], cqp[0:r, 0:QL], sm[0:r, 6:7], gqa_b[0:r], ALU.mult, ALU.mult, [k6, ksm + "i", "gqa"], ["cqn" + S_])
        yield
        b7 = gbank(); k7 = "ps%d" % b7
        cqTp = psh(b7).rearrange("p (c t) -> p c t", c=8)
        for c in range(3):
            tr(cqTp[:, c, 0:r], cqn[0:r, c * 128:(c + 1) * 128], ident_b[0:r, 0:r], ["cqn" + S_, "ident"], [k7])
        evac(cqT[:, :, 0:r], cqTp[:, 0:3, 0:r], [k7], ["cqT" + S_])
        qfl = qf.rearrange("p h d -> p (h d)")
        for half in range(2):
            yield
            b8 = gbank(); k8 = "ps%d" % b8
            qp = psf(b8)
            for c in range(3):
                mm(qp[0:r, 0:QL], cqT[:, c, 0:r], wuq_b[:, c, half * QL:(half + 1) * QL], c == 0, c == 2, ["cqT" + S_, "wuq"], [k8])
            evac(qfl[0:r, half * QL:(half + 1) * QL], qp[0:r, 0:QL], [k8], ["qf%d" % half + S_])
        kq = ["qf0" + S_, "qf1" + S_]
        qr = qf[0:r, :, 64:96]
        tt(qt1[0:r], qr, cs[0:r, 0:32].unsqueeze(1).to_broadcast([r, NH, 32]), ALU.mult, kq + [kcs], ["qt1" + S_])
        tt(qt2[0:r, :, 0:16], qf[0:r, :, 80:96], cs[0:r, 32:48].unsqueeze(1).to_broadcast([r, NH, 16]), ALU.mult, kq + [kcs], ["qt2a" + S_])
        tt(qt2[0:r, :, 16:32], qf[0:r, :, 64:80], cs[0:r, 48:64].unsqueeze(1).to_broadcast([r, NH, 16]), ALU.mult, kq + [kcs], ["qt2b" + S_])
        tt(qr, qt1[0:r], qt2[0:r], ALU.add, ["qt1" + S_, "qt2a" + S_, "qt2b" + S_], ["qf0" + S_, "qf1" + S_])
        act(sq[0:r, 0:768], qfl[0:r], AF.Square, kq, ["sq" + S_])
        P.op("dve", lambda e: e.tensor_reduce(sm[0:r, 16:24], sq[0:r, 0:768].rearrange("p (h d) -> p h d", h=NH), AX.X, ALU.add),
             reads=["sq" + S_], writes=[ksm + "j"])
        rsqrt_of(sm[0:r, 16:24], sm[0:r, 16:24], 1.0 / DQK, r, ksm + "j", ksm + "j")
        tt(qf[0:r], qf[0:r], sm[0:r, 16:24].unsqueeze(2).to_broadcast([r, NH, DQK]), ALU.mult, kq + [ksm + "j"], kq)
        tt(Qn[0:r], qf[0:r], gqk96[0:r].unsqueeze(1).to_broadcast([r, NH, DQK]), ALU.mult, kq + ["gqk"], ["Qn" + S_])
        if mode == "own":
            yield
            b9 = gbank(); k9 = "ps%d" % b9
            qTp = psh(b9).rearrange("p (h t) -> p h t", h=NH)
            for h in range(NH):
                tr(qTp[0:DQK, h, 0:r], Qn[0:r, h, :], ident_b[0:r, 0:r], ["Qn" + S_, "ident"], [k9])
            evac(QT[qslot][0:DQK, :, qcol:qcol + r], qTp[0:DQK, :, 0:r], [k9], ["QT%d_%d" % (qslot, qcol)])
        else:
            yield
            b9 = gbank(); k9 = "ps%d" % b9
            qnTp = psh(b9).rearrange("p (h t) -> p h t", h=NH)
            for h in range(NH):
                tr(qnTp[0:64, h, 0:r], Qn[0:r, h, 0:64], ident_b[0:r, 0:r], ["Qn" + S_, "ident"], [k9])
            qnT = Kn[0]
            qnTv = qnT.rearrange("p h d -> p (h d)")[0:64, 0:NH * r].rearrange("p (h t) -> p h t", h=NH)
            evac(qnTv, qnTp[0:64, :, 0:r], [k9], ["Kn0a", "Kn0b"])
            yield
            b10 = gbank(); k10 = "ps%d" % b10
            qpTp = psh(b10).rearrange("p (h t) -> p h t", h=NH)
            for h in range(NH):
                tr(qpTp[0:32, h, 0:r], Qn[0:r, h, 64:96], ident_b[0:r, 0:r], ["Qn" + S_, "ident"], [k10])
            evac(qpeT[0:32, 0:r, :].rearrange("p t h -> p h t"), qpTp[0:32, :, 0:r], [k10], ["qpeT"])
            for c in range(2):
                yield
                b11 = gbank(); k11 = "ps%d" % b11
                ql = psf(b11).rearrange("p (h t) -> p h t", h=NH)
                for h in range(NH):
                    mm(ql[:, h, 0:r], wukT[0:64, h, c * 128:(c + 1) * 128], qnTv[:, h, :], True, True, ["wukT", "Kn0a", "Kn0b"], [k11])
                evac(qlatT[:, c, 0:r, :].rearrange("p t h -> p h t"), ql[:, :, 0:r], [k11], ["qlatT"])

    att_u = {"n": 0}

    def attention_superblock(j, qslot):
        nkb = 4 * j + 4
        qk = ["QT%d_0" % qslot, "QT%d_128" % qslot]
        units = [(h, kp2) for h in range(NH) for kp2 in range(nkb // 2)]
        slots = []
        for idx in range(len(units) + 1):
            if idx < len(units):
                h, kp2 = units[idx]
                sb_ = gbank(); ks = "ps%d" % sb_
                sT = psf(sb_).rearrange("p (a q) -> p a q", a=2)
                pslot = att_u["n"] % 3; att_u["n"] += 1
                slots.append(pslot)
                pt = PT[pslot]; kpt = "PT%d" % pslot
                for a in range(2):
                    kb = kp2 * 2 + a
                    masked = kb >= 4 * j
                    mm(sT[:, a, :], KT[0:DQK, h, kb * 128:(kb + 1) * 128], QT[qslot][0:DQK, h, :], True, not masked,
                       ["KT%d" % kb] + qk, [ks])
                    if masked:
                        mm(sT[:, a, :], ident_b, maskp_b[:, kb - 4 * j, :], False, True, ["ident", "maskp"], [ks])
                act(pt, sT, AF.Exp, [ks], [kpt])
            if idx >= 1:
                h, kp2 = units[idx - 1]
                pslot = slots[idx - 1]
                pt = PT[pslot]; kpt = "PT%d" % pslot
                ob = [4 + 2 * (h % 2), 5 + 2 * (h % 2)]
                okeys = ["ps%d" % ob[0], "ps%d" % ob[1]]
                for a in range(2):
                    kb = kp2 * 2 + a
                    for half in range(2):
                        mm(psf(ob[half])[:, 0:65], pt[:, a, half * 128:(half + 1) * 128], V[:, kb, h, :],
                           kb == 0, kb == nkb - 1, [kpt, "V%d" % kb, "Vones"], [okeys[half]])
                if kp2 == nkb // 2 - 1:
                    for half in range(2):
                        recip(rden[:, half:half + 1], psf(ob[half])[:, 64:65], [okeys[half]], ["rden%d" % half])
                        ts(Otok[:, half, h, :], psf(ob[half])[:, 0:64], rden[:, half:half + 1], None, ALU.mult, None,
                           [okeys[half], "rden%d" % half], ["Otok%d" % half])
            yield
        for half in range(2):
            b = gbank(); k = "ps%d" % b
            oTp = psh(b).rearrange("p (c t) -> p c t", c=8)
            ofl = Otok[:, half].rearrange("p h d -> p (h d)")
            for c in range(4):
                tr(oTp[:, c, 0:128], ofl[:, c * 128:(c + 1) * 128], ident_b, ["Otok%d" % half, "ident"], [k])
            col = j * 256 + half * 128
            evac(OT[:, :, col:col + 128], oTp[:, 0:4, 0:128], [k], ["OT"])
        yield

    def interleave(g1, g2, r1=1, r2=1):
        it1, it2 = iter(g1), iter(g2)
        d1 = d2 = False
        while not (d1 and d2):
            for _ in range(r1):
                if not d1:
                    try:
                        next(it1)
                    except StopIteration:
                        d1 = True
            for _ in range(r2):
                if not d2:
                    try:
                        next(it2)
                    except StopIteration:
                        d2 = True
            yield

    def chain(gens):
        for g_ in gens:
            yield from g_

    def tile_blocks(i):
        gs = []
        for blk in range(4):
            kb = i * 4 + blk
            rows = slice(kb * 128, (kb + 1) * 128)
            gs.append(token_block(xk[rows, :], cs_k[rows, :], 128, "own" if blk < 2 else "other", kb=kb, qslot=i % 2,
                                  qcol=(blk % 2) * 128, latdst=lat_k[rows, :], kpedst=kpe_k[rows, :]))
        return chain([interleave(gs[0], gs[1]), interleave(gs[2], gs[3])])

    def drain(g_):
        for _ in g_:
            pass

    done('p1pre')
    drain(token_block(xs[0:TS, :], cs_s[0:TS, :], TS, "sample", latdst=lat_s[0:TS, :], kpedst=kpe_s[0:TS, :]))
    done('p1s')
    drain(tile_blocks(0))
    for i in range(NSB):
        nA = NH * (2 * i + 2) + 2
        if i + 1 < NSB:
            nB = 50
            drain(interleave(attention_superblock(i, i % 2), tile_blocks(i + 1), max(1, round(nA / nB)), max(1, round(nB / nA))))
        else:
            drain(attention_superblock(i, i % 2))
        done('p1a%d' % i)

    P.barrier()
    A.release(m_persist)
    NGB = 6
    Cg = [A.alloc([8, LAT], BF16) for _ in range(NGB)]
    Rg = [A.alloc([128, ROPE], BF16) for _ in range(2)]
    ptT = A.alloc([NPAIR], I32); ptf = A.alloc([NPAIR], F32)
    rgc_f = A.alloc([16], F32)
    idxl_f = A.alloc([NPAIR, 16], F32); idxl = A.alloc([NPAIR, 16], I32)
    idxr_f = A.alloc([NPAIR, 2], F32); idxr = A.alloc([NPAIR, 2], I32)
    mpair_f = A.alloc([64], F32); mpair_b = A.alloc([64], BF16)
    mnew_f = A.alloc([NSQ * 32], F32); mnew_b = A.alloc([NSQ * 32], BF16)
    CTs = [A.alloc([2, 256], BF16) for _ in range(4)]
    kpTs = [A.alloc([256], BF16) for _ in range(4)]
    kpTq = [A.alloc([256], BF16) for _ in range(4)]
    sqs = [A.alloc([4, 256], BF16) for _ in range(3)]
    rinv_s = [A.alloc([2, NH], F32) for _ in range(3)]
    scs = [A.alloc([2, 64], F32) for _ in range(3)]
    PTs = [A.alloc([2, 64], BF16) for _ in range(3)]
    scn_f = A.alloc([NSQ * 32], F32); PnewT = A.alloc([NSQ * 32], BF16)
    olat_n = A.alloc([LAT], BF16); rden_s = A.alloc([1], F32)
    olatT_all = A.alloc([2, NH, TS], BF16)
    Os = A.alloc([512], BF16)
    NQ = NSQ * 32

    dma("sp", ptT, ptT_d, writes=["ptT"])
    dma("actq", mpair_f, maskpair_d, writes=["mpairf"])
    dma("sp", mnew_f[0:TS], masknew_d, writes=["mnewf"])
    evac(mpair_b, mpair_f, ["mpairf"], ["mpair"], eng="dve")
    evac(mnew_b[0:TS], mnew_f[0:TS], ["mnewf"], ["mnew"], eng="dve")
    for k in range(16):
        P.op("pool", lambda e, k=k: e.memset(rgc_f[:, k:k + 1], float(k)), writes=["rgc%d" % k])
    rgk = ["rgc%d" % k for k in range(16)]
    evac(ptf, ptT, ["ptT"], ["ptf"], eng="dve")
    for pr in range(NPAIR):
        stt(idxl[:, pr, :], ptf[:, pr:pr + 1].to_broadcast([128, 16]), 16.0, rgc_f, ALU.mult, ALU.add, ["ptf"] + rgk, ["idxl"])
        stt(idxr[:, pr, :], ptf[:, pr:pr + 1].to_broadcast([128, 2]), 2.0, rgc_f[:, 0:2], ALU.mult, ALU.add, ["ptf"] + rgk, ["idxr"])
    lat16 = cache_lat.rearrange("n (g e) -> (n g) e", g=16)
    rope2 = cache_rope.rearrange("n (g e) -> (n g) e", g=2)

    def gather(out2d, src, idx_ap, reads, writes):
        return P.op("poolq", lambda e: e.indirect_dma_start(out=out2d, out_offset=None, in_=src,
                                                            in_offset=bass.IndirectOffsetOnAxis(ap=idx_ap, axis=0)),
                    reads=reads, writes=writes)

    scn = psf(2)
    qlat_all = [qlatT[:, c].rearrange("p t h -> p (t h)") for c in range(2)]
    qpe_all = qpeT[0:32].rearrange("p t h -> p (t h)")
    for c in range(2):
        mm(scn[0:TS, 0:NQ], CnewT[:, c, 0:TS], qlat_all[c], c == 0, False, ["CnewT", "qlatT"], ["ps2"])
    mm(scn[0:TS, 0:NQ], kpenewT[0:32, 0:TS], qpe_all, False, False, ["kpenewT", "qpeT"], ["ps2"])
    mm(scn[0:TS, 0:NQ], ident_b[0:TS, 0:TS], mnew_b[0:TS, :], False, True, ["ident", "mnew"], ["ps2"])
    tt(scn_f[0:TS].rearrange("p (t h) -> p t h", h=NH), scn[0:TS, 0:NQ].rearrange("p (t h) -> p t h", h=NH),
       rinvnew[0:TS].unsqueeze(1).to_broadcast([TS, TS, NH]), ALU.mult, ["ps2", "rinvnew"], ["scnf"])
    act(PnewT[0:TS], scn_f[0:TS], AF.Exp, ["scnf"], ["PnewT"])

    done('p2pre')
    Tb = psh(0); kT = "ps0"
    OL = psf(7); DEN = psf(1)
    SS = psf(6)
    NS4, NS3, PF = 4, 3, 3
    its = [(pr, rg, r2) for pr in range(NPAIR) for rg in range(16) for r2 in range(4)]
    NIT = len(its)

    def cg_of(pr, rg):
        gi = (pr * 16 + rg) % NGB
        return Cg[gi], "Cg%d" % gi

    def issue_gather(G):
        if G >= NPAIR * 16:
            return
        pr, rg = G // 16, G % 16
        cg, kcg = cg_of(pr, rg)
        gather(cg.rearrange("p r c -> p (r c)"), lat16, idxl[:, pr, rg:rg + 1], ["idxl"], [kcg])

    def issue_rope(pr):
        if pr >= NPAIR:
            return
        rgt = Rg[pr % 2]
        for hf in range(2):
            gather(rgt[:, hf * 64:(hf + 1) * 64, :].rearrange("p r d -> p (r d)"), rope2, idxr[:, pr, hf:hf + 1], ["idxr"], ["Rg%d" % (pr % 2)])

    def stage_A(n):
        pr, rg, r2 = its[n]
        if r2 == 0:
            if rg == 0:
                issue_rope(pr + 1)
            issue_gather(pr * 16 + rg + PF)
        cg, kcg = cg_of(pr, rg)
        rgt = Rg[pr % 2]; krg = "Rg%d" % (pr % 2)
        s4 = n % NS4
        r0 = r2 * 2
        rglob = rg * 8 + r0
        for a in range(2):
            for c in range(2):
                tr(Tb[:, c * 256 + a * 128:c * 256 + (a + 1) * 128], cg[:, r0 + a, c * 128:(c + 1) * 128], ident_b, [kcg, "ident"], [kT])
        for a in range(2):
            tr(Tb[0:32, 512 + a * 128:512 + (a + 1) * 128], rgt[:, rglob + a, :], ident_b, [krg, "ident"], [kT])
        evac(CTs[s4], Tb[:, 0:512].rearrange("p (c k) -> p c k", c=2), [kT], ["CTs%d" % s4], eng="act")
        evac(kpTs[s4][0:32], Tb[0:32, 512:768], [kT], ["kpTs%d" % s4], eng="act")
        tt(kpTq[s4][0:32], kpTs[s4][0:32], kpTs[s4][0:32], ALU.mult, ["kpTs%d" % s4], ["kpTq%d" % s4])

    def stage_B(n):
        s4, s3, sl = n % NS4, n % NS3, n % 2
        kb0 = 2 + 2 * sl
        kkn = ["ps%d" % kb0, "ps%d" % (kb0 + 1)]
        for hc in range(4):
            dst = psf(kb0 + hc // 2)[:, (hc % 2) * 256:(hc % 2) * 256 + 256]
            for c in range(2):
                mm(dst, wuk_b[:, c, hc * 128:(hc + 1) * 128], CTs[s4][:, c, :], c == 0, c == 1, ["wuk", "CTs%d" % s4], [kkn[hc // 2]])
        for b2 in range(2):
            act(sqs[s3][:, 2 * b2:2 * b2 + 2, :].rearrange("p a k -> p (a k)"), psf(kb0 + b2), AF.Square, [kkn[b2]], ["sqs%d_%d" % (s3, b2)])

    def stage_CD(n):
        pr, rg, r2 = its[n]
        s4, s3, sl = n % NS4, n % NS3, n % 2
        so = sl * 256
        kss = "ps6_%d" % sl
        for a in range(2):
            for hc in range(4):
                mm(SS[:, so + a * 8:so + a * 8 + 8], sqs[s3][:, hc, a * 128:(a + 1) * 128], ind_b[:, hc, :], hc == 0, False,
                   ["sqs%d_%d" % (s3, hc // 2), "ind"], [kss])
            mm(SS[:, so + a * 8:so + a * 8 + 8], kpTq[s4][0:32, a * 128:(a + 1) * 128], ones_b[0:32, 0:8], False, True,
               ["kpTq%d" % s4, "onesb"], [kss])
        rv = rinv_s[s3]
        rsqrt_of(rv.rearrange("p a h -> p (a h)"), SS[:, so:so + 16], 1.0 / DQK, 128, kss, "rinvs%d" % s3)
        qlp = [qlatT[:, c, pr * 8:(pr + 1) * 8, :].rearrange("p t h -> p (t h)") for c in range(2)]
        qpp = qpeT[0:32, pr * 8:(pr + 1) * 8, :].rearrange("p t h -> p (t h)")
        for a in range(2):
            dst = SS[:, so + 64 + a * 64:so + 128 + a * 64]
            for c in range(2):
                mm(dst, CTs[s4][:, c, a * 128:(a + 1) * 128], qlp[c], c == 0, False, ["CTs%d" % s4, "qlatT"], [kss])
            mm(dst, kpTs[s4][0:32, a * 128:(a + 1) * 128], qpp, False, False, ["kpTs%d" % s4, "qpeT"], [kss])
            mm(dst, ident_b, mpair_b, False, True, ["ident", "mpair"], [kss])
        tt(scs[s3].rearrange("p a (t h) -> p a t h", h=NH),
           SS[:, so + 64:so + 192].rearrange("p (a t h) -> p a t h", a=2, h=NH),
           rv.unsqueeze(2).to_broadcast([128, 2, 8, NH]), ALU.mult, [kss, "rinvs%d" % s3], ["scs%d" % s3])
        act(PTs[s3], scs[s3], AF.Exp, ["scs%d" % s3], ["PTs%d" % s3])

    def stage_E(n):
        pr, rg, r2 = its[n]
        s3 = n % NS3
        cg, kcg = cg_of(pr, rg)
        r0 = r2 * 2
        for a in range(2):
            first = (rg == 0 and r2 == 0 and a == 0)
            mm(OL[0:64, 0:LAT], PTs[s3][:, a, :], cg[:, r0 + a, :], first, False, ["PTs%d" % s3, kcg], ["ps7"])
            mm(DEN[0:64, 0:8], PTs[s3][:, a, :], ones_b[:, 0:8], first, False, ["PTs%d" % s3, "onesb"], ["ps1"])
        if rg == 15 and r2 == 3:
            mm(OL[0:64, 0:LAT], PnewT[0:TS, pr * 64:(pr + 1) * 64], Cnew[0:TS, :], False, True, ["PnewT", "Cnew"], ["ps7"])
            mm(DEN[0:64, 0:8], PnewT[0:TS, pr * 64:(pr + 1) * 64], ones_b[0:TS, 0:8], False, True, ["PnewT", "onesb"], ["ps1"])
            recip(rden_s[0:64], DEN[0:64, 0:1], ["ps1"], ["rdens"])
            ts(olat_n[0:64], OL[0:64, 0:LAT], rden_s[0:64, 0:1], None, ALU.mult, None, ["ps7", "rdens"], ["olatn"])
            for c in range(2):
                tr(Tb[:, c * 64:(c + 1) * 64], olat_n[0:64, c * 128:(c + 1) * 128], ident_b[0:64, 0:64], ["olatn", "ident"], [kT])
            for c in range(2):
                evac(olatT_all[:, c, :, pr * 8:(pr + 1) * 8], Tb[:, c * 64:(c + 1) * 64].rearrange("p (t h) -> p h t", h=NH), [kT], ["olatT"], eng="act")

    issue_rope(0)
    for G in range(PF):
        issue_gather(G)
    for n in range(NIT + 3):
        if n < NIT:
            stage_A(n)
        if 0 <= n - 1 < NIT:
            stage_B(n - 1)
        if 0 <= n - 2 < NIT:
            stage_CD(n - 2)
        if 0 <= n - 3 < NIT:
            stage_E(n - 3)

    OSP = psf(2)
    for h in range(NH):
        for c in range(2):
            mm(OSP[0:TS, h * 64:(h + 1) * 64], olatT_all[:, c, h, :], wuv_b[:, c, h * 64:(h + 1) * 64], c == 0, c == 1, ["olatT", "wuv"], ["ps2"])
    evac(Os[0:TS], OSP[0:TS, :], ["ps2"], ["Os"])
    for c in range(4):
        tr(Tb[:, c * 64:c * 64 + TS], Os[0:TS, c * 128:(c + 1) * 128], ident_b[0:TS, 0:TS], ["Os", "ident"], [kT])
    for c in range(4):
        evac(OT[:, c, TP:TP + TS], Tb[:, c * 64:c * 64 + TS], [kT], ["OT"])

    done('p2')
    P.barrier()
    A.release(m_persist)
    NG, SBG, SQG = cfg.NG, cfg.SBG, cfg.SQG
    TGp, TGs, TH = SBG * 256, SQG * 4, SBG * 32
    TG = TGp + TGs
    NBLK = SBG * 2 + 1
    x1 = A.alloc([NBLK, D], F32)
    hTg = A.alloc([8, TG + TH], BF16)
    YT = A.alloc([4, TG], BF16)
    lngT = A.alloc([4], F32); lnbT = A.alloc([4], F32); convwT = A.alloc([4, CW], F32); convbT = A.alloc([4], F32)
    sm3 = A.alloc([4 * NBLK + 8], F32)
    hn3 = [A.alloc([D], BF16) for _ in range(2)]
    wglu = A.alloc([8, 1024], BF16)
    wgm = A.alloc([8, 1024], BF16); wco_b = A.alloc([4, 1024], BF16); wo_b = A.alloc([4, 1024], BF16)
    wout_b = A.alloc([8, 1024], BF16)
    dgr = [A.alloc([128], BF16) for _ in range(6)]
    m_p3 = A.mark()
    dma("sp", lngT, lngT_d, writes=["lngT"]); dma("actq", lnbT, lnbT_d, writes=["lnbT"])
    dma("sp", convwT, convwT_d, writes=["convwT"]); dma("actq", convbT, convbT_d, writes=["convbT"])
    ring8 = {"g": 0}

    def gb8():
        ring8["g"] = (ring8["g"] + 1) % 8
        return ring8["g"]

    def ntiles(total, step):
        return [(n0, min(step, total - n0)) for n0 in range(0, total, step)]

    def to_feature_major(src_rows, r, col, gT, kgT, kx, slot, dstT, kdst):
        hb = hn3[slot % 2]; kh = "hn3_%d" % (slot % 2)
        c0 = 4 * NBLK + (slot % 2) * 4
        act(hb[0:r], src_rows, AF.Square, [kx], [kh, "sm3_%d" % (slot % 2)], accum=sm3[0:r, c0:c0 + 1])
        rsqrt_of(sm3[0:r, c0 + 1:c0 + 2], sm3[0:r, c0:c0 + 1], 1.0 / D, r, "sm3_%d" % (slot % 2), "sm3b_%d" % (slot % 2))
        ts(hb[0:r], src_rows, sm3[0:r, c0 + 1:c0 + 2], None, ALU.mult, None, [kx, "sm3b_%d" % (slot % 2)], [kh])
        b = gb8(); k = "ps%d" % b
        pT = psh(b).rearrange("p (c t) -> p c t", c=8)
        for c in range(8):
            tr(pT[:, c, 0:r], hb[0:r, c * 128:(c + 1) * 128], ident_b[0:r, 0:r], [kh, "ident"], [k])
        tt(dstT[:, :, col:col + r], pT[:, :, 0:r], gT.unsqueeze(2).to_broadcast([128, 8, r]), ALU.mult, [k, kgT], [kdst])

    for g in range(NG):
        sb0, sq0 = g * SBG, g * SQG
        blocks = []
        for s_ in range(SBG):
            for b2 in range(2):
                rows = slice((sb0 + s_) * 512 + b2 * 128, (sb0 + s_) * 512 + b2 * 128 + 128)
                yrows = slice((sb0 + s_) * 256 + b2 * 128, (sb0 + s_) * 256 + b2 * 128 + 128)
                blocks.append((xk[rows, :], y_own[yrows, :], 128, s_ * 256 + b2 * 128))
        blocks.append((xs[sq0 * 4:sq0 * 4 + TGs, :], y_own[TP + sq0 * 4:TP + sq0 * 4 + TGs, :], TGs, TGp))
        A.release(m_p3)
        ubuf = A.alloc([4, SBG, 288], BF16); ubuf_s = A.alloc([4, SQG, 36], BF16)
        xhs = A.alloc([D], F32)
        ycv = A.alloc([4, TG], F32); ysq = A.alloc([4, 128], F32)
        mu = A.alloc([128], F32); var = A.alloc([128], F32); rs = A.alloc([128], F32)
        sg = [A.alloc([256], F32) for _ in range(2)]
        sts = xhs[:, 0:DCONV]; usn = A.alloc([4, 32], BF16); cstp = xhs[:, DCONV:2 * DCONV]
        dma("poolq", wglu, w_in[:, 0:1024].rearrange("(c p) n -> p c n", p=128), writes=["wglu"])
        dma("poolq", wco_b, w_co.rearrange("(c p) n -> p c n", p=128), writes=["wco"])
        dma("poolq", wgm, w_in[:, I_GM:I_GM + 1024].rearrange("(c p) n -> p c n", p=128), writes=["wgm"])
        dma("poolq", wo_b, w_o.rearrange("(c p) n -> p c n", p=128), writes=["wo"])
        dma("poolq", wout_b, w_out.rearrange("(c p) n -> p c n", p=128), writes=["wout"])
        for bi, (xsrc, ydst, r, col) in enumerate(blocks):
            dma(dmaq(), x1[0:r, bi, :], xsrc, writes=["x1_%d" % bi])
            to_feature_major(x1[0:r, bi, :], r, col, gmixT, "gmixT", "x1_%d" % bi, bi, hTg, "hTg")
        for h0, hr in ntiles(TH, 128):
            dma(dmaq(), xhs[0:hr], xh[sb0 * 32 + h0:sb0 * 32 + h0 + hr, :], writes=["xhs"])
            to_feature_major(xhs[0:hr], hr, TG + h0, gmixT, "gmixT", "xhs", h0 // 128, hTg, "hTg")
        P.op("pool", lambda e: e.memset(ubuf_s, 0.0), writes=["ubuf_s"])
        gl_tiles = [(s_ * 256, 256, ("sb", s_)) for s_ in range(SBG)] + [(TGp, TGs, ("smp", 0)), (TG, TH, ("halo", 0))]
        gi_ = 0
        for c in range(4):
            for (n0, n, kind) in gl_tiles:
                ba = gb8(); bg = gb8()
                for k in range(8):
                    mm(psf(ba)[:, 0:n], wglu[:, k, c * 128:(c + 1) * 128], hTg[:, k, n0:n0 + n], k == 0, k == 7, ["wglu", "hTg"], ["ps%d" % ba])
                for k in range(8):
                    mm(psf(bg)[:, 0:n], wglu[:, k, 512 + c * 128:512 + (c + 1) * 128], hTg[:, k, n0:n0 + n], k == 0, k == 7, ["wglu", "hTg"], ["ps%d" % bg])
                sgi = sg[gi_ % 2]; ksg = "sg%d" % (gi_ % 2); gi_ += 1
                act(sgi[:, 0:n], psf(bg)[:, 0:n], AF.Sigmoid, ["ps%d" % bg], [ksg])
                if kind[0] == "sb":
                    dst = ubuf[:, c, kind[1], 32:288]; a_ = psf(ba)[:, 0:n]; s_v = sgi[:, 0:n]
                elif kind[0] == "smp":
                    dst = ubuf_s[:, c, :, 30:34]
                    a_ = psf(ba)[:, 0:n].rearrange("p (s t) -> p s t", t=4); s_v = sgi[:, 0:n].rearrange("p (s t) -> p s t", t=4)
                else:
                    dst = ubuf[:, c, :, 0:32]
                    a_ = psf(ba)[:, 0:n].rearrange("p (s t) -> p s t", t=32); s_v = sgi[:, 0:n].rearrange("p (s t) -> p s t", t=32)
                tt(dst, a_, s_v, ALU.mult, ["ps%d" % ba, ksg], ["ubuf%d" % c if kind[0] != "smp" else "ubuf_s"])
        dma("poolq", wglu, w_in[:, I_GC:I_GC + 1024].rearrange("(c p) n -> p c n", p=128), writes=["wglu"])
        for s0_, ns in ntiles(SQG, 4):
            rws = ns * 30
            dma(dmaq(), sts[0:rws], state_d[(sq0 + s0_) * 30:(sq0 + s0_) * 30 + rws, :], writes=["xhs"])
            for c in range(4):
                b = gb8()
                tr(psf(b)[:, 0:rws], sts[0:rws, c * 128:(c + 1) * 128], ident_f[0:rws, 0:rws], ["xhs", "identf"], ["ps%d" % b])
                evac(ubuf_s[:, c, s0_:s0_ + ns, 0:30], psf(b)[:, 0:rws].rearrange("p (s t) -> p s t", t=30), ["ps%d" % b], ["ubuf_s"])
        dma("sp", cst_s[sq0:sq0 + SQG, 0:26, :], state_d.rearrange("(s t) c -> s t c", t=30)[sq0:sq0 + SQG, 4:30, :],
            semkey="o_cs", final=True)
        for c in range(4):
            yp = ycv[:, c, 0:TGp].rearrange("p (s t) -> p s t", t=256)
            ys_ = ycv[:, c, TGp:TG].rearrange("p (s t) -> p s t", t=4)
            cbanks = [gb8() for _ in range(SBG)]
            for j in range(CW):
                dslot = (c * CW + j) % 6
                dg = dgr[dslot]; kdg = "dg%d" % dslot
                ts(dg, ident_b, convwT[:, c, j:j + 1], None, ALU.mult, None, ["ident", "convwT"], [kdg])
                for s_ in range(SBG):
                    mm(psf(cbanks[s_])[:, 0:256], dg, ubuf[:, c, s_, 2 + j:258 + j], j == 0, j == CW - 1, [kdg, "ubuf%d" % c], ["ps%d" % cbanks[s_]])
            for s_ in range(SBG):
                act(ycv[:, c, s_ * 256:(s_ + 1) * 256], psf(cbanks[s_])[:, 0:256], AF.Identity, ["ps%d" % cbanks[s_], "convbT"], ["ycv%d" % c],
                    bias=convbT[:, c:c + 1])
            ts(ys_, ubuf_s[:, c, :, 0:4], convwT[:, c, 0:1], convbT[:, c:c + 1], ALU.mult, ALU.add, ["ubuf_s", "convwT", "convbT"], ["ycvs%d" % c])
            for j in range(1, CW):
                stt(ys_, ubuf_s[:, c, :, j:j + 4], convwT[:, c, j:j + 1], ys_, ALU.mult, ALU.add, ["ubuf_s", "convwT", "ycvs%d" % c], ["ycvs%d" % c])
        ykeys = ["ycv%d" % c for c in range(4)] + ["ycvs%d" % c for c in range(4)]
        for (n0, n) in ntiles(TG, 128):
            act(ysq[:, :, 0:n], ycv[:, :, n0:n0 + n], AF.Square, ykeys, ["ysq"])
            b1 = gb8(); b2 = gb8()
            for c in range(4):
                mm(psf(b1)[:, 0:n], ones_f, ycv[:, c, n0:n0 + n], c == 0, c == 3, ["onesf"] + ykeys, ["ps%d" % b1])
            for c in range(4):
                mm(psf(b2)[:, 0:n], ones_f, ysq[:, c, 0:n], c == 0, c == 3, ["onesf", "ysq"], ["ps%d" % b2])
            P.op("act", lambda e, b1=b1, n=n: e.mul(mu[:, 0:n], psf(b1)[:, 0:n], 1.0 / DCONV), reads=["ps%d" % b1], writes=["mu"])
            tt(var[:, 0:n], mu[:, 0:n], mu[:, 0:n], ALU.mult, ["mu"], ["var"])
            stt(var[:, 0:n], psf(b2)[:, 0:n], 1.0 / DCONV, var[:, 0:n], ALU.mult, ALU.subtract, ["ps%d" % b2, "var"], ["var"])
            rsqrt_of(rs[:, 0:n], var[:, 0:n], 1.0, 128, "var", "rs")
            for c in range(4):
                tt(ysq[:, c, 0:n], ycv[:, c, n0:n0 + n], mu[:, 0:n], ALU.subtract, ykeys + ["mu", "ysq"], ["ysq"])
                tt(ysq[:, c, 0:n], ysq[:, c, 0:n], rs[:, 0:n], ALU.mult, ["ysq", "rs"], ["ysq"])
                act(YT[:, c, n0:n0 + n], ysq[:, c, 0:n], AF.Silu, ["ysq", "lngT", "lnbT"], ["YT"], scale=lngT[:, c:c + 1], bias=lnbT[:, c:c + 1])
        if g == NG - 1:
            for c in range(4):
                b = gb8()
                tr(psh(b)[0:32, 0:128], ubuf[:, c, SBG - 1, 256:288], ident_b, ["ubuf%d" % c, "ident"], ["ps%d" % b])
                evac(cstp[0:32, c * 128:(c + 1) * 128], psh(b)[0:32, 0:128], ["ps%d" % b], ["xhs"])
            dma("sp", cst_p, cstp[0:32], reads=["xhs"], semkey="o_cp", final=True)
        for c in range(4):
            evac(usn[:, c, 0:TGs].rearrange("p (s t) -> p s t", t=4), ubuf_s[:, c, :, 30:34], ["ubuf_s"], ["usn"], eng="dve")
        for c in range(4):
            b = gb8()
            tr(psh(b)[0:TGs, 0:128], usn[:, c, 0:TGs], ident_b, ["usn", "ident"], ["ps%d" % b])
            evac(sts[0:TGs, c * 128:(c + 1) * 128], psh(b)[0:TGs, 0:128], ["ps%d" % b], ["xhs"])
        for s_ in range(SQG):
            dma(dmaq(), cst_s[sq0 + s_, 26:30, :], sts[s_ * 4:(s_ + 1) * 4, :], reads=["xhs"], semkey="o_cs2", final=True)
        done('p3a%d' % g)
        P.barrier()
        A.release(m_p3)
        mergedT = A.alloc([8, TG], BF16)
        wgc = wglu
        sgc = [A.alloc([256], F32) for _ in range(2)]; sgm = [A.alloc([256], F32) for _ in range(2)]
        t1 = [A.alloc([256], F32) for _ in range(2)]; t2 = [A.alloc([256], F32) for _ in range(2)]
        mt = [(s_ * 256, 256, (sb0 + s_) * 256) for s_ in range(SBG)] + [(TGp, TGs, TP + sq0 * 4)]
        mi = 0
        for m in range(8):
            for (n0, n, ocol) in mt:
                s2 = mi % 2; mi += 1
                bA = gb8(); bB = gb8()
                kA, kB = "ps%d" % bA, "ps%d" % bB
                for k in range(8):
                    mm(psf(bA)[:, 0:n], wgc[:, k, m * 128:(m + 1) * 128], hTg[:, k, n0:n0 + n], k == 0, k == 7, ["wglu", "hTg"], [kA])
                for k in range(4):
                    mm(psf(bA)[:, 256:256 + n], wco_b[:, k, m * 128:(m + 1) * 128], YT[:, k, n0:n0 + n], k == 0, k == 3, ["wco", "YT"], [kA])
                for k in range(8):
                    mm(psf(bB)[:, 0:n], wgm[:, k, m * 128:(m + 1) * 128], hTg[:, k, n0:n0 + n], k == 0, k == 7, ["wgm", "hTg"], [kB])
                for k in range(4):
                    mm(psf(bB)[:, 256:256 + n], wo_b[:, k, m * 128:(m + 1) * 128], OT[:, k, ocol:ocol + n], k == 0, k == 3, ["wo", "OT"], [kB])
                act(sgc[s2][:, 0:n], psf(bA)[:, 0:n], AF.Sigmoid, [kA], ["sgc%d" % s2])
                tt(t1[s2][:, 0:n], psf(bA)[:, 256:256 + n], sgc[s2][:, 0:n], ALU.mult, [kA, "sgc%d" % s2], ["t1_%d" % s2])
                act(sgm[s2][:, 0:n], psf(bB)[:, 0:n], AF.Sigmoid, [kB], ["sgm%d" % s2])
                tt(t2[s2][:, 0:n], psf(bB)[:, 256:256 + n], sgm[s2][:, 0:n], ALU.mult, [kB, "sgm%d" % s2], ["t2_%d" % s2])
                tt(mergedT[:, m, n0:n0 + n], t1[s2][:, 0:n], t2[s2][:, 0:n], ALU.add, ["t1_%d" % s2, "t2_%d" % s2], ["mergedT"])
        for bi, (xsrc, ydst, r, col) in enumerate(blocks):
            for half in range(2):
                b = gb8(); k_ = "ps%d" % b
                for k in range(8):
                    mm(psf(b)[0:r, :], mergedT[:, k, col:col + r], wout_b[:, k, half * 512:(half + 1) * 512], k == 0, k == 7, ["mergedT", "wout"], [k_])
                tt(x1[0:r, bi, half * 512:(half + 1) * 512], psf(b)[0:r, :], x1[0:r, bi, half * 512:(half + 1) * 512], ALU.add,
                   [k_, "x1_%d" % bi], ["x1_%d" % bi])
        done('p3b%d' % g)
        P.barrier()
        A.release(m_p3)
        h2T = hTg
        actT = [A.alloc([4, TG], BF16) for _ in range(2)]
        wg_b = [wgm[:, :, 0:512], wgm[:, :, 512:1024]]
        wu_b = [wglu[:, :, 0:512], wglu[:, :, 512:1024]]
        wd_b = [wout_b[:, 0:4, :], wout_b[:, 4:8, :]]
        sgt = [A.alloc([512], F32) for _ in range(2)]
        for bi, (xsrc, ydst, r, col) in enumerate(blocks):
            to_feature_major(x1[0:r, bi, :], r, col, gffnT, "gffnT", "x1_%d" % bi, bi, h2T, "h2T")
        fgroups = ntiles(DFF // 128, 4)
        si = 0
        for fg, (f0, nf) in enumerate(fgroups):
            s2 = fg % 2
            dma("poolq", wg_b[s2][:, :, 0:nf * 128], w_gate[:, f0 * 128:(f0 + nf) * 128].rearrange("(c p) n -> p c n", p=128), writes=["wg%d" % s2])
            dma("poolq", wu_b[s2][:, :, 0:nf * 128], w_up[:, f0 * 128:(f0 + nf) * 128].rearrange("(c p) n -> p c n", p=128), writes=["wu%d" % s2])
            dma("poolq", wd_b[s2][:, 0:nf, :], w_down[f0 * 128:(f0 + nf) * 128, :].rearrange("(c p) n -> p c n", p=128), writes=["wd%d" % s2])
            for fi in range(nf):
                for (n0, n) in ntiles(TG, 512):
                    bG = gb8(); bU = gb8()
                    for k in range(8):
                        mm(psf(bG)[:, 0:n], wg_b[s2][:, k, fi * 128:(fi + 1) * 128], h2T[:, k, n0:n0 + n], k == 0, k == 7, ["wg%d" % s2, "h2T"], ["ps%d" % bG])
                    for k in range(8):
                        mm(psf(bU)[:, 0:n], wu_b[s2][:, k, fi * 128:(fi + 1) * 128], h2T[:, k, n0:n0 + n], k == 0, k == 7, ["wu%d" % s2, "h2T"], ["ps%d" % bU])
                    st_ = sgt[si % 2]; kst = "sgt%d" % (si % 2); si += 1
                    act(st_[:, 0:n], psf(bG)[:, 0:n], AF.Silu, ["ps%d" % bG], [kst])
                    tt(actT[s2][:, fi, n0:n0 + n], psf(bU)[:, 0:n], st_[:, 0:n], ALU.mult, ["ps%d" % bU, kst], ["actT%d" % s2])
            for bi, (xsrc, ydst, r, col) in enumerate(blocks):
                for half in range(2):
                    b = gb8(); k_ = "ps%d" % b
                    for fi in range(nf):
                        mm(psf(b)[0:r, :], actT[s2][:, fi, col:col + r], wd_b[s2][:, fi, half * 512:(half + 1) * 512], fi == 0, fi == nf - 1,
                           ["actT%d" % s2, "wd%d" % s2], [k_])
                    tt(x1[0:r, bi, half * 512:(half + 1) * 512], psf(b)[0:r, :], x1[0:r, bi, half * 512:(half + 1) * 512], ALU.add,
                       [k_, "x1_%d" % bi], ["x1_%d" % bi])
        for bi, (xsrc, ydst, r, col) in enumerate(blocks):
            dma(dmaq(), ydst, x1[0:r, bi, :], reads=["x1_%d" % bi], semkey="o_y%d" % (bi % 4), final=True)
        P.barrier()

    P.emit()
    cfg.arena_peak = A.peak


def _host_inputs(cfg, inp):
    f32 = np.float32
    NSB, NSQ, TS = cfg.NSB, cfg.NSQ, cfg.TS
    inv_freq = (1.0 / (10000.0 ** (np.arange(0, ROPE, 2, dtype=f32) / f32(ROPE)))).astype(f32)

    def cs_table(pos):
        ang = pos.astype(f32)[:, None] * inv_freq[None, :]
        c, s = np.cos(ang).astype(f32), np.sin(ang).astype(f32)
        return np.ascontiguousarray(np.concatenate([c, c, -s, s], axis=1))

    ident = np.eye(128, dtype=f32)
    ind = np.zeros((128, 4, 8), f32)
    for c in range(4):
        for p in range(128):
            ind[p, c, (c * 128 + p) // 64] = 1.0
    maskpair = np.full((128, 64), NEG, f32)
    for s2 in range(2):
        maskpair[s2 * 64:(s2 + 1) * 64, s2 * 32:(s2 + 1) * 32] = 0.0
    masknew = np.full((TS, NSQ * 32), NEG, f32)
    for s in range(NSQ):
        for t in range(4):
            for q in range(t, 4):
                masknew[s * 4 + t, s * 32 + q * 8:s * 32 + q * 8 + 8] = 0.0
    cs_s = cs_table(np.tile(cfg.PAST + np.arange(4), NSQ))
    w = {k: np.ascontiguousarray(inp[k][0]) for k in ("w_in", "w_uq", "w_o_mla", "w_conv_out", "w_out", "w_gate", "w_up", "w_down")}
    w["w_uk"] = np.ascontiguousarray(inp["w_uk"][0].reshape(LAT, 512))
    w["w_uv"] = np.ascontiguousarray(inp["w_uv"][0].reshape(LAT, 512))
    shared = dict(w)
    shared.update(
        ident=ident, ind=ind, maskpair=maskpair, masknew=masknew, cs_s=cs_s,
        cache_lat=inp["cache_kv_latent"][0].reshape(cfg.NPHYS, 128 * LAT),
        cache_rope=inp["cache_k_rope"][0].reshape(cfg.NPHYS, 128 * ROPE),
        gmixT=np.ascontiguousarray(inp["norm_mix_g"][0].reshape(8, 128).T),
        gffnT=np.ascontiguousarray(inp["norm_ffn_g"][0].reshape(8, 128).T),
        convwT=np.ascontiguousarray(inp["conv_w"][0].reshape(CW, 4, 128).transpose(2, 1, 0)),
        convbT=np.ascontiguousarray(inp["conv_b"][0].reshape(4, 128).T),
        lngT=np.ascontiguousarray(inp["conv_ln_g"][0].reshape(4, 128).T),
        lnbT=np.ascontiguousarray(inp["conv_ln_b"][0].reshape(4, 128).T),
        gqa=inp["q_a_norm_g"].reshape(1, QL), gkv=inp["kv_a_norm_g"].reshape(1, LAT),
        gq=inp["q_norm_g"].reshape(1, 80), gk=inp["k_norm_g"].reshape(1, 80),
    )
    maps, meta = [], []
    for core in range(cfg.NC):
        b, p = core // 2, core % 2
        pos_k, pos_own = [], []
        for i in range(NSB):
            own = (2 * i + p) * 256 + np.arange(256)
            oth = (2 * i + 1 - p) * 256 + np.arange(256)
            pos_k += [own, oth]
            pos_own.append(own)
        pos_k = np.concatenate(pos_k); pos_own = np.concatenate(pos_own)
        xp = inp["x_prompt"][b]
        xh = np.zeros((NSB * 32, D), f32)
        for i in range(NSB):
            st = (2 * i + p) * 256
            if st > 0:
                xh[i * 32 + 2:(i + 1) * 32] = xp[st - 30:st]
        maskp = np.full((128, 4, 256), NEG, f32)
        kk = np.arange(128)[:, None]; qq = np.arange(256)[None, :]
        for m in range(2):
            maskp[:, m, :] = np.where(m * 128 + kk <= qq, 0.0, NEG)
        if p == 1:
            maskp[:, 2:4, :] = 0.0
        sq0 = core * NSQ
        pt = inp["page_table"][sq0:sq0 + NSQ]
        ptT = np.ascontiguousarray(pt.reshape(cfg.NPAIR, 128).T.astype(np.int32))
        m = dict(shared)
        m.update(
            xk=np.ascontiguousarray(xp[pos_k]), xs=np.ascontiguousarray(inp["x_sample"][sq0:sq0 + NSQ].reshape(TS, D)), xh=xh,
            cs_k=cs_table(pos_k), maskp=maskp,
            state=np.ascontiguousarray(inp["state_conv"][0, sq0:sq0 + NSQ].reshape(NSQ * 30, DCONV)), ptT=ptT,
        )
        maps.append(m)
        meta.append((b, p, pos_own, sq0))
    return maps, meta


_CACHE = {}


def run(cfg, inputs):
    key = (cfg.NB, cfg.SEQ, cfg.DB)
    if key not in _CACHE:
        _CACHE[key] = build(cfg)[0]
    nc = _CACHE[key]
    inp = {k: np.asarray(v) for k, v in inputs.items()}
    maps, meta = _host_inputs(cfg, inp)
    res = run_bass_kernel_spmd(nc, maps, core_ids=list(range(cfg.NC)))
    f32 = np.float32
    NB, SEQ, DB, NSQ = cfg.NB, cfg.SEQ, cfg.DB, cfg.NSQ
    y_p = np.zeros((NB, SEQ, D), f32); y_s = np.zeros((DB, 4, D), f32)
    lat_p = np.zeros((1, NB, SEQ, LAT), f32); kpe_p = np.zeros((1, NB, SEQ, ROPE), f32)
    cs_p = np.zeros((1, NB, 30, DCONV), f32)
    lat_s = np.zeros((1, DB, 4, LAT), f32); kpe_s = np.zeros((1, DB, 4, ROPE), f32); cs_s = np.zeros((1, DB, 30, DCONV), f32)
    for core, (b, p, pos_own, sq0) in enumerate(meta):
        r = res.results[core]
        y_p[b, pos_own] = r["y_own"][:cfg.TP]
        y_s[sq0:sq0 + NSQ] = r["y_own"][cfg.TP:].reshape(NSQ, 4, D)
        own_rows = np.concatenate([i * 512 + np.arange(256) for i in range(cfg.NSB)])
        lat_p[0, b, pos_own] = r["lat_k"][own_rows]
        kpe_p[0, b, pos_own] = r["kpe_k"][own_rows]
        if p == 1:
            cs_p[0, b] = r["cst_p"][2:32]
        lat_s[0, sq0:sq0 + NSQ] = r["lat_s"].reshape(NSQ, 4, LAT)
        kpe_s[0, sq0:sq0 + NSQ] = r["kpe_s"].reshape(NSQ, 4, ROPE)
        cs_s[0, sq0:sq0 + NSQ] = r["cst_s"]
    return (y_p, y_s, lat_p, kpe_p, cs_p, lat_s, kpe_s, cs_s)


def kernel(**inputs):
    return run(Cfg(), inputs)
```
